# Optimizing a Trainium2 kernel written in Bass

```python
import jax, jax.numpy as jnp
from jax import lax
import numpy as np

D_MODEL = 2048
BATCH = 4
SEQ = 2048
DEPTH = 2

GRID_W = 64
CTX_LEN = 256
F32 = jnp.float32
EPS = 1e-6
ROPE_THETA = 10000.0
Q_BLOCK = 128

HEAD_DIM = 128
GQA_HEADS = 8
GQA_KV_HEADS = 2
GQA_GROUP = GQA_HEADS // GQA_KV_HEADS
HGRN_HEADS = 4
HGRN_DK = 128
HGRN_DV = 128
HGRN_W = HGRN_HEADS * HGRN_DK
HGRN_CHUNK = 64
MLA_HEADS = 4
MLA_Q_RANK = 512
MLA_KV_RANK = 256
MLA_NOPE = 128
MLA_ROPE = 64
MLA_V = 128
MLA_QK = MLA_NOPE + MLA_ROPE

MIX_WIDTH = GQA_HEADS * HEAD_DIM + HGRN_HEADS * HGRN_DV + MLA_HEADS * MLA_V
IN_SIZES = (GQA_HEADS * HEAD_DIM, GQA_KV_HEADS * HEAD_DIM, GQA_KV_HEADS * HEAD_DIM,
            HGRN_W, HGRN_HEADS * HGRN_DV, HGRN_W, HGRN_W, HGRN_HEADS * HGRN_DV,
            MLA_Q_RANK, MLA_KV_RANK, MLA_ROPE)
IN_WIDTH = sum(IN_SIZES)
FFN_HIDDEN = 5504
N_MOD = 9

kernel_name = 'hybrid_dit_gqa_hgrn2_mla_macaron'


def rms_norm(x, gain=None):
    xf = x.astype(F32)
    y = xf * lax.rsqrt(jnp.mean(xf * xf, axis=-1, keepdims=True) + EPS)
    if gain is not None:
        y = y * gain.astype(F32)
    return y.astype(x.dtype)


def modulation(cvec, w_mod, b_mod):
    m = jax.nn.silu(cvec) @ w_mod + b_mod
    m = m.reshape(m.shape[:-1] + (N_MOD, D_MODEL))
    return [m[..., None, i, :] for i in range(N_MOD)]


def modulate(x, shift, scale):
    return rms_norm(x) * (1.0 + scale) + shift


def swiglu(h, w_in, w_out):
    gate, up = jnp.split(h @ w_in, 2, axis=-1)
    return (jax.nn.silu(gate) * up) @ w_out


def axial_rope_tables(rows, rot_dim):
    row = jnp.repeat(jnp.arange(rows, dtype=F32), GRID_W)
    col = jnp.tile(jnp.arange(GRID_W, dtype=F32), rows)
    axis_dim = rot_dim // 2
    inv_freq = ROPE_THETA ** (-jnp.arange(0, axis_dim, 2, dtype=F32) / axis_dim)
    ang_r = row[:, None] * inv_freq
    ang_c = col[:, None] * inv_freq
    ang = jnp.concatenate([ang_r, ang_r, ang_c, ang_c], axis=-1)
    return jnp.cos(ang), jnp.sin(ang)


def apply_rope(x, cos, sin):
    a1, a2, b1, b2 = jnp.split(x, 4, axis=-1)
    rot = jnp.concatenate([-a2, a1, -b2, b1], axis=-1)
    y = x.astype(F32) * cos[:, None, :] + rot.astype(F32) * sin[:, None, :]
    return y.astype(x.dtype)


def block_attention(q, k, v, scale):
    b, kh, g, tq, d = q.shape
    nb = tq // Q_BLOCK
    qb = q.reshape(b, kh, g, nb, Q_BLOCK, d).transpose(3, 0, 1, 2, 4, 5)

    def one_block(q_blk):
        s = jnp.einsum('bkgqd,bksd->bkgqs', q_blk, k).astype(F32) * scale
        p = jax.nn.softmax(s, axis=-1).astype(v.dtype)
        return jnp.einsum('bkgqs,bkse->bkgqe', p, v)

    ob = lax.map(one_block, qb)
    return ob.transpose(1, 2, 3, 0, 4, 5).reshape(b, kh, g, tq, v.shape[-1])


def gla_chunkwise(q, k, v, log_f, s0):
    b, h, t, dk = q.shape
    dv = v.shape[-1]
    n = t // HGRN_CHUNK

    def chunks(a):
        return a.astype(F32).reshape(b, h, n, HGRN_CHUNK, a.shape[-1]).transpose(2, 0, 1, 3, 4)

    lower_tri = jnp.tril(jnp.ones((HGRN_CHUNK, HGRN_CHUNK), dtype=bool))[:, :, None]

    def step(state, inp):
        qc, kc, vc, gc = inp
        cum = jnp.cumsum(gc, axis=-2)
        o_inter = jnp.einsum('bhtk,bhkv->bhtv', qc * jnp.exp(cum), state)
        rel = jnp.where(lower_tri, cum[:, :, :, None, :] - cum[:, :, None, :, :], -jnp.inf)
        scores = jnp.einsum('bhtk,bhtsk,bhsk->bhts', qc, jnp.exp(rel), kc)
        o = o_inter + jnp.einsum('bhts,bhsv->bhtv', scores, vc)
        last = cum[:, :, -1:, :]
        new_state = (jnp.exp(last[:, :, 0, :])[..., None] * state
                     + jnp.einsum('bhsk,bhsv->bhkv', kc * jnp.exp(last - cum), vc))
        return new_state, o

    s_fin, o = lax.scan(step, s0, (chunks(q), chunks(k), chunks(v), chunks(log_f)))
    return o.transpose(1, 2, 0, 3, 4).reshape(b, h, t, dv), s_fin


def hgrn_direction(q, i, z, lb, s0):
    zf = z.astype(F32)
    lbf = lb.reshape(HGRN_HEADS, HGRN_DK)
    log_f = jnp.logaddexp(jnp.log(lbf), jnp.log1p(-lbf) + jax.nn.log_sigmoid(zf))
    k = (1.0 - lbf) * jax.nn.sigmoid(-zf)
    o, s = gla_chunkwise(jnp.swapaxes(q, 1, 2) * HGRN_DK ** -0.5, jnp.swapaxes(k, 1, 2),
                         jnp.swapaxes(i, 1, 2), jnp.swapaxes(log_f, 1, 2), s0)
    return jnp.swapaxes(o, 1, 2).astype(i.dtype), s


def hgrn2_bidirectional(q_l, i_l, zf_l, zb_l, q_c, i_c, zf_c, zb_c, lb):
    s0 = jnp.zeros((q_l.shape[0], HGRN_HEADS, HGRN_DK, HGRN_DV), F32)
    flip = lambda a: jnp.flip(a, axis=1)
    o_cf, s_f = hgrn_direction(q_c, i_c, zf_c, lb[0], s0)
    o_lf, _ = hgrn_direction(q_l, i_l, zf_l, lb[0], s_f)
    o_cb, s_b = hgrn_direction(flip(q_c), flip(i_c), flip(zb_c), lb[1], s0)
    o_lb, _ = hgrn_direction(flip(q_l), flip(i_l), flip(zb_l), lb[1], s_b)
    return o_lf + flip(o_lb), o_cf + flip(o_cb)


def split_in(p):
    offsets = []
    acc = 0
    for s in IN_SIZES[:-1]:
        acc += s
        offsets.append(acc)
    return jnp.split(p, offsets, axis=-1)


def heads(a, n):
    return a.reshape(a.shape[0], a.shape[1], n, -1)


def kv_layout(a):
    return jnp.swapaxes(a, 1, 2)


def gqa_q_layout(a):
    return a.reshape(a.shape[0], a.shape[1], GQA_KV_HEADS, GQA_GROUP, HEAD_DIM).transpose(0, 2, 3, 1, 4)


def merge_heads(o):
    return o.transpose(0, 3, 1, 2, 4).reshape(o.shape[0], o.shape[3], -1)


def token_mixers(h_lat, h_ctx, w_in, w_uq, w_ukv, w_out, gqa_q_gain, gqa_k_gain, mla_q_gain,
                 mla_kv_gain, lb, hgrn_norm_gain, rope_h, rope_r, need_ctx):
    t_lat = h_lat.shape[1]
    gq_l, gk_l, gv_l, hq_l, hi_l, hf_l, hb_l, hg_l, cq_l, ckv_l, kr_l = split_in(h_lat @ w_in)
    gq_c, gk_c, gv_c, hq_c, hi_c, hf_c, hb_c, hg_c, cq_c, ckv_c, kr_c = split_in(h_ctx @ w_in)

    k_a = jnp.concatenate([kv_layout(apply_rope(rms_norm(heads(gk_l, GQA_KV_HEADS), gqa_k_gain), *rope_h)),
                           kv_layout(rms_norm(heads(gk_c, GQA_KV_HEADS), gqa_k_gain))], axis=2)
    v_a = jnp.concatenate([kv_layout(heads(gv_l, GQA_KV_HEADS)), kv_layout(heads(gv_c, GQA_KV_HEADS))], axis=2)
    q_a = gqa_q_layout(apply_rope(rms_norm(heads(gq_l, GQA_HEADS), gqa_q_gain), *rope_h))
    o_a = merge_heads(block_attention(q_a, k_a, v_a, HEAD_DIM ** -0.5))

    def mla_q(cq):
        r = heads(rms_norm(cq, mla_q_gain) @ w_uq, MLA_HEADS)
        return r[..., :MLA_NOPE], r[..., MLA_NOPE:]

    def mla_kv(ckv, kr):
        r = heads(rms_norm(ckv, mla_kv_gain) @ w_ukv, MLA_HEADS)
        k = jnp.concatenate([r[..., :MLA_NOPE], jnp.broadcast_to(kr, r.shape[:-1] + (MLA_ROPE,))], axis=-1)
        return k, r[..., MLA_NOPE:]

    k_cl, v_cl = mla_kv(ckv_l, apply_rope(kr_l[:, :, None, :], *rope_r))
    k_cc, v_cc = mla_kv(ckv_c, kr_c[:, :, None, :])
    k_m = jnp.concatenate([kv_layout(k_cl), kv_layout(k_cc)], axis=2)
    v_m = jnp.concatenate([kv_layout(v_cl), kv_layout(v_cc)], axis=2)
    qn_l, qr_l = mla_q(cq_l)
    q_m = kv_layout(jnp.concatenate([qn_l, apply_rope(qr_l, *rope_r)], axis=-1))[:, :, None]
    o_m = merge_heads(block_attention(q_m, k_m, v_m, MLA_QK ** -0.5))

    o_bl, o_bc = hgrn2_bidirectional(heads(hq_l, HGRN_HEADS), heads(hi_l, HGRN_HEADS), heads(hf_l, HGRN_HEADS),
                                     heads(hb_l, HGRN_HEADS), heads(hq_c, HGRN_HEADS), heads(hi_c, HGRN_HEADS),
                                     heads(hf_c, HGRN_HEADS), heads(hb_c, HGRN_HEADS), lb)

    def hgrn_out(o, g):
        y = rms_norm(o, hgrn_norm_gain) * jax.nn.silu(heads(g, HGRN_HEADS))
        return y.reshape(y.shape[0], y.shape[1], -1)

    y_lat = jnp.concatenate([o_a, hgrn_out(o_bl, hg_l), o_m], axis=-1) @ w_out
    if not need_ctx:
        return y_lat, None

    q_ac = gqa_q_layout(rms_norm(heads(gq_c, GQA_HEADS), gqa_q_gain))
    o_ac = merge_heads(block_attention(q_ac, k_a[:, :, t_lat:], v_a[:, :, t_lat:], HEAD_DIM ** -0.5))
    qn_c, qr_c = mla_q(cq_c)
    q_mc = kv_layout(jnp.concatenate([qn_c, qr_c], axis=-1))[:, :, None]
    o_mc = merge_heads(block_attention(q_mc, k_m[:, :, t_lat:], v_m[:, :, t_lat:], MLA_QK ** -0.5))
    y_ctx = jnp.concatenate([o_ac, hgrn_out(o_bc, hg_c), o_mc], axis=-1) @ w_out
    return y_lat, y_ctx


def setup_inputs(seed: int = 0) -> dict:
    key = jax.random.key(seed)
    ks = jax.random.split(key, 24)

    def nrm(k, shape, scale):
        return jax.random.normal(k, shape, F32) * scale

    d = D_MODEL
    return {
        'x': nrm(ks[0], (BATCH, SEQ, d), 1.0),
        'c': nrm(ks[1], (BATCH, d), 1.0),
        'ctx': nrm(ks[2], (BATCH, CTX_LEN, d), 1.0),
        'c_ctx': nrm(ks[3], (d,), 1.0),
        'w_mod': nrm(ks[4], (DEPTH, d, N_MOD * d), 0.5 * d ** -0.5),
        'b_mod': nrm(ks[5], (DEPTH, N_MOD * d), 0.02),
        'w_ffn1_in': nrm(ks[6], (DEPTH, d, 2 * FFN_HIDDEN), d ** -0.5),
        'w_ffn1_out': nrm(ks[7], (DEPTH, FFN_HIDDEN, d), FFN_HIDDEN ** -0.5),
        'w_in': nrm(ks[8], (DEPTH, d, IN_WIDTH), d ** -0.5),
        'w_uq': nrm(ks[9], (DEPTH, MLA_Q_RANK, MLA_HEADS * MLA_QK), MLA_Q_RANK ** -0.5),
        'w_ukv': nrm(ks[10], (DEPTH, MLA_KV_RANK, MLA_HEADS * (MLA_NOPE + MLA_V)), MLA_KV_RANK ** -0.5),
        'w_out': nrm(ks[11], (DEPTH, MIX_WIDTH, d), MIX_WIDTH ** -0.5),
        'gqa_q_gain': 1.0 + nrm(ks[12], (DEPTH, HEAD_DIM), 0.02),
        'gqa_k_gain': 1.0 + nrm(ks[13], (DEPTH, HEAD_DIM), 0.02),
        'mla_q_gain': 1.0 + nrm(ks[14], (DEPTH, MLA_Q_RANK), 0.02),
        'mla_kv_gain': 1.0 + nrm(ks[15], (DEPTH, MLA_KV_RANK), 0.02),
        'hgrn_lb_logits': nrm(ks[16], (2, DEPTH, HGRN_W), 0.5),
        'hgrn_norm_gain': 1.0 + nrm(ks[17], (DEPTH, HGRN_DV), 0.02),
        'w_ffn2_in': nrm(ks[18], (DEPTH, d, 2 * FFN_HIDDEN), d ** -0.5),
        'w_ffn2_out': nrm(ks[19], (DEPTH, FFN_HIDDEN, d), FFN_HIDDEN ** -0.5),
        'final_gain': 1.0 + nrm(ks[20], (d,), 0.02),
    }


def reference(x, c, ctx, c_ctx, w_mod, b_mod, w_ffn1_in, w_ffn1_out, w_in, w_uq, w_ukv, w_out,
              gqa_q_gain, gqa_k_gain, mla_q_gain, mla_kv_gain, hgrn_lb_logits, hgrn_norm_gain,
              w_ffn2_in, w_ffn2_out, final_gain):
    rows = x.shape[1] // GRID_W
    rope_h = axial_rope_tables(rows, HEAD_DIM)
    rope_r = axial_rope_tables(rows, MLA_ROPE)
    lb_cum = jnp.cumsum(jax.nn.softmax(hgrn_lb_logits.astype(F32), axis=1), axis=1)
    lower_bounds = lb_cum - lb_cum[:, :1]

    x_lat, x_ctx = x, ctx
    for l in range(DEPTH):
        last = l == DEPTH - 1
        m_l = modulation(c, w_mod[l], b_mod[l])
        m_c = modulation(c_ctx, w_mod[l], b_mod[l])
        x_lat = x_lat + 0.5 * m_l[2] * swiglu(modulate(x_lat, m_l[0], m_l[1]), w_ffn1_in[l], w_ffn1_out[l])
        x_ctx = x_ctx + 0.5 * m_c[2] * swiglu(modulate(x_ctx, m_c[0], m_c[1]), w_ffn1_in[l], w_ffn1_out[l])
        y_lat, y_ctx = token_mixers(modulate(x_lat, m_l[3], m_l[4]), modulate(x_ctx, m_c[3], m_c[4]),
                                    w_in[l], w_uq[l], w_ukv[l], w_out[l], gqa_q_gain[l], gqa_k_gain[l],
                                    mla_q_gain[l], mla_kv_gain[l], lower_bounds[:, l], hgrn_norm_gain[l],
                                    rope_h, rope_r, not last)
        x_lat = x_lat + m_l[5] * y_lat
        x_lat = x_lat + 0.5 * m_l[8] * swiglu(modulate(x_lat, m_l[6], m_l[7]), w_ffn2_in[l], w_ffn2_out[l])
        if not last:
            x_ctx = x_ctx + m_c[5] * y_ctx
            x_ctx = x_ctx + 0.5 * m_c[8] * swiglu(modulate(x_ctx, m_c[6], m_c[7]), w_ffn2_in[l], w_ffn2_out[l])
    return rms_norm(x_lat, final_gain)
```

```python
import numpy as np
import ml_dtypes
import concourse.bass as bass
import concourse.mybir as mybir
from concourse.bass_utils import run_bass_kernel_spmd

F32 = mybir.dt.float32
BF16 = mybir.dt.bfloat16
AF = mybir.ActivationFunctionType
ALU = mybir.AluOpType
AX = mybir.AxisListType

PE, ACT, DVE, POOL, SP = "pe", "act", "dve", "pool", "sp"
ENGS = (PE, ACT, DVE, POOL, SP)
N_DMA_SEMS = 8
ANNOTATE = False

D = 2048
KC = 16
TL = 1024
TCX = 256
T = TL + TCX
SEQ = 2048
FH = 5504
FC = 43
NMOD = 9
EPS = 1e-6
CH = 16
NCH = T // CH
TILES = ((0, 512), (512, 512), (1024, 256))


class Op:
    __slots__ = ("eng", "fn", "idx", "deps", "is_dma", "signal", "seq", "dsem", "dval", "nd", "tag")

    def __init__(self, eng, fn, is_dma):
        self.eng = eng
        self.fn = fn
        self.is_dma = is_dma
        self.signal = False
        self.seq = None
        self.deps = []
        self.dsem = None
        self.dval = None
        self.nd = None


class _Rec:
    def __getattr__(self, name):
        def f(*a, **k):
            self.call = (name, a, k)
            return self
        return f


_ESZ = {}


def _esize(dtype):
    k = str(dtype)
    if k not in _ESZ:
        _ESZ[k] = 4 if "32" in k else (2 if "16" in k else (1 if "8" in k else 4))
    return _ESZ[k]


def _footprint(ap, psum_bank):
    sp = str(ap.space)
    t = ap.tensor
    if "PSUM" in sp:
        return ("P", psum_bank[t.name], 0, 1 << 30, 0, 128)
    if "DRAM" in sp:
        return ("D", t.name, 0, 1 << 30, 0, 128)
    apl = ap.ap
    pstep, pcount = apl[0]
    off = ap.offset
    if pstep:
        p0 = off // pstep
        foff = off - p0 * pstep
    else:
        p0, foff = 0, off
    ext = 0
    for st, c in apl[1:]:
        ext += (c - 1) * abs(st)
    esz = _esize(ap.dtype)
    base = t.manual_sbuf_range[0]
    return ("S", None, base + foff * esz, base + (foff + ext + 1) * esz, p0, p0 + pcount)


def _ov(a, b):
    return a[0] == b[0] and a[1] == b[1] and a[2] < b[3] and b[2] < a[3] and a[4] < b[5] and b[4] < a[5]


def _covers(a, b):
    return a[0] == b[0] and a[1] == b[1] and a[2] <= b[2] and a[3] >= b[3] and a[4] <= b[4] and a[5] >= b[5]


def _is_ap(x):
    return hasattr(x, "tensor") and hasattr(x, "ap") and hasattr(x, "offset")


class Prog:
    def __init__(self, nc):
        self.nc = nc
        self.ops = {e: [] for e in ENGS}
        self.last_writer = {}
        self.readers = {}
        self.barrier_ops = []
        self.psum_bank = {}
        self.wlog = []
        self.rlog = []

    def _addr_deps(self, op, call, deps):
        name, a, k = call
        outs, ins = [], []
        if "out" in k:
            outs.append(k["out"])
        elif a and _is_ap(a[0]):
            outs.append(a[0])
        for i, x in enumerate(a):
            if _is_ap(x) and not (i == 0 and "out" not in k):
                ins.append(x)
        for kk_, x in k.items():
            if kk_ != "out" and _is_ap(x):
                ins.append(x)
        wf = [_footprint(x, self.psum_bank) for x in outs]
        rf = []
        for x in ins:
            f = _footprint(x, self.psum_bank)
            (wf if f[0] == "P" else rf).append(f)
        for f in rf:
            for f2, o2 in self.wlog:
                if _ov(f, f2):
                    deps[id(o2)] = o2
        for f in wf:
            for f2, o2 in self.wlog:
                if _ov(f, f2):
                    deps[id(o2)] = o2
            for f2, o2 in self.rlog:
                if _ov(f, f2):
                    deps[id(o2)] = o2
        for f in wf:
            self.wlog = [(f2, o2) for f2, o2 in self.wlog if not _covers(f, f2)]
            self.rlog = [(f2, o2) for f2, o2 in self.rlog if not _covers(f, f2)]
            self.wlog.append((f, op))
        for f in rf:
            if f[0] == "D":
                continue
            done = False
            if not op.is_dma:
                for i, (f2, o2) in enumerate(self.rlog):
                    if f2 == f and o2.eng == op.eng and not o2.is_dma:
                        self.rlog[i] = (f, op)
                        done = True
                        break
            if not done:
                self.rlog.append((f, op))

    def add(self, eng, fn, reads=(), writes=(), dma=False):
        rec = _Rec()
        fn(rec)
        op = Op(eng, rec.call, dma)
        import sys as _sys
        f = _sys._getframe(1)
        op.tag = f"L{f.f_lineno}"
        if f.f_back is not None and f.f_code.co_name in ("<lambda>", "A", "load_w", "dma_out", "sink"):
            op.tag += f"<L{f.f_back.f_lineno}"
            if f.f_back.f_back is not None:
                op.tag += f"<L{f.f_back.f_back.f_lineno}"
        op.idx = len(self.ops[eng])
        deps = {}
        for r in reads:
            w = self.last_writer.get(r)
            if w is not None:
                deps[id(w)] = w
        for r in writes:
            w = self.last_writer.get(r)
            if w is not None:
                deps[id(w)] = w
            for rd in self.readers.get(r, ()):
                deps[id(rd)] = rd
        for b in self.barrier_ops:
            deps[id(b)] = b
        self._addr_deps(op, rec.call, deps)
        for d in deps.values():
            if d.eng == eng and not d.is_dma and not dma:
                if eng == PE or eng == SP:
                    continue
            op.deps.append(d)
            d.signal = True
        for r in writes:
            self.last_writer[r] = op
            self.readers[r] = []
        for r in reads:
            if r in writes:
                continue
            self.readers.setdefault(r, []).append(op)
        if dma:
            op.signal = True
        self.ops[eng].append(op)
        return op

    def barrier(self):
        bl = []
        for e in ENGS:
            ops = self.ops[e]
            last_c = None
            for o in reversed(ops):
                if not o.is_dma:
                    last_c = o
                    break
            if last_c is not None:
                bl.append(last_c)
            n = 0
            for o in reversed(ops):
                if o.is_dma:
                    bl.append(o)
                    n += 1
                    if n >= N_DMA_SEMS:
                        break
        self.barrier_ops = bl
        self.last_writer = {}
        self.readers = {}
        self.wlog = []
        self.rlog = []

    def emit(self, final_wait_ops=()):
        nc = self.nc
        sems = {e: nc.alloc_semaphore(name=f"s_{e}") for e in ENGS}
        dsems = {e: [nc.alloc_semaphore(name=f"d_{e}{i}") for i in range(N_DMA_SEMS)]
                 for e in (SP, POOL, ACT)}
        for e in ENGS:
            seq = 0
            nd = 0
            for op in self.ops[e]:
                if op.is_dma:
                    op.dsem = dsems[e][nd % N_DMA_SEMS]
                    op.dval = 16 * (nd // N_DMA_SEMS + 1)
                    op.nd = nd
                    nd += 1
                elif op.signal:
                    seq += 1
                    op.seq = seq
        engobj = {PE: "tensor", ACT: "scalar", DVE: "vector", POOL: "gpsimd", SP: "sync"}
        with nc.Block() as block:
            for e in ENGS:
                ops = self.ops[e]

                def body(eng, ops=ops, e=e):
                    waited = {}
                    for op in ops:
                        need = {}
                        for d in op.deps:
                            if d.is_dma:
                                key = ("d", d.eng, d.nd % N_DMA_SEMS)
                                val = d.dval
                                sem = d.dsem
                            else:
                                key = ("e", d.eng)
                                val = d.seq
                                sem = sems[d.eng]
                            if need.get(key, (None, 0))[1] < val:
                                need[key] = (sem, val)
                        if op.is_dma and op.nd >= N_DMA_SEMS:
                            key = ("d", e, op.nd % N_DMA_SEMS)
                            val = op.dval - 16
                            if need.get(key, (None, 0))[1] < val:
                                need[key] = (op.dsem, val)
                        for key, (sem, val) in need.items():
                            if waited.get(key, 0) >= val:
                                continue
                            eng.wait_ge(sem, val)
                            waited[key] = val
                        name, a, k = op.fn
                        ins = getattr(eng, name)(*a, **k)
                        if ANNOTATE:
                            ins.annotate(op.tag)
                        if op.is_dma:
                            ins.then_inc(op.dsem, 16)
                        elif op.signal:
                            ins.then_inc(sems[e], 1)
                    if e == SP:
                        for fo in final_wait_ops:
                            eng.wait_ge(fo.dsem, fo.dval)

                getattr(block, engobj[e])(body)


class Arena:
    def __init__(self, nc, lo=16640, hi=229000):
        self.nc = nc
        self.lo = lo
        self.hi = hi
        self.top = lo
        self.n = 0

    def alloc(self, name, shape, dtype):
        per = 1
        for s in shape[1:]:
            per *= s
        nbytes = per * (4 if dtype == F32 else 2)
        nbytes = (nbytes + 63) // 64 * 64
        assert self.top + nbytes <= self.hi, f"SBUF arena overflow at {name}: {self.top}+{nbytes}"
        h = self.nc.alloc_sbuf_tensor_at(f"{name}_{self.n}", list(shape), dtype, offset=self.top)
        self.n += 1
        self.top += nbytes
        return h

    def mark(self):
        return self.top

    def release(self, m):
        self.top = m


C_ID, C_ONE, C_MF, C_MB, C_RH, C_RR, C_EPS, C_RM, C_SM = 0, 128, 256, 320, 384, 512, 576, 577, 581
NCST = 581 + T


def host_consts():
    c = np.zeros((128, NCST), np.float32)
    c[:, C_ID:C_ID + 128] = np.eye(128, dtype=np.float32)
    c[:, C_ONE:C_ONE + 128] = 1.0
    s = np.arange(64)[:, None]
    t = np.arange(64)[None, :]
    c[:64, C_MF:C_MF + 64] = (s <= t) & (s // CH == t // CH)
    c[:64, C_MB:C_MB + 64] = (s >= t) & (s // CH == t // CH)
    for j in range(4):
        c[:64, C_RM + j] = (np.arange(64) // CH == j)
    def rmat(dim):
        r = np.zeros((128, 128), np.float32)
        q = dim // 4
        for i in range(q):
            r[q + i, i] = -1.0
            r[i, q + i] = 1.0
            r[3 * q + i, 2 * q + i] = -1.0
            r[2 * q + i, 3 * q + i] = 1.0
        return r
    c[:, C_RH:C_RH + 128] = rmat(128)
    c[:, C_RR:C_RR + 64] = rmat(64)[:, :64]
    c[:, C_EPS] = EPS
    sm = np.ones(T, np.float32)
    sm[::CH] = 0.0
    c[:, C_SM:C_SM + T] = sm[None, :]
    return c


def host_rope(half):
    pos = np.arange(half * TL, (half + 1) * TL)
    row = (pos // 64).astype(np.float32)
    col = (pos % 64).astype(np.float32)
    out = np.zeros((128, 4, TL), np.float32)
    for idx, dim in ((0, 128), (2, 64)):
        ad = dim // 2
        inv = (10000.0 ** (-np.arange(0, ad, 2, dtype=np.float32) / ad)).astype(np.float32)
        ar = row[:, None] * inv
        ac = col[:, None] * inv
        ang = np.concatenate([ar, ar, ac, ac], axis=-1).astype(np.float32)
        out[:dim, idx, :] = np.cos(ang).T
        out[:dim, idx + 1, :] = np.sin(ang).T
    return out


V_C, V_BM, V_GQ, V_GK, V_MQG, V_MKG, V_LB, V_HNG, V_FG = 0, 32, 176, 177, 178, 182, 184, 200, 201
NVEC = 217


def host_vec(inp, l, b):
    v = np.zeros((128, NVEC), np.float32)
    cc = np.stack([inp["c"][b], inp["c_ctx"]], axis=-1)
    v[:, V_C:V_C + 32] = cc.reshape(16, 128, 2).transpose(1, 0, 2).reshape(128, 32)
    v[:, V_BM:V_BM + 144] = inp["b_mod"][l].reshape(144, 128).T
    v[:, V_GQ] = inp["gqa_q_gain"][l]
    v[:, V_GK] = inp["gqa_k_gain"][l]
    v[:, V_MQG:V_MQG + 4] = inp["mla_q_gain"][l].reshape(4, 128).T
    v[:, V_MKG:V_MKG + 2] = inp["mla_kv_gain"][l].reshape(2, 128).T
    lg = inp["hgrn_lb_logits"]
    v[:, V_LB:V_LB + 16] = lg.reshape(2, 2, 4, 128).transpose(3, 0, 1, 2).reshape(128, 16)
    v[:, V_HNG] = inp["hgrn_norm_gain"][l]
    v[:, V_FG:V_FG + 16] = inp["final_gain"].reshape(16, 128).T
    return v


class Ctx:
    pass


def setup_common(nc, P, ar, CST, VEC, bf16_bank=True):
    g = Ctx()
    g.nc, g.P, g.ar = nc, P, ar
    g.ps = [nc.alloc_psum_tensor(f"ps{i}", [128, 512], F32) for i in range(7 if bf16_bank else 8)]
    if bf16_bank:
        g.psb = nc.alloc_psum_tensor("psb", [128, 1024], BF16)
    g.cst = ar.alloc("cst", [128, NCST], F32)
    g.vec = ar.alloc("vec", [128, NVEC], F32)
    g.idb = ar.alloc("idb", [128, 128], BF16)
    g.oneb = ar.alloc("oneb", [128, 128], BF16)
    g.rhb = ar.alloc("rhb", [128, 128], BF16)
    g.rrb = ar.alloc("rrb", [128, 64], BF16)
    g.mods = ar.alloc("mods", [128, 144, 2], F32)
    g.onep = ar.alloc("onep", [128, 144, 2], F32)
    g.hgate = ar.alloc("hgate", [128, 144, 2], F32)
    P.add(SP, lambda e: e.dma_start(out=g.cst[:], in_=CST), writes=["cst"], dma=True)
    P.add(SP, lambda e: e.dma_start(out=g.vec[:], in_=VEC), writes=["vec"], dma=True)
    P.add(DVE, lambda e: e.tensor_copy(out=g.idb[:], in_=g.cst[:, C_ID:C_ID + 128]), reads=["cst"], writes=["idb"])
    P.add(DVE, lambda e: e.tensor_copy(out=g.oneb[:], in_=g.cst[:, C_ONE:C_ONE + 128]), reads=["cst"], writes=["oneb"])
    P.add(DVE, lambda e: e.tensor_copy(out=g.rhb[:], in_=g.cst[:, C_RH:C_RH + 128]), reads=["cst"], writes=["rhb"])
    P.add(DVE, lambda e: e.tensor_copy(out=g.rrb[:], in_=g.cst[:, C_RR:C_RR + 64]), reads=["cst"], writes=["rrb"])
    g.eps = g.cst[:, C_EPS:C_EPS + 1]
    for i, t in enumerate(g.ps):
        P.psum_bank[t.name] = i
    if bf16_bank:
        P.psum_bank[g.psb.name] = 7
    return g


def derive_mods(g):
    P = g.P
    P.add(DVE, lambda e: e.tensor_scalar_add(out=g.onep[:], in0=g.mods[:], scalar1=1.0), reads=["mods"], writes=["onep"])
    P.add(DVE, lambda e: e.tensor_scalar_mul(out=g.hgate[:], in0=g.mods[:], scalar1=0.5), reads=["mods"], writes=["hgate"])


def col_rstd(g, srcs, n_feat, rstd_ap, t0, n, sq_bufs, keys_r, key_w, tmp):
    P = g.P
    psn = g.ps[0]
    nk = len(srcs)
    for i, (ap, rk) in enumerate(srcs):
        sb = sq_bufs[i % 2]
        P.add(ACT, lambda e, ap=ap, sb=sb: e.activation(out=sb[:, 0:n], in_=ap, func=AF.Square),
              reads=rk, writes=[("sq", i % 2)])
        P.add(PE, lambda e, sb=sb, i=i: e.matmul(psn[:, 0:n], lhsT=g.oneb[:], rhs=sb[:, 0:n], start=(i == 0), stop=(i == nk - 1)),
              reads=[("sq", i % 2), "oneb"], writes=[("ps", 0)])
    P.add(ACT, lambda e: e.activation(out=tmp[:, 0:n], in_=psn[:, 0:n], func=AF.Sqrt, bias=g.eps, scale=1.0 / n_feat),
          reads=["cst"], writes=[("ps", 0), "rstd_tmp"])
    P.add(DVE, lambda e: e.reciprocal(out=rstd_ap, in_=tmp[:, 0:n]), reads=["rstd_tmp"], writes=[key_w])


def norm_mod(g, xT, hT, i_shift, i_scale, bufs, tiles=TILES):
    P = g.P
    for ti, (t0, n) in enumerate(tiles):
        which = 0 if t0 < TL else 1
        srcs = [(xT[:, c, t0:t0 + n], [("x", c, ti)]) for c in range(KC)]
        col_rstd(g, srcs, D, bufs.rstd[:, t0:t0 + n], t0, n, bufs.sq, None, ("rstd", ti), bufs.rtmp)
        for c in range(KC):
            tb = bufs.tmp[c % 2]
            P.add(DVE, lambda e, c=c, tb=tb: e.tensor_tensor(out=tb[:, 0:n], in0=xT[:, c, t0:t0 + n], in1=bufs.rstd[:, t0:t0 + n], op=ALU.mult),
                  reads=[("x", c, ti), ("rstd", ti)], writes=[("nt", c % 2)])
            P.add(ACT, lambda e, c=c, tb=tb: e.activation(out=hT[:, c, t0:t0 + n], in_=tb[:, 0:n], func=AF.Identity,
                                                         bias=g.mods[:, i_shift * 16 + c, which:which + 1],
                                                         scale=g.onep[:, i_scale * 16 + c, which:which + 1]),
                  reads=[("nt", c % 2), "mods", "onep"], writes=[("h", c, ti)])


def load_w(g, eng, dst, src, key):
    return g.P.add(eng, lambda e: e.dma_start(out=dst, in_=src), writes=[key], dma=True)


def ffn(g, xT, hT, w_in, w_out, i_gate, bufs, tiles=TILES, GS=8):
    P = g.P
    ps = g.ps
    cnt = 0
    cnt2 = 0
    nld = 0
    for g0 in range(0, FC, GS):
        gs = min(GS, FC - g0)
        for jl in range(gs):
            j = g0 + jl
            b = nld % 2
            nld += 1
            load_w(g, POOL, bufs.wg[b][:], w_in[:, j * 128:(j + 1) * 128].rearrange("(c p) n -> p c n", p=128), ("wg", b))
            load_w(g, POOL, bufs.wu[b][:], w_in[:, FH + j * 128:FH + (j + 1) * 128].rearrange("(c p) n -> p c n", p=128), ("wu", b))
            for ti, (t0, n) in enumerate(tiles):
                q = cnt % 2
                cnt += 1
                pg, pu = ps[1 + q], ps[3 + q]
                hk = [("h", c, ti) for c in range(KC)]
                for c in range(KC):
                    P.add(PE, lambda e, c=c, pg=pg, b=b: e.matmul(pg[:, 0:n], lhsT=bufs.wg[b][:, c, :], rhs=hT[:, c, t0:t0 + n], start=(c == 0), stop=(c == KC - 1)),
                          reads=[("wg", b)] + (hk if c == 0 else []), writes=[("ps", 1 + q)])
                for c in range(KC):
                    P.add(PE, lambda e, c=c, pu=pu, b=b: e.matmul(pu[:, 0:n], lhsT=bufs.wu[b][:, c, :], rhs=hT[:, c, t0:t0 + n], start=(c == 0), stop=(c == KC - 1)),
                          reads=[("wu", b)], writes=[("ps", 3 + q)])
                sg = bufs.sgt[q]
                P.add(ACT, lambda e, pg=pg, sg=sg: e.activation(out=sg[:, 0:n], in_=pg[:, 0:n], func=AF.Silu),
                      writes=[("ps", 1 + q), ("sgt", q)])
                P.add(DVE, lambda e, pu=pu, sg=sg, jl=jl: e.tensor_tensor(out=bufs.aT[:, jl, t0:t0 + n], in0=pu[:, 0:n], in1=sg[:, 0:n], op=ALU.mult),
                      reads=[("sgt", q)], writes=[("ps", 3 + q), ("a", jl, ti)])
        for db in range(4):
            b = db % 2
            load_w(g, POOL, bufs.wo[b][:, 0:gs, :], w_out[g0 * 128:(g0 + gs) * 128, db * 512:(db + 1) * 512].rearrange("(c p) n -> p c n", p=128), ("wo", b))
            for dc in range(4):
                ch = db * 4 + dc
                for ti, (t0, n) in enumerate(tiles):
                    which = 0 if t0 < TL else 1
                    q = cnt2 % 2
                    cnt2 += 1
                    po = ps[5 + q]
                    for jl in range(gs):
                        P.add(PE, lambda e, jl=jl, po=po, b=b, dc=dc: e.matmul(po[:, 0:n], lhsT=bufs.wo[b][:, jl, dc * 128:(dc + 1) * 128], rhs=bufs.aT[:, jl, t0:t0 + n], start=(jl == 0), stop=(jl == gs - 1)),
                              reads=[("wo", b), ("a", jl, ti)], writes=[("ps", 5 + q)])
                    P.add(DVE, lambda e, po=po, ch=ch, which=which: e.scalar_tensor_tensor(out=xT[:, ch, t0:t0 + n], in0=po[:, 0:n], scalar=g.hgate[:, i_gate * 16 + ch, which:which + 1], in1=xT[:, ch, t0:t0 + n], op0=ALU.mult, op1=ALU.add),
                          reads=["hgate"], writes=[("ps", 5 + q), ("x", ch, ti)])


def alloc_ffn_bufs(ar):
    b = Ctx()
    b.rstd = ar.alloc("rstd", [128, T], F32)
    b.rtmp = ar.alloc("rtmp", [128, 512], F32)
    b.sq = [ar.alloc("sq", [128, 512], BF16) for _ in range(2)]
    b.tmp = [ar.alloc("ntmp", [128, 512], F32) for _ in range(2)]
    b.aT = ar.alloc("aT", [128, 8, T], BF16)
    b.wg = [ar.alloc("wg", [128, KC, 128], BF16) for _ in range(2)]
    b.wu = [ar.alloc("wu", [128, KC, 128], BF16) for _ in range(2)]
    b.wo = [ar.alloc("wo", [128, 8, 512], BF16) for _ in range(2)]
    b.sgt = [ar.alloc("sgt", [128, 512], F32) for _ in range(2)]
    return b


def compute_mods(g, ar, w_mod):
    P = g.P
    m = ar.mark()
    scb = ar.alloc("scb", [128, KC, 2], BF16)
    wm = [ar.alloc("wm", [128, KC, 512], BF16) for _ in range(2)]
    P.add(ACT, lambda e: e.activation(out=scb[:].rearrange("p c w -> p (c w)"), in_=g.vec[:, V_C:V_C + 32], func=AF.Silu),
          reads=["vec"], writes=["scb"])
    psm = g.ps[0]
    for blk in range(36):
        b = blk % 2
        load_w(g, POOL, wm[b][:], w_mod[:, blk * 512:(blk + 1) * 512].rearrange("(c p) n -> p c n", p=128), ("wm", b))
        for jj in range(4):
            j = blk * 4 + jj
            for k in range(KC):
                P.add(PE, lambda e, b=b, jj=jj, j=j, k=k: e.matmul(psm[:, 2 * j:2 * j + 2], lhsT=wm[b][:, k, jj * 128:(jj + 1) * 128], rhs=scb[:, k, :], start=(k == 0), stop=(k == KC - 1)),
                      reads=[("wm", b), "scb"], writes=[("ps", 0)])
    P.add(DVE, lambda e: e.tensor_tensor(out=g.mods[:], in0=psm[:, 0:288].rearrange("p (j w) -> p j w", w=2),
                                         in1=g.vec[:, V_BM:V_BM + 144].rearrange("p (j o) -> p j o", o=1).to_broadcast([128, 144, 2]), op=ALU.add),
          reads=["vec"], writes=[("ps", 0), "mods"])
    derive_mods(g)
    P.barrier()
    ar.release(m)


def rope_apply(g, src_bf, np_, rmat, cos, sin, dst, ps_i, t0, n, tmp1, tmp2, rk, wk):
    P = g.P
    pr = g.ps[ps_i]
    P.add(PE, lambda e: e.matmul(pr[0:np_, 0:n], lhsT=rmat, rhs=src_bf[0:np_, t0:t0 + n], start=True, stop=True),
          reads=rk + ["rhb", "rrb"], writes=[("ps", ps_i)])
    P.add(DVE, lambda e: e.tensor_tensor(out=tmp1[0:np_, 0:n], in0=pr[0:np_, 0:n], in1=sin[0:np_, t0:t0 + n], op=ALU.mult),
          reads=["rope"], writes=[("ps", ps_i), "rt1"])
    P.add(POOL, lambda e: e.tensor_tensor(out=tmp2[0:np_, 0:n], in0=src_bf[0:np_, t0:t0 + n], in1=cos[0:np_, t0:t0 + n], op=ALU.mult),
          reads=rk + ["rope"], writes=["rt2"])
    P.add(DVE, lambda e: e.tensor_tensor(out=dst[0:np_, t0:t0 + n], in0=tmp1[0:np_, 0:n], in1=tmp2[0:np_, 0:n], op=ALU.add),
          reads=["rt1", "rt2"], writes=wk)


STOP_AFTER = None


def build_A(l, last, F=None, half=0):
    if F is None:
        nc = bass.Bass("TRN2", target_bir_lowering=False)
        dt = lambda n, s, d=F32, k="ExternalInput": nc.dram_tensor(n, s, d, kind=k).ap()
    else:
        nc = F.nc
        dt = lambda n, s, d=F32, k="ExternalInput": F.tensor(n, s, d, k, l, half)
    XT = dt("xT", [D, T]); VEC = dt("vec", [128, NVEC]); CST = dt("cst", [128, NCST]); ROPE = dt("rope", [128, 4, TL])
    WMOD = dt("w_mod", [D, NMOD * D]); W1I = dt("w1i", [D, 2 * FH]); W1O = dt("w1o", [FH, D]); WIN = dt("w_in", [D, 4928])
    WUQ = dt("w_uq", [512, 768]); WUKV = dt("w_ukv", [256, 1024])
    o = "ExternalOutput"
    XO = dt("xo", [D, T], F32, o); MODS = dt("mods", [128, 288], F32, o)
    GQ = dt("gq", [128, 8, T], BF16, o); GK = dt("gk", [128, 2, T], BF16, o); GV = dt("gv", [T, 256], BF16, o)
    MQN = dt("mqn", [128, 4, T], BF16, o); MQR = dt("mqr", [64, 4, T], BF16, o); MKN = dt("mkn", [128, 4, T], BF16, o)
    MKR = dt("mkr", [64, T], BF16, o); MV = dt("mv", [T, 512], BF16, o)
    HQE = dt("hqe", [128, 2, 4, T], BF16, o); HKT = dt("hkt", [128, 2, 4, T], BF16, o); HKE = dt("hke", [T, 2, 4, 128], BF16, o)
    HV = dt("hv", [T, 512], BF16, o); HSC = dt("hsc", [128, 2, 3, 4, NCH], F32, o); HG = dt("hg", [128, 4, T], F32, o)

    if F is None:
        P = Prog(nc)
        ar = Arena(nc)
        g = setup_common(nc, P, ar, CST, VEC)
    else:
        P, ar, g = F.P, F.ar, F.g
    fin = []
    hT = ar.alloc("hT", [128, KC, T], BF16)
    m0 = ar.mark()
    xT = ar.alloc("xT", [128, KC, T], F32)
    for c in range(KC):
        P.add(SP, lambda e, c=c: e.dma_start(out=xT[:, c, :], in_=XT[c * 128:(c + 1) * 128, :]),
              writes=[("x", c, 0), ("x", c, 1), ("x", c, 2)], dma=True)
    if F is None:
        compute_mods(g, ar, WMOD)
        fin.append(P.add(SP, lambda e: e.dma_start(out=MODS, in_=g.mods[:].rearrange("p j w -> p (j w)")), reads=["mods"], dma=True))
    if STOP_AFTER == "mods":
        P.emit(final_wait_ops=fin)
        return nc
    fb = alloc_ffn_bufs(ar)
    norm_mod(g, xT, hT, 0, 1, fb)
    ffn(g, xT, hT, W1I, W1O, 2, fb)
    for c in range(KC):
        fin.append(P.add(SP, lambda e, c=c: e.dma_start(out=XO[c * 128:(c + 1) * 128, :], in_=xT[:, c, :]),
                         reads=[("x", c, 0), ("x", c, 1), ("x", c, 2)], dma=True))
    norm_mod(g, xT, hT, 3, 4, fb)
    P.barrier()
    ar.release(m0)

    rope = ar.alloc("rope", [128, 4, TL], F32)
    P.add(SP, lambda e: e.dma_start(out=rope[:], in_=ROPE), writes=["rope"], dma=True)
    cosH, sinH, cosR, sinR = rope[:, 0, :], rope[:, 1, :], rope[:, 2, :], rope[:, 3, :]
    wc = [ar.alloc("wc", [128, KC, 128], BF16) for _ in range(3)]
    rstd = ar.alloc("rstd2", [128, T], F32)
    rtmp = ar.alloc("rtmp2", [128, 512], F32)
    sq = [ar.alloc("sq2", [128, 512], BF16) for _ in range(2)]
    NS = 10
    stg = [ar.alloc("stg", [128, T], F32) for _ in range(NS)]
    sb16 = [ar.alloc("sb16", [128, T], BF16) for _ in range(6)]
    rt1 = ar.alloc("rt1", [128, 512], F32)
    rt2 = ar.alloc("rt2", [128, 512], F32)
    state = {"nw": 0, "nps": 0, "ndma": 0}

    def proj_fm(col0, ncols, sink):
        b = state["nw"] % 3
        state["nw"] += 1
        load_w(g, POOL, wc[b][:, :, 0:ncols], WIN[:, col0:col0 + ncols].rearrange("(c p) n -> p c n", p=128), ("wc", b))
        for ti, (t0, n) in enumerate(TILES):
            pi = 1 + state["nps"] % 4
            state["nps"] += 1
            pp = g.ps[pi]
            for c in range(KC):
                P.add(PE, lambda e, c=c, pp=pp, b=b: e.matmul(pp[0:ncols, 0:n], lhsT=wc[b][:, c, 0:ncols], rhs=hT[:, c, t0:t0 + n], start=(c == 0), stop=(c == KC - 1)),
                      reads=[("wc", b)] + ([("h", cc, ti) for cc in range(KC)] if c == 0 else []), writes=[("ps", pi)])
            sink(ti, t0, n, pp, pi)

    def evac_to(dst, key, np_=128, eng=ACT, func=None):
        def sink(ti, t0, n, pp, pi):
            if func is not None:
                P.add(ACT, lambda e: e.activation(out=dst[0:np_, t0:t0 + n], in_=pp[0:np_, 0:n], func=func), writes=[("ps", pi), (key, ti)])
            elif eng == ACT:
                P.add(ACT, lambda e: e.copy(out=dst[0:np_, t0:t0 + n], in_=pp[0:np_, 0:n]), writes=[("ps", pi), (key, ti)])
            else:
                P.add(DVE, lambda e: e.tensor_copy(out=dst[0:np_, t0:t0 + n], in_=pp[0:np_, 0:n]), writes=[("ps", pi), (key, ti)])
        return sink

    def k3(key):
        return [(key, 0), (key, 1), (key, 2)]

    def dma_out(dst, src, rk):
        q = (SP, ACT)[state["ndma"] % 1]
        state["ndma"] += 1
        fin.append(P.add(q, lambda e: e.dma_start(out=dst, in_=src), reads=rk, dma=True))

    for hh in range(10):
        s_raw = stg[hh % 2]
        kraw = f"raw{hh % 2}"
        proj_fm(hh * 128, 128, evac_to(s_raw, kraw))
        gcol = V_GQ if hh < 8 else V_GK
        qn = sb16[hh % 2]
        kqn = f"qn{hh % 2}"
        for ti, (t0, n) in enumerate(TILES):
            col_rstd(g, [(s_raw[:, t0:t0 + n], [(kraw, ti)])], 128, rstd[:, t0:t0 + n], t0, n, sq, None, ("rstd", ti), rtmp)
            P.add(DVE, lambda e, t0=t0, n=n: e.scalar_tensor_tensor(out=qn[:, t0:t0 + n], in0=s_raw[:, t0:t0 + n], scalar=g.vec[:, gcol:gcol + 1], in1=rstd[:, t0:t0 + n], op0=ALU.mult, op1=ALU.mult),
                  reads=[(kraw, ti), ("rstd", ti), "vec"], writes=[(kqn, ti)])
        ob = sb16[2 + hh % 2]
        kob = f"ob{hh % 2}"
        for ti, (t0, n) in enumerate(TILES[:2]):
            rope_apply(g, qn, 128, g.rhb[:], cosH, sinH, ob, 5 + ti % 2, t0, n, rt1, rt2, [(kqn, ti)], [(kob, ti)])
        P.add(POOL, lambda e: e.tensor_copy(out=ob[:, TL:T], in_=qn[:, TL:T]), reads=[(kqn, 2)], writes=[(kob, 2)])
        dma_out(GQ[:, hh, :] if hh < 8 else GK[:, hh - 8, :], ob[:], k3(kob))

    lbv = ar.alloc("lbv", [128, 2, 4], F32)
    oml = ar.alloc("oml", [128, 2, 4], F32)
    lg = g.vec[:, V_LB:V_LB + 16].rearrange("p (d l h) -> p d l h", d=2, l=2)
    if l == 0:
        P.add(DVE, lambda e: e.memset(lbv[:], 0.0), writes=["lbv"])
    else:
        P.add(DVE, lambda e: e.tensor_tensor(out=lbv[:], in0=lg[:, :, 1, :], in1=lg[:, :, 0, :], op=ALU.subtract), reads=["vec"], writes=["lbv"])
        P.add(ACT, lambda e: e.activation(out=lbv[:], in_=lbv[:], func=AF.Sigmoid), writes=["lbv"])
    P.add(DVE, lambda e: e.tensor_scalar(out=oml[:], in0=lbv[:], scalar1=-1.0, scalar2=1.0, op0=ALU.mult, op1=ALU.add), reads=["lbv"], writes=["oml"])
    hsc = ar.alloc("hsc", [128, 2, 3, 4, NCH], F32)
    smask = g.cst[:, C_SM:C_SM + T]
    ket = [ar.alloc("ket", [128, 10, 128], BF16) for _ in range(2)]
    nket = 0
    for h in range(4):
        sq_raw = stg[2]
        proj_fm(1536 + h * 128, 128, evac_to(sq_raw, "hq"))
        for d in range(2):
            z = stg[3]
            proj_fm((2560 if d == 0 else 3072) + h * 128, 128, evac_to(z, "z", func=AF.Sigmoid))
            f, lf, kk, cum, G = stg[4], stg[5], stg[6], stg[7], stg[8]
            A = lambda eng, fn, r, w: P.add(eng, fn, reads=r, writes=w)
            A(DVE, lambda e, d=d, h=h: e.tensor_scalar(out=f[:], in0=z[:], scalar1=oml[:, d, h:h + 1], scalar2=lbv[:, d, h:h + 1], op0=ALU.mult, op1=ALU.add),
              k3("z") + ["oml", "lbv"], ["f"])
            A(ACT, lambda e: e.activation(out=lf[:], in_=f[:], func=AF.Ln), ["f"], ["lf"])
            A(POOL, lambda e: e.tensor_scalar(out=kk[:], in0=f[:], scalar1=-1.0, scalar2=1.0, op0=ALU.mult, op1=ALU.add), ["f"], ["kk"])
            A(DVE, lambda e: e.tensor_tensor_scan(out=cum[:], data0=smask, data1=lf[:], initial=0.0, op0=ALU.mult, op1=ALU.add), ["lf", "cst"], ["cum"])
            cum3 = cum[:].rearrange("p (c t) -> p c t", t=CH)
            G3 = G[:].rearrange("p (c t) -> p c t", t=CH)
            if d == 0:
                Gs, G3s, kG = cum, cum3, "cum"
                last_ap = cum3[:, :, CH - 1:CH]
            else:
                A(POOL, lambda e: e.tensor_tensor(out=G[:], in0=lf[:], in1=cum[:], op=ALU.subtract), ["lf", "cum"], ["G"])
                A(DVE, lambda e: e.tensor_tensor(out=G3, in0=G3, in1=cum3[:, :, CH - 1:CH].to_broadcast([128, NCH, CH]), op=ALU.add), ["cum"], ["G"])
                Gs, G3s, kG = G, G3, "G"
                last_ap = G3[:, :, 0:1]
            mid_ap = G3s[:, :, CH // 2:CH // 2 + 1]
            sc0 = hsc[:, d, 0, h, :].rearrange("p (c o) -> p c o", o=1)
            sc1 = hsc[:, d, 1, h, :].rearrange("p (c o) -> p c o", o=1)
            sc2 = hsc[:, d, 2, h, :].rearrange("p (c o) -> p c o", o=1)
            A(ACT, lambda e, sc0=sc0, mid_ap=mid_ap: e.activation(out=sc0, in_=mid_ap, func=AF.Exp), [kG], [("hsc", d, 0, h)])
            A(ACT, lambda e, sc1=sc1, last_ap=last_ap: e.activation(out=sc1, in_=last_ap, func=AF.Exp), [kG], [("hsc", d, 1, h)])
            A(DVE, lambda e, sc2=sc2, last_ap=last_ap, mid_ap=mid_ap: e.tensor_tensor(out=sc2, in0=last_ap, in1=mid_ap, op=ALU.subtract), [kG], [("hsc", d, 2, h)])
            A(ACT, lambda e, sc2=sc2: e.activation(out=sc2, in_=sc2, func=AF.Exp), [], [("hsc", d, 2, h)])
            Gp = stg[9]
            Gp3 = Gp[:].rearrange("p (c t) -> p c t", t=CH)
            A(DVE, lambda e, G3s=G3s, mid_ap=mid_ap: e.tensor_tensor(out=Gp3, in0=G3s, in1=mid_ap.to_broadcast([128, NCH, CH]), op=ALU.subtract), [kG], ["Gp"])
            e1, e2 = stg[4], stg[5]
            A(ACT, lambda e: e.activation(out=e1[:], in_=Gp[:], func=AF.Exp), ["Gp"], ["f"])
            A(ACT, lambda e: e.activation(out=e2[:], in_=Gp[:], func=AF.Exp, scale=-1.0), ["Gp"], ["lf"])
            qe = sb16[0]
            keT = sb16[1]
            A(DVE, lambda e: e.scalar_tensor_tensor(out=qe[:], in0=sq_raw[:], scalar=float(128 ** -0.5), in1=e1[:], op0=ALU.mult, op1=ALU.mult),
              k3("hq") + ["f"], ["qe"])
            A(POOL, lambda e: e.tensor_tensor(out=keT[:], in0=kk[:], in1=e2[:], op=ALU.mult), ["kk", "lf"], ["keT"])
            dma_out(HQE[:, d, h, :], qe[:], ["qe"])
            dma_out(HKT[:, d, h, :], keT[:], ["keT"])
            kb = ket[nket % 2]
            kkb = ("ket", nket % 2)
            nket += 1
            for tt in range(10):
                P.add(PE, lambda e, tt=tt: e.transpose(out=g.psb[:, (tt % 8) * 128:(tt % 8 + 1) * 128], in_=keT[:, tt * 128:(tt + 1) * 128], identity=g.idb[:]),
                      reads=["keT", "idb"], writes=[("ps", 7)])
                P.add(DVE, lambda e, tt=tt, kb=kb: e.tensor_copy(out=kb[:, tt, :], in_=g.psb[:, (tt % 8) * 128:(tt % 8 + 1) * 128]),
                      writes=[("ps", 7), kkb])
            dma_out(HKE[:, d, h, :].rearrange("(tt p) k -> p tt k", p=128), kb[:], [kkb])
        og = stg[2 + 0]
    for h in range(4):
        gg = stg[h % 2]
        proj_fm(3584 + h * 128, 128, evac_to(gg, f"gg{h % 2}", func=AF.Silu))
        dma_out(HG[:, h, :], gg[:], k3(f"gg{h % 2}"))
    dma_out(HSC, hsc[:], [("hsc", d, i, h) for d in range(2) for i in range(3) for h in range(4)])

    wuq = ar.alloc("wuq", [128, 4, 768], BF16)
    wukv = ar.alloc("wukv", [128, 2, 1024], BF16)
    load_w(g, POOL, wuq[:], WUQ.rearrange("(c p) n -> p c n", p=128), "wuq")
    load_w(g, POOL, wukv[:], WUKV.rearrange("(c p) n -> p c n", p=128), "wukv")
    for nm, col0, nchk, gcol0, nfeat in (("cq", 4096, 4, V_MQG, 512), ("ckv", 4608, 2, V_MKG, 256)):
        raws = [stg[c] for c in range(nchk)]
        for c in range(nchk):
            proj_fm(col0 + c * 128, 128, evac_to(raws[c], f"{nm}{c}"))
        for ti, (t0, n) in enumerate(TILES):
            col_rstd(g, [(raws[c][:, t0:t0 + n], [(f"{nm}{c}", ti)]) for c in range(nchk)], nfeat, rstd[:, t0:t0 + n], t0, n, sq, None, ("rstd", ti), rtmp)
        cn = [sb16[c] for c in range(nchk)]
        for c in range(nchk):
            for ti, (t0, n) in enumerate(TILES):
                P.add(DVE, lambda e, c=c, t0=t0, n=n: e.scalar_tensor_tensor(out=cn[c][:, t0:t0 + n], in0=raws[c][:, t0:t0 + n], scalar=g.vec[:, gcol0 + c:gcol0 + c + 1], in1=rstd[:, t0:t0 + n], op0=ALU.mult, op1=ALU.mult),
                      reads=[(f"{nm}{c}", ti), ("rstd", ti), "vec"], writes=[(f"{nm}n{c}", ti)])
        if nm == "cq":
            for h in range(4):
                on = stg[4 + h % 2]
                onb = ket[h % 2][:].rearrange("p a b -> p (a b)")
                orb = ket[(h + 1) % 2][:].rearrange("p a b -> p (a b)")
                for ti, (t0, n) in enumerate(TILES):
                    pi = 1 + state["nps"] % 4
                    state["nps"] += 1
                    pp = g.ps[pi]
                    for c in range(4):
                        P.add(PE, lambda e, c=c, pp=pp, h=h: e.matmul(pp[:, 0:n], lhsT=wuq[:, c, h * 192:h * 192 + 128], rhs=cn[c][:, t0:t0 + n], start=(c == 0), stop=(c == 3)),
                              reads=["wuq"] + [(f"cqn{cc}", ti) for cc in range(4)], writes=[("ps", pi)])
                    P.add(ACT, lambda e, pp=pp, t0=t0, n=n, onb=onb: e.copy(out=onb[:, t0:t0 + n], in_=pp[:, 0:n]), writes=[("ps", pi), ("mqn_s", h % 2, ti)])
                dma_out(MQN[:, h, :], onb, [("mqn_s", h % 2, ti) for ti in range(3)])
                qr_raw = sb16[4]
                for ti, (t0, n) in enumerate(TILES):
                    pi = 1 + state["nps"] % 4
                    state["nps"] += 1
                    pp = g.ps[pi]
                    for c in range(4):
                        P.add(PE, lambda e, c=c, pp=pp, h=h: e.matmul(pp[0:64, 0:n], lhsT=wuq[:, c, h * 192 + 128:h * 192 + 192], rhs=cn[c][:, t0:t0 + n], start=(c == 0), stop=(c == 3)),
                              reads=["wuq"] + [(f"cqn{cc}", ti) for cc in range(4)], writes=[("ps", pi)])
                    P.add(ACT, lambda e, pp=pp, t0=t0, n=n: e.copy(out=qr_raw[0:64, t0:t0 + n], in_=pp[0:64, 0:n]), writes=[("ps", pi), ("qrraw", ti)])
                qro = sb16[5]
                for ti, (t0, n) in enumerate(TILES[:2]):
                    rope_apply(g, qr_raw, 64, g.rrb[0:64, :], cosR, sinR, qro, 5 + ti % 2, t0, n, rt1, rt2, [("qrraw", ti)], [("qro", ti)])
                P.add(POOL, lambda e: e.tensor_copy(out=qro[0:64, TL:T], in_=qr_raw[0:64, TL:T]), reads=[("qrraw", 2)], writes=[("qro", 2)])
                dma_out(MQR[:, h, :], qro[0:64, :], k3("qro"))
        else:
            for h in range(4):
                onb = ket[h % 2][:].rearrange("p a b -> p (a b)")
                for ti, (t0, n) in enumerate(TILES):
                    pi = 1 + state["nps"] % 4
                    state["nps"] += 1
                    pp = g.ps[pi]
                    for c in range(2):
                        P.add(PE, lambda e, c=c, pp=pp, h=h: e.matmul(pp[:, 0:n], lhsT=wukv[:, c, h * 256:h * 256 + 128], rhs=cn[c][:, t0:t0 + n], start=(c == 0), stop=(c == 1)),
                              reads=["wukv"] + [(f"ckvn{cc}", ti) for cc in range(2)], writes=[("ps", pi)])
                    P.add(ACT, lambda e, pp=pp, t0=t0, n=n, onb=onb: e.copy(out=onb[:, t0:t0 + n], in_=pp[:, 0:n]), writes=[("ps", pi), ("mkn_s", h % 2, ti)])
                dma_out(MKN[:, h, :], onb, [("mkn_s", h % 2, ti) for ti in range(3)])
            vst = [ar.alloc("vst", [128, 512], BF16) for _ in range(2)]
            wv4 = wukv[:].rearrange("p c (h x) -> p c h x", x=256)
            for tt in range(10):
                pi = 1 + state["nps"] % 4
                state["nps"] += 1
                pp = g.ps[pi]
                for h in range(4):
                    for c in range(2):
                        P.add(PE, lambda e, c=c, pp=pp, tt=tt, h=h: e.matmul(pp[:, h * 128:(h + 1) * 128], lhsT=cn[c][:, tt * 128:(tt + 1) * 128], rhs=wv4[:, c, h, 128:256], start=(c == 0), stop=(c == 1)),
                              reads=["wukv"] + [(f"ckvn{cc}", ti) for cc in range(2) for ti in range(3)], writes=[("ps", pi)])
                vb = vst[tt % 2]
                P.add(ACT, lambda e, pp=pp, vb=vb: e.copy(out=vb[:], in_=pp[:, 0:512]), writes=[("ps", pi), ("vst", tt % 2)])
                dma_out(MV[tt * 128:(tt + 1) * 128, :], vb[:], [("vst", tt % 2)])
    kr_raw = sb16[2]
    proj_fm(4864, 64, evac_to(kr_raw, "krraw", np_=64))
    kro = sb16[3]
    for ti, (t0, n) in enumerate(TILES[:2]):
        rope_apply(g, kr_raw, 64, g.rrb[0:64, :], cosR, sinR, kro, 5 + ti % 2, t0, n, rt1, rt2, [("krraw", ti)], [("kro", ti)])
    P.add(POOL, lambda e: e.tensor_copy(out=kro[0:64, TL:T], in_=kr_raw[0:64, TL:T]), reads=[("krraw", 2)], writes=[("kro", 2)])
    dma_out(MKR, kro[0:64, :], k3("kro"))

    wt = ar.alloc("wt", [128, KC, 512], BF16)
    tst = [ar.alloc("tst", [128, 512], BF16) for _ in range(2)]
    for col0, ncols, DST in ((1280, 256, GV), (2048, 512, HV)):
        load_w(g, POOL, wt[:, :, 0:ncols], WIN[:, col0:col0 + ncols].rearrange("(c p) n -> p c n", p=128), "wt")
        for tt in range(10):
            ti = 0 if tt < 4 else (1 if tt < 8 else 2)
            pi = 1 + state["nps"] % 4
            state["nps"] += 1
            pp = g.ps[pi]
            for c in range(KC):
                P.add(PE, lambda e, c=c, pp=pp, tt=tt: e.matmul(pp[:, 0:ncols], lhsT=hT[:, c, tt * 128:(tt + 1) * 128], rhs=wt[:, c, 0:ncols], start=(c == 0), stop=(c == KC - 1)),
                      reads=["wt"] + ([("h", cc, ti) for cc in range(KC)] if c == 0 else []), writes=[("ps", pi)])
            vb = tst[tt % 2]
            P.add(ACT, lambda e, pp=pp, vb=vb: e.copy(out=vb[:, 0:ncols], in_=pp[:, 0:ncols]), writes=[("ps", pi), ("tst", tt % 2)])
            dma_out(DST[tt * 128:(tt + 1) * 128, :], vb[:, 0:ncols], [("tst", tt % 2)])
    if F is not None:
        return fin
    P.emit(final_wait_ops=fin)
    return nc


DEBUG_MIX = False
NKEY = SEQ + TCX
NKC = NKEY // 128


def build_B(l, last, F=None, half=0):
    if F is None:
        nc = bass.Bass("TRN2", target_bir_lowering=False)
        dt = lambda n, s, d=F32, k="ExternalInput": nc.dram_tensor(n, s, d, kind=k).ap()
    else:
        nc = F.nc
        dt = lambda n, s, d=F32, k="ExternalInput": F.tensor_b(n, s, d, k, l, half)
    XT = dt("xT", [D, T]); VEC = dt("vec", [128, NVEC]); CST = dt("cst", [128, NCST]); MODS = dt("mods", [128, 288])
    GQ = dt("gq", [128, 8, T], BF16); GKA = dt("gk", [128, 2, NKEY], BF16); GVA = dt("gv", [NKEY, 256], BF16)
    MQN = dt("mqn", [128, 4, T], BF16); MQR = dt("mqr", [64, 4, T], BF16); MKN = dt("mkn", [128, 4, NKEY], BF16)
    MKR = dt("mkr", [64, NKEY], BF16); MVA = dt("mv", [NKEY, 512], BF16)
    HQE = dt("hqe", [128, 2, 4, T], BF16); HKT = dt("hkt", [128, 2, 4, T], BF16); HKE = dt("hke", [T, 2, 4, 128], BF16)
    HV = dt("hv", [T, 512], BF16); HSC = dt("hsc", [128, 2, 3, 4, NCH], F32); HG = dt("hg", [128, 4, T], F32)
    PKE = dt("pke", [TL, 2, 4, 128], BF16); PV = dt("pv", [TL, 512], BF16); PSC = dt("psc", [128, 2, 3, 4, 64], F32)
    WOUT = dt("w_out", [D, D]); W2I = dt("w2i", [D, 2 * FH]); W2O = dt("w2o", [FH, D])
    OUT = dt("out", [D, TL if last else T], F32, "ExternalOutput")
    MIXD = dt("mixdbg", [128, KC, T], BF16, "ExternalOutput") if DEBUG_MIX else None

    if F is None:
        P = Prog(nc)
        ar = Arena(nc)
        g = setup_common(nc, P, ar, CST, VEC)
        P.add(SP, lambda e: e.dma_start(out=g.mods[:].rearrange("p j w -> p (j w)"), in_=MODS), writes=["mods"], dma=True)
        derive_mods(g)
    else:
        P, ar, g = F.P, F.ar, F.g
    ps = g.ps
    fin = []
    pdirs = (0, 1) if F is None else ((1,) if half == 0 else (0,))

    SEGS = ((0, 0, 0, TL), (TL, 1, 0, TL), (2 * TL, 0, TL, TCX))

    def ld_feat(dst_tile, np_, name, unf_ap, idx):
        if F is None:
            P.add(SP, lambda e: e.dma_start(out=dst_tile[0:np_, :], in_=unf_ap), dma=True)
            return
        for d0, hf, s0, n in SEGS:
            src = F.mid[(name, l, hf)]
            sap = src[:, idx, s0:s0 + n] if idx is not None else src[:, s0:s0 + n]
            P.add(SP, lambda e, sap=sap, d0=d0, n=n: e.dma_start(out=dst_tile[0:np_, d0:d0 + n], in_=sap), dma=True)

    def ld_tok(dst_tile, name, unf_t, c0, c1):
        if F is None:
            P.add(SP, lambda e: e.dma_start(out=dst_tile[:], in_=unf_t[:, c0:c1].rearrange("(c p) n -> p c n", p=128)), dma=True)
            return
        for d0, hf, s0, n in SEGS:
            sap = F.mid[(name, l, hf)][s0:s0 + n, c0:c1].rearrange("(c p) n -> p c n", p=128)
            P.add(SP, lambda e, sap=sap, d0=d0, n=n: e.dma_start(out=dst_tile[:, d0 // 128:(d0 + n) // 128, :], in_=sap), dma=True)

    qtiles = TILES[:2] if last else TILES
    mixT = ar.alloc("mixT", [128, KC, T], BF16)
    m0 = ar.mark()

    oacc = ar.alloc("oacc", [128, 4, T], F32)
    m_scan = ar.mark()
    hqe = ar.alloc("hqe", [128, 4, T], BF16)
    hkt = ar.alloc("hkt", [128, 4, T], BF16)
    hke = ar.alloc("hke", [64, 20, 4, 128], BF16)
    hv = ar.alloc("hv", [64, 20, 512], BF16)
    pke = ar.alloc("pke", [64, 16, 4, 128], BF16)
    pv = ar.alloc("pv", [64, 16, 512], BF16)
    hsc = ar.alloc("hsc", [128, 2, 3, 4, NCH], F32)
    psc = ar.alloc("psc", [128, 2, 3, 4, 64], F32)
    S = ar.alloc("S", [128, 4, 128], F32)
    Sbs = [ar.alloc("Sb", [128, 4, 128], BF16) for _ in range(2)]
    stmp = ar.alloc("stmp", [128, 4, 128], F32)
    sc = ar.alloc("sc", [64, 4, 64], BF16)
    keMs = [ar.alloc("keM", [64, 4, 4, 128], BF16) for _ in range(2)]
    P.add(SP, lambda e: e.dma_start(out=hv[:], in_=HV.rearrange("(c p) n -> p c n", p=64)), dma=True)
    PVs = PV if F is None else F.mid[("hv", l, 1 - half)][0:TL, :]
    PSCs = PSC if F is None else F.mid[("hsc", l, 1 - half)][:, :, :, :, 0:64]
    P.add(SP, lambda e: e.dma_start(out=pv[:], in_=PVs.rearrange("(c p) n -> p c n", p=64)), dma=True)
    P.add(SP, lambda e: e.dma_start(out=hsc[:], in_=HSC), dma=True)
    P.add(SP, lambda e: e.dma_start(out=psc[:], in_=PSCs), dma=True)
    P.add(POOL, lambda e: e.memset(oacc[:], 0.0))
    bc = lambda ap: ap.rearrange("p (h o) -> p h o", o=1).to_broadcast([128, 4, 128])
    nstep = 0
    for d in range(2):
        for h in range(4):
            P.add(SP, lambda e, h=h: e.dma_start(out=hqe[:, h, :], in_=HQE[:, d, h, :]), dma=True)
            P.add(SP, lambda e, h=h: e.dma_start(out=hkt[:, h, :], in_=HKT[:, d, h, :]), dma=True)
        P.add(SP, lambda e: e.dma_start(out=hke[:], in_=HKE[:, d, :, :].rearrange("(c p) h k -> p c h k", p=64)), dma=True)
        PKEs = PKE[:, d, :, :] if F is None else F.mid[("hke", l, 1 - half)][0:TL, d, :, :]
        if d in pdirs:
            P.add(SP, lambda e: e.dma_start(out=pke[:], in_=PKEs.rearrange("(c p) h k -> p c h k", p=64)), dma=True)
        P.add(DVE, lambda e: e.memset(S[:], 0.0))
        mcol = C_MF if d == 0 else C_MB
        mask = g.cst[0:64, mcol:mcol + 64]
        ctx_t = [16, 17, 18, 19]
        lat_t = list(range(16))
        jord = [0, 1, 2, 3]
        if d == 1:
            ctx_t, lat_t, jord = ctx_t[::-1], lat_t[::-1], jord[::-1]
        steps = [("own", c) for c in ctx_t] + ([("par", c) for c in lat_t] if d in pdirs else []) + [("own", c) for c in lat_t]
        for kind, tt in steps:
            KE, V, SCL = (hke, hv, hsc) if kind == "own" else (pke, pv, psc)
            t0 = tt * 64
            want_out = kind == "own" and not (last and tt >= 16)
            keM = keMs[nstep % 2]
            nstep += 1
            pa, po = ps[1], ps[2]
            for j in range(4):
                P.add(POOL, lambda e, j=j: e.tensor_scalar(out=keM[:, j, :, :], in0=KE[0:64, tt, :, :], scalar1=g.cst[0:64, C_RM + j:C_RM + j + 1], scalar2=None, op0=ALU.mult))
            if want_out:
                for h in range(4):
                    P.add(PE, lambda e, h=h: e.matmul(pa[0:64, h * 64:(h + 1) * 64], lhsT=hkt[:, h, t0:t0 + 64], rhs=hqe[:, h, t0:t0 + 64], start=True, stop=True))
                P.add(DVE, lambda e: e.tensor_tensor(out=sc[:], in0=pa[0:64, 0:256].rearrange("p (h t) -> p h t", h=4),
                                                     in1=mask.rearrange("p (o t) -> p o t", o=1).to_broadcast([64, 4, 64]), op=ALU.mult))
                for h in range(4):
                    P.add(PE, lambda e, h=h: e.matmul(po[:, h * 64:(h + 1) * 64], lhsT=V[0:64, tt, h * 128:(h + 1) * 128], rhs=sc[0:64, h, :], start=(h == 0), stop=False, skip_group_check=True))
            for ji, j in enumerate(jord):
                sci = tt * 4 + j
                if want_out:
                    Sb = Sbs[(nstep * 4 + ji) % 2]
                    P.add(DVE, lambda e, Sb=Sb: e.tensor_tensor(out=Sb[:], in0=S[:], in1=bc(SCL[:, d, 0, :, sci]), op=ALU.mult))
                    for h in range(4):
                        P.add(PE, lambda e, h=h, Sb=Sb: e.matmul(po[:, h * 64 + j * CH:h * 64 + (j + 1) * CH], lhsT=Sb[:, h, :], rhs=hqe[:, h, t0 + j * CH:t0 + (j + 1) * CH],
                                                                 start=False, stop=(ji == 3 and h == 3), skip_group_check=True))
                pS = ps[3 + ji % 2]
                for h in range(4):
                    P.add(PE, lambda e, h=h: e.matmul(pS[:, h * 128:(h + 1) * 128], lhsT=keM[0:64, j, h, :], rhs=V[0:64, tt, h * 128:(h + 1) * 128], start=True, stop=True))
                P.add(POOL, lambda e: e.tensor_tensor(out=S[:], in0=S[:], in1=bc(SCL[:, d, 1, :, sci]), op=ALU.mult))
                P.add(DVE, lambda e: e.tensor_tensor(out=stmp[:], in0=pS[:].rearrange("p (h v) -> p h v", h=4), in1=bc(SCL[:, d, 2, :, sci]), op=ALU.mult))
                P.add(POOL, lambda e: e.tensor_tensor(out=S[:], in0=S[:], in1=stmp[:], op=ALU.add))
            if want_out:
                P.add(DVE, lambda e: e.tensor_tensor(out=oacc[:, :, t0:t0 + 64], in0=po[:, 0:256].rearrange("p (h t) -> p h t", h=4), in1=oacc[:, :, t0:t0 + 64], op=ALU.add))
    ar.release(m_scan)
    hgt = ar.alloc("hgt", [128, T], F32)
    rstd = ar.alloc("rstd", [128, T], F32)
    rtmp = ar.alloc("rtmp", [128, 512], F32)
    sq = [ar.alloc("sq", [128, 512], BF16) for _ in range(2)]
    ytmp = ar.alloc("ytmp", [128, 512], F32)
    okeys = []
    for h in range(4):
        P.add(SP, lambda e, h=h: e.dma_start(out=hgt[:], in_=HG[:, h, :]), writes=["hgt"], dma=True)
        for ti, (t0, n) in enumerate(qtiles):
            col_rstd(g, [(oacc[:, h, t0:t0 + n], okeys)], 128, rstd[:, t0:t0 + n], t0, n, sq, None, ("rstd", ti), rtmp)
            P.add(DVE, lambda e, h=h: e.scalar_tensor_tensor(out=ytmp[:, 0:n], in0=oacc[:, h, t0:t0 + n], scalar=g.vec[:, V_HNG:V_HNG + 1], in1=rstd[:, t0:t0 + n], op0=ALU.mult, op1=ALU.mult),
                  reads=okeys + [("rstd", ti), "vec"], writes=["ytmp"])
            P.add(POOL, lambda e, h=h: e.tensor_tensor(out=mixT[:, 8 + h, t0:t0 + n], in0=ytmp[:, 0:n], in1=hgt[:, t0:t0 + n], op=ALU.mult),
                  reads=["ytmp", "hgt"], writes=[("mix", 8 + h, ti)])
    P.barrier()
    ar.release(m0)

    qh = [ar.alloc("qh", [128, T], BF16) for _ in range(2)]
    qr = [ar.alloc("qr", [64, T], BF16) for _ in range(2)]
    kT = [ar.alloc("kT", [128, NKEY], BF16) for _ in range(2)]
    kr = ar.alloc("kr", [64, NKEY], BF16)
    vv = [ar.alloc("vv", [128, NKC, 128], BF16) for _ in range(2)]
    pT = [ar.alloc("pT", [128, 512], BF16) for _ in range(2)]
    rec = ar.alloc("rec", [128, 512], F32)
    sqb = [ar.alloc("sqa", [128, 512], BF16) for _ in range(2)]
    mx = ar.alloc("mx", [128, 16], F32)
    bias = ar.alloc("bias", [128, 2], F32)
    ld_feat(kr, 64, "mkr", MKR, None)
    cnt = {"s": 0, "o": 0, "sq": 0}

    def max_sq(srcs, ntok, dst_col, tiles):
        cols = []
        for ti, (t0, n) in enumerate(tiles):
            for i, (ap, np_, rk) in enumerate(srcs):
                b = cnt["sq"] % 2
                cnt["sq"] += 1
                P.add(ACT, lambda e, ap=ap, b=b, np_=np_: e.activation(out=sqb[b][0:np_, 0:n], in_=ap[0:np_, t0:t0 + n], func=AF.Square), reads=rk, writes=[("sqa", b)])
                P.add(PE, lambda e, b=b, i=i, np_=np_: e.matmul(ps[0][:, 0:n], lhsT=g.oneb[0:np_, :], rhs=sqb[b][0:np_, 0:n], start=(i == 0), stop=(i == len(srcs) - 1)),
                      reads=[("sqa", b), "oneb"], writes=[("ps", 0)])
            P.add(DVE, lambda e, ti=ti: e.reduce_max(out=mx[:, 8 + ti:9 + ti], in_=ps[0][:, 0:n], axis=AX.X), writes=[("ps", 0), ("mxp", ti)])
            cols.append(ti)
        P.add(DVE, lambda e: e.reduce_max(out=mx[:, dst_col:dst_col + 1], in_=mx[:, 8:8 + len(cols)], axis=AX.X),
              reads=[("mxp", ti) for ti in cols], writes=[("mx", dst_col)])

    KT5 = ((0, 512), (512, 512), (1024, 512), (1536, 512), (2048, 256))
    heads = [("g", h) for h in range(8)] + [("m", h) for h in range(4)]
    for hi_, (kind, h) in enumerate(heads):
        b = hi_ % 2
        if kind == "g":
            kv = h // 4
            scale = 128 ** -0.5
            mixi = h
            P.add(SP, lambda e: e.dma_start(out=qh[b][:], in_=GQ[:, h, :]), writes=[("qh", b)], dma=True)
            if h % 4 == 0:
                kb = kv % 2
                ld_feat(kT[kb], 128, "gk", GKA[:, kv, :] if F is None else None, kv)
                ld_tok(vv[kb], "gv", GVA, kv * 128, (kv + 1) * 128)
                max_sq([(kT[kb], 128, [("kT", kb)])], NKEY, 1, KT5)
            qs = [(qh[b], 128, [("qh", b)])]
        else:
            scale = 192 ** -0.5
            mixi = 12 + h
            kb = h % 2
            P.add(SP, lambda e: e.dma_start(out=qh[b][:], in_=MQN[:, h, :]), writes=[("qh", b)], dma=True)
            P.add(SP, lambda e: e.dma_start(out=qr[b][:], in_=MQR[:, h, :]), writes=[("qr", b)], dma=True)
            ld_feat(kT[kb], 128, "mkn", MKN[:, h, :] if F is None else None, h)
            ld_tok(vv[kb], "mv", MVA, h * 128, (h + 1) * 128)
            max_sq([(kT[kb], 128, [("kT", kb)]), (kr, 64, ["kr"])], NKEY, 1, KT5)
            qs = [(qh[b], 128, [("qh", b)]), (qr[b], 64, [("qr", b)])]
        max_sq(qs, T, 0, TILES)
        P.add(DVE, lambda e: e.tensor_tensor(out=mx[:, 2:3], in0=mx[:, 0:1], in1=mx[:, 1:2], op=ALU.mult), reads=[("mx", 0), ("mx", 1)], writes=[("mx", 2)])
        P.add(ACT, lambda e: e.activation(out=mx[:, 3:4], in_=mx[:, 2:3], func=AF.Sqrt, scale=float(scale * scale)), reads=[("mx", 2)], writes=[("mx", 3)])
        bcol = hi_ % 2
        P.add(DVE, lambda e: e.tensor_scalar_mul(out=bias[:, bcol:bcol + 1], in0=mx[:, 3:4], scalar1=-1.0), reads=[("mx", 3)], writes=[("bias", bcol)])
        for ti, (t0, n) in enumerate(qtiles):
            chunks = list(range(NKC)) if t0 < TL else [16, 17]
            oi = cnt["o"] % 2
            cnt["o"] += 1
            po, psm = ps[3 + oi], ps[5 + oi]
            for ci, c in enumerate(chunks):
                si = 1 + cnt["s"] % 2
                pb = cnt["s"] % 2
                cnt["s"] += 1
                pss = ps[si]
                if kind == "g":
                    P.add(PE, lambda e: e.matmul(pss[:, 0:n], lhsT=kT[kb][:, c * 128:(c + 1) * 128], rhs=qh[b][:, t0:t0 + n], start=True, stop=True),
                          reads=[("kT", kb), ("qh", b)], writes=[("ps", si)])
                else:
                    P.add(PE, lambda e: e.matmul(pss[:, 0:n], lhsT=kT[kb][:, c * 128:(c + 1) * 128], rhs=qh[b][:, t0:t0 + n], start=True, stop=False),
                          reads=[("kT", kb), ("qh", b)], writes=[("ps", si)])
                    P.add(PE, lambda e: e.matmul(pss[:, 0:n], lhsT=kr[0:64, c * 128:(c + 1) * 128], rhs=qr[b][0:64, t0:t0 + n], start=False, stop=True),
                          reads=["kr", ("qr", b)], writes=[("ps", si)])
                P.add(ACT, lambda e: e.activation(out=pT[pb][:, 0:n], in_=pss[:, 0:n], func=AF.Exp, bias=bias[:, bcol:bcol + 1], scale=float(scale)),
                      reads=[("bias", bcol)], writes=[("ps", si), ("pT", pb)])
                P.add(PE, lambda e: e.matmul(po[:, 0:n], lhsT=vv[kb][:, c, :], rhs=pT[pb][:, 0:n], start=(ci == 0), stop=(ci == len(chunks) - 1)),
                      reads=[("vv", kb), ("pT", pb)], writes=[("ps", 3 + oi)])
                P.add(PE, lambda e: e.matmul(psm[:, 0:n], lhsT=g.oneb[:], rhs=pT[pb][:, 0:n], start=(ci == 0), stop=(ci == len(chunks) - 1)),
                      reads=["oneb", ("pT", pb)], writes=[("ps", 5 + oi)])
            P.add(DVE, lambda e: e.reciprocal(out=rec[:, 0:n], in_=psm[:, 0:n]), writes=[("ps", 5 + oi), "rec"])
            P.add(DVE, lambda e: e.tensor_tensor(out=mixT[:, mixi, t0:t0 + n], in0=po[:, 0:n], in1=rec[:, 0:n], op=ALU.mult),
                  reads=["rec"], writes=[("ps", 3 + oi), ("mix", mixi, ti)])
    P.barrier()
    ar.release(m0)

    if DEBUG_MIX:
        nm = TL if last else T
        fin.append(P.add(SP, lambda e: e.dma_start(out=MIXD[:, :, 0:nm], in_=mixT[:, :, 0:nm]), dma=True))
        P.barrier()
    xT = ar.alloc("xT", [128, KC, T], F32)
    for c in range(KC):
        P.add(SP, lambda e, c=c: e.dma_start(out=xT[:, c, :], in_=XT[c * 128:(c + 1) * 128, :]),
              writes=[("x", c, 0), ("x", c, 1), ("x", c, 2)], dma=True)
    fb = alloc_ffn_bufs(ar)
    nps = 0
    for ch in range(KC):
        b = ch % 2
        load_w(g, POOL, fb.wg[b][:], WOUT[:, ch * 128:(ch + 1) * 128].rearrange("(c p) n -> p c n", p=128), ("wg", b))
        for ti, (t0, n) in enumerate(qtiles):
            which = 0 if t0 < TL else 1
            q = nps % 2
            nps += 1
            pp = ps[5 + q]
            for c in range(KC):
                P.add(PE, lambda e, c=c: e.matmul(pp[:, 0:n], lhsT=fb.wg[b][:, c, :], rhs=mixT[:, c, t0:t0 + n], start=(c == 0), stop=(c == KC - 1)),
                      reads=[("wg", b)], writes=[("ps", 5 + q)])
            P.add(DVE, lambda e: e.scalar_tensor_tensor(out=xT[:, ch, t0:t0 + n], in0=pp[:, 0:n], scalar=g.mods[:, 5 * 16 + ch, which:which + 1], in1=xT[:, ch, t0:t0 + n], op0=ALU.mult, op1=ALU.add),
                  reads=["mods"], writes=[("ps", 5 + q), ("x", ch, ti)])
    P.barrier()
    hT = mixT
    norm_mod(g, xT, hT, 6, 7, fb, tiles=qtiles)
    ffn(g, xT, hT, W2I, W2O, 8, fb, tiles=qtiles)
    if last:
        for ti, (t0, n) in enumerate(qtiles):
            col_rstd(g, [(xT[:, c, t0:t0 + n], [("x", c, ti)]) for c in range(KC)], D, fb.rstd[:, t0:t0 + n], t0, n, fb.sq, None, ("rstd", ti), fb.rtmp)
            for c in range(KC):
                P.add(DVE, lambda e, c=c: e.scalar_tensor_tensor(out=xT[:, c, t0:t0 + n], in0=xT[:, c, t0:t0 + n], scalar=g.vec[:, V_FG + c:V_FG + c + 1], in1=fb.rstd[:, t0:t0 + n], op0=ALU.mult, op1=ALU.mult),
                      reads=[("rstd", ti), "vec"], writes=[("x", c, ti)])
    ncols = TL if last else T
    for c in range(KC):
        fin.append(P.add(SP, lambda e, c=c: e.dma_start(out=OUT[c * 128:(c + 1) * 128, :], in_=xT[:, c, 0:ncols]),
                         reads=[("x", c, 0), ("x", c, 1), ("x", c, 2)], dma=True))
    if F is not None:
        return fin
    P.emit(final_wait_ops=fin)
    return nc


class Fused:
    def __init__(self, nc, P, ar, g):
        self.nc, self.P, self.ar, self.g = nc, P, ar, g
        self.mid, self.xin, self.w, self.rope, self.outs = {}, {}, {}, {}, {}

    def tensor(self, n, s, d, k, l, half):
        if n == "xT":
            return self.xin[(l, half)]
        if n == "rope":
            return self.rope[half]
        if n in ("vec", "cst", "mods", "w_mod"):
            return None
        if k == "ExternalInput":
            return self.w[(n, l)]
        t = self.nc.dram_tensor(f"{n}_{l}_{half}", s, d, kind="Internal").ap()
        self.mid[(n, l, half)] = t
        return t

    def tensor_b(self, n, s, d, k, l, half):
        if n == "xT":
            return self.mid[("xo", l, half)]
        if n in ("vec", "cst", "mods", "gk", "gv", "mkn", "mkr", "mv", "pke", "pv", "psc"):
            return None
        if n in ("gq", "mqn", "mqr", "hqe", "hkt", "hke", "hv", "hsc", "hg"):
            return self.mid[(n, l, half)]
        if n == "out":
            return self.xin[(l + 1, half)] if (l + 1, half) in self.xin else self.outs[half]
        if n == "mixdbg":
            return self.nc.dram_tensor(f"mixdbg_{l}_{half}", s, d, kind="ExternalOutput").ap()
        return self.w[(n, l)]


W_A = (("w1i", [D, 2 * FH]), ("w1o", [FH, D]), ("w_in", [D, 4928]), ("w_uq", [512, 768]), ("w_ukv", [256, 1024]))
W_B = (("w_out", [D, D]), ("w2i", [D, 2 * FH]), ("w2o", [FH, D]))
DEPTH = 2


def build_fused():
    nc = bass.Bass("TRN2", target_bir_lowering=False)
    ein = lambda n, s, d=F32: nc.dram_tensor(n, s, d, kind="ExternalInput").ap()
    CST = ein("cst", [128, NCST])
    VECS = [ein(f"vec{l}", [128, NVEC]) for l in range(DEPTH)]
    WMODS = [ein(f"w_mod{l}", [D, NMOD * D]) for l in range(DEPTH)]
    P = Prog(nc)
    ar = Arena(nc)
    g = setup_common(nc, P, ar, CST, VECS[0])
    F = Fused(nc, P, ar, g)
    for h in range(2):
        F.xin[(0, h)] = ein(f"xT{h}", [D, T])
        F.rope[h] = ein(f"rope{h}", [128, 4, TL])
        F.outs[h] = nc.dram_tensor(f"out{h}", [D, TL], F32, kind="ExternalOutput").ap()
        for l in range(1, DEPTH):
            F.xin[(l, h)] = nc.dram_tensor(f"x{l}_{h}", [D, T], F32, kind="Internal").ap()
    for l in range(DEPTH):
        for n, shp in W_A + W_B:
            F.w[(n, l)] = ein(f"{n}{l}", shp)
    base = ar.mark()
    fin = []
    for l in range(DEPTH):
        last = l == DEPTH - 1
        if l > 0:
            P.add(SP, lambda e: e.dma_start(out=g.vec[:], in_=VECS[l]), dma=True)
        ar.release(base)
        compute_mods(g, ar, WMODS[l])
        for half in range(2):
            ar.release(base)
            build_A(l, last, F, half)
        for half in range(2):
            ar.release(base)
            f = build_B(l, last, F, half)
            if last:
                fin += f
    P.emit(final_wait_ops=fin)
    return nc


NCORES = 4
_CACHE = {}


def fused_inputs(inp, b, cst, ropes):
    m = {"cst": cst}
    for h in range(2):
        xl = inp["x"][b, h * TL:(h + 1) * TL]
        m[f"xT{h}"] = np.ascontiguousarray(np.concatenate([xl, inp["ctx"][b]], axis=0).T.astype(np.float32))
        m[f"rope{h}"] = ropes[h]
    for l in range(DEPTH):
        m[f"vec{l}"] = host_vec(inp, l, b)
        m[f"w_mod{l}"] = inp["w_mod"][l]
        m[f"w1i{l}"] = inp["w_ffn1_in"][l]
        m[f"w1o{l}"] = inp["w_ffn1_out"][l]
        m[f"w_in{l}"] = inp["w_in"][l]
        m[f"w_uq{l}"] = inp["w_uq"][l]
        m[f"w_ukv{l}"] = inp["w_ukv"][l]
        m[f"w_out{l}"] = inp["w_out"][l]
        m[f"w2i{l}"] = inp["w_ffn2_in"][l]
        m[f"w2o{l}"] = inp["w_ffn2_out"][l]
    return m


def kernel(**inputs):
    inp = {k: np.asarray(v) for k, v in inputs.items()}
    cst = host_consts()
    ropes = [host_rope(0), host_rope(1)]
    if "nc" not in _CACHE:
        _CACHE["nc"] = build_fused()
    cores = list(range(NCORES))
    res = run_bass_kernel_spmd(_CACHE["nc"], [fused_inputs(inp, b, cst, ropes) for b in cores], core_ids=cores)
    out = np.zeros((4, SEQ, D), np.float32)
    for b in cores:
        for h in range(2):
            out[b, h * TL:(h + 1) * TL, :] = res.results[b][f"out{h}"].T
    return out


def _a_inputs(inp, l, core, xT, cst, ropes):
    b, half = core // 2, core % 2
    return {"xT": xT, "vec": host_vec(inp, l, b), "cst": cst, "rope": ropes[half],
            "w_mod": inp["w_mod"][l], "w1i": inp["w_ffn1_in"][l], "w1o": inp["w_ffn1_out"][l], "w_in": inp["w_in"][l],
            "w_uq": inp["w_uq"][l], "w_ukv": inp["w_ukv"][l]}


def _b_inputs(inp, l, core, ra, cst):
    b, half = core // 2, core % 2
    r = ra[core]
    r0, r1 = ra[2 * b], ra[2 * b + 1]
    rp = ra[core ^ 1]
    catk = lambda k: np.ascontiguousarray(np.concatenate([r0[k][..., :TL], r1[k][..., :TL], r0[k][..., TL:]], axis=-1))
    catv = lambda k: np.ascontiguousarray(np.concatenate([r0[k][:TL], r1[k][:TL], r0[k][TL:]], axis=0))
    pdir = 1 if half == 0 else 0
    pke = np.zeros((TL, 2, 4, 128), ml_dtypes.bfloat16)
    pke[:, pdir] = rp["hke"][:TL, pdir]
    psc = np.ones((128, 2, 3, 4, 64), np.float32)
    psc[:, pdir] = rp["hsc"][:, pdir, :, :, :64]
    psc[:, 1 - pdir, 2] = 0.0
    return {"xT": r["xo"], "vec": host_vec(inp, l, b), "cst": cst, "mods": r["mods"],
            "gq": r["gq"], "gk": catk("gk"), "gv": catv("gv"),
            "mqn": r["mqn"], "mqr": r["mqr"], "mkn": catk("mkn"), "mkr": catk("mkr"), "mv": catv("mv"),
            "hqe": r["hqe"], "hkt": r["hkt"], "hke": r["hke"], "hv": r["hv"], "hsc": r["hsc"], "hg": r["hg"],
            "pke": pke, "pv": np.ascontiguousarray(rp["hv"][:TL]), "psc": psc,
            "w_out": inp["w_out"][l], "w2i": inp["w_ffn2_in"][l], "w2o": inp["w_ffn2_out"][l]}
```

```python
import numpy as np
import ml_dtypes
import concourse.bass as bass
import concourse.mybir as mybir
from concourse.bass_utils import run_bass_kernel_spmd

F32 = mybir.dt.float32
BF16 = mybir.dt.bfloat16
AF = mybir.ActivationFunctionType
ALU = mybir.AluOpType
AX = mybir.AxisListType

PE, ACT, DVE, POOL, SP = "pe", "act", "dve", "pool", "sp"
ENGS = (PE, ACT, DVE, POOL, SP)
N_DMA_SEMS = 8
ANNOTATE = False
EMBED_WAIT = False

D = 2048
KC = 16
TL = 1024
TCX = 256
T = TL + TCX
SEQ = 2048
FH = 5504
FC = 43
NMOD = 9
EPS = 1e-6
CH = 16
NCH = T // CH
TILES = ((0, 512), (512, 512), (1024, 256))


class Op:
    __slots__ = ("eng", "fn", "idx", "deps", "is_dma", "signal", "seq", "dsem", "dval", "nd", "tag")

    def __init__(self, eng, fn, is_dma):
        self.eng = eng
        self.fn = fn
        self.is_dma = is_dma
        self.signal = False
        self.seq = None
        self.deps = []
        self.dsem = None
        self.dval = None
        self.nd = None


class _Rec:
    def __getattr__(self, name):
        def f(*a, **k):
            self.call = (name, a, k)
            return self
        return f


_ESZ = {}


def _esize(dtype):
    k = str(dtype)
    if k not in _ESZ:
        _ESZ[k] = 4 if "32" in k else (2 if "16" in k else (1 if "8" in k else 4))
    return _ESZ[k]


def _footprint(ap, psum_bank):
    sp = str(ap.space)
    t = ap.tensor
    if "PSUM" in sp:
        return ("P", psum_bank[t.name], 0, 1 << 30, 0, 128)
    if "DRAM" in sp:
        return ("D", t.name, 0, 1 << 30, 0, 128)
    apl = ap.ap
    pstep, pcount = apl[0]
    off = ap.offset
    if pstep:
        p0 = off // pstep
        foff = off - p0 * pstep
    else:
        p0, foff = 0, off
    ext = 0
    for st, c in apl[1:]:
        ext += (c - 1) * abs(st)
    esz = _esize(ap.dtype)
    base = t.manual_sbuf_range[0]
    return ("S", None, base + foff * esz, base + (foff + ext + 1) * esz, p0, p0 + pcount)


def _ov(a, b):
    return a[0] == b[0] and a[1] == b[1] and a[2] < b[3] and b[2] < a[3] and a[4] < b[5] and b[4] < a[5]


def _covers(a, b):
    return a[0] == b[0] and a[1] == b[1] and a[2] <= b[2] and a[3] >= b[3] and a[4] <= b[4] and a[5] >= b[5]


def _is_ap(x):
    return hasattr(x, "tensor") and hasattr(x, "ap") and hasattr(x, "offset")


class Prog:
    def __init__(self, nc):
        self.nc = nc
        self.ops = {e: [] for e in ENGS}
        self.last_writer = {}
        self.readers = {}
        self.barrier_ops = []
        self.psum_bank = {}
        self.wlog = []
        self.rlog = []

    def _addr_deps(self, op, call, deps):
        name, a, k = call
        outs, ins = [], []
        if "out" in k:
            outs.append(k["out"])
        elif a and _is_ap(a[0]):
            outs.append(a[0])
        for i, x in enumerate(a):
            if _is_ap(x) and not (i == 0 and "out" not in k):
                ins.append(x)
        for kk_, x in k.items():
            if kk_ != "out" and _is_ap(x):
                ins.append(x)
        wf = [_footprint(x, self.psum_bank) for x in outs]
        rf = []
        for x in ins:
            f = _footprint(x, self.psum_bank)
            (wf if f[0] == "P" else rf).append(f)
        for f in rf:
            for f2, o2 in self.wlog:
                if _ov(f, f2):
                    deps[id(o2)] = o2
        for f in wf:
            for f2, o2 in self.wlog:
                if _ov(f, f2):
                    deps[id(o2)] = o2
            for f2, o2 in self.rlog:
                if _ov(f, f2):
                    deps[id(o2)] = o2
        for f in wf:
            self.wlog = [(f2, o2) for f2, o2 in self.wlog if not _covers(f, f2)]
            self.rlog = [(f2, o2) for f2, o2 in self.rlog if not _covers(f, f2)]
            self.wlog.append((f, op))
        for f in rf:
            if f[0] == "D":
                continue
            done = False
            if not op.is_dma:
                for i, (f2, o2) in enumerate(self.rlog):
                    if f2 == f and o2.eng == op.eng and not o2.is_dma:
                        self.rlog[i] = (f, op)
                        done = True
                        break
            if not done:
                self.rlog.append((f, op))

    def add(self, eng, fn, reads=(), writes=(), dma=False):
        rec = _Rec()
        fn(rec)
        op = Op(eng, rec.call, dma)
        import sys as _sys
        f = _sys._getframe(1)
        op.tag = f"L{f.f_lineno}"
        if f.f_back is not None and f.f_code.co_name in ("<lambda>", "A", "load_w", "dma_out", "sink"):
            op.tag += f"<L{f.f_back.f_lineno}"
            if f.f_back.f_back is not None:
                op.tag += f"<L{f.f_back.f_back.f_lineno}"
        op.idx = len(self.ops[eng])
        deps = {}
        for r in reads:
            w = self.last_writer.get(r)
            if w is not None:
                deps[id(w)] = w
        for r in writes:
            w = self.last_writer.get(r)
            if w is not None:
                deps[id(w)] = w
            for rd in self.readers.get(r, ()):
                deps[id(rd)] = rd
        for b in self.barrier_ops:
            deps[id(b)] = b
        self._addr_deps(op, rec.call, deps)
        newest = {}
        for d in deps.values():
            if d.is_dma:
                op.deps.append(d)
                continue
            if d.eng == eng and not dma and (eng == PE or eng == SP):
                continue
            cur = newest.get(d.eng)
            if cur is None or d.idx > cur.idx:
                newest[d.eng] = d
        for d in newest.values():
            op.deps.append(d)
            d.signal = True
        for r in writes:
            self.last_writer[r] = op
            self.readers[r] = []
        for r in reads:
            if r in writes:
                continue
            self.readers.setdefault(r, []).append(op)
        if dma:
            op.signal = True
        self.ops[eng].append(op)
        return op

    def barrier(self):
        bl = []
        for e in ENGS:
            ops = self.ops[e]
            last_c = None
            for o in reversed(ops):
                if not o.is_dma:
                    last_c = o
                    break
            if last_c is not None:
                bl.append(last_c)
            n = 0
            for o in reversed(ops):
                if o.is_dma:
                    bl.append(o)
                    n += 1
                    if n >= N_DMA_SEMS:
                        break
        self.barrier_ops = bl
        self.last_writer = {}
        self.readers = {}
        self.wlog = []
        self.rlog = []

    def emit(self, final_wait_ops=()):
        nc = self.nc
        sems = {e: nc.alloc_semaphore(name=f"s_{e}") for e in ENGS}
        dsems = {e: [nc.alloc_semaphore(name=f"d_{e}{i}") for i in range(N_DMA_SEMS)]
                 for e in (SP, POOL, ACT)}
        for e in ENGS:
            seq = 0
            nd = 0
            for op in self.ops[e]:
                if op.is_dma:
                    op.dsem = dsems[e][nd % N_DMA_SEMS]
                    op.dval = 16 * (nd // N_DMA_SEMS + 1)
                    op.nd = nd
                    nd += 1
                elif op.signal:
                    seq += 1
                    op.seq = seq
        engobj = {PE: "tensor", ACT: "scalar", DVE: "vector", POOL: "gpsimd", SP: "sync"}
        with nc.Block() as block:
            for e in ENGS:
                ops = self.ops[e]

                def body(eng, ops=ops, e=e):
                    waited = {}
                    for op in ops:
                        need = {}
                        for d in op.deps:
                            if d.is_dma:
                                key = ("d", d.eng, d.nd % N_DMA_SEMS)
                                val = d.dval
                                sem = d.dsem
                            else:
                                key = ("e", d.eng)
                                val = d.seq
                                sem = sems[d.eng]
                            if need.get(key, (None, 0))[1] < val:
                                need[key] = (sem, val)
                        if op.is_dma and op.nd >= N_DMA_SEMS:
                            key = ("d", e, op.nd % N_DMA_SEMS)
                            val = op.dval - 16
                            if need.get(key, (None, 0))[1] < val:
                                need[key] = (op.dsem, val)
                        todo = []
                        for key, (sem, val) in need.items():
                            if waited.get(key, 0) >= val:
                                continue
                            todo.append((sem, val))
                            waited[key] = val
                        emb = todo.pop() if (todo and EMBED_WAIT) else None
                        for sem, val in todo:
                            eng.wait_ge(sem, val)
                        name, a, k = op.fn
                        ins = getattr(eng, name)(*a, **k)
                        if emb is not None:
                            ins._wait_ge(emb[0], emb[1])
                        if ANNOTATE:
                            ins.annotate(op.tag)
                        if op.is_dma:
                            ins.then_inc(op.dsem, 16)
                        elif op.signal:
                            ins.then_inc(sems[e], 1)
                    if e == SP:
                        for fo in final_wait_ops:
                            eng.wait_ge(fo.dsem, fo.dval)

                getattr(block, engobj[e])(body)


class Arena:
    def __init__(self, nc, lo=16640, hi=229000):
        self.nc = nc
        self.lo = lo
        self.hi = hi
        self.top = lo
        self.n = 0

    def alloc(self, name, shape, dtype):
        per = 1
        for s in shape[1:]:
            per *= s
        nbytes = per * (4 if dtype == F32 else 2)
        nbytes = (nbytes + 63) // 64 * 64
        assert self.top + nbytes <= self.hi, f"SBUF arena overflow at {name}: {self.top}+{nbytes}"
        h = self.nc.alloc_sbuf_tensor_at(f"{name}_{self.n}", list(shape), dtype, offset=self.top)
        self.n += 1
        self.top += nbytes
        return h

    def mark(self):
        return self.top

    def release(self, m):
        self.top = m


C_ID, C_ONE, C_MF, C_MB, C_RH, C_RR, C_EPS, C_RM, C_SM = 0, 128, 256, 320, 384, 512, 576, 577, 581
NCST = 581 + T


def host_consts():
    c = np.zeros((128, NCST), np.float32)
    c[:, C_ID:C_ID + 128] = np.eye(128, dtype=np.float32)
    c[:, C_ONE:C_ONE + 128] = 1.0
    s = np.arange(64)[:, None]
    t = np.arange(64)[None, :]
    c[:64, C_MF:C_MF + 64] = (s <= t) & (s // CH == t // CH)
    c[:64, C_MB:C_MB + 64] = (s >= t) & (s // CH == t // CH)
    for j in range(4):
        c[:64, C_RM + j] = (np.arange(64) // CH == j)
    def rmat(dim):
        r = np.zeros((128, 128), np.float32)
        q = dim // 4
        for i in range(q):
            r[q + i, i] = -1.0
            r[i, q + i] = 1.0
            r[3 * q + i, 2 * q + i] = -1.0
            r[2 * q + i, 3 * q + i] = 1.0
        return r
    c[:, C_RH:C_RH + 128] = rmat(128)
    c[:, C_RR:C_RR + 64] = rmat(64)[:, :64]
    c[:, C_EPS] = EPS
    sm = np.ones(T, np.float32)
    sm[::CH] = 0.0
    c[:, C_SM:C_SM + T] = sm[None, :]
    return c


def host_rope(half):
    pos = np.arange(half * TL, (half + 1) * TL)
    row = (pos // 64).astype(np.float32)
    col = (pos % 64).astype(np.float32)
    out = np.zeros((128, 4, TL), np.float32)
    for idx, dim in ((0, 128), (2, 64)):
        ad = dim // 2
        inv = (10000.0 ** (-np.arange(0, ad, 2, dtype=np.float32) / ad)).astype(np.float32)
        ar = row[:, None] * inv
        ac = col[:, None] * inv
        ang = np.concatenate([ar, ar, ac, ac], axis=-1).astype(np.float32)
        out[:dim, idx, :] = np.cos(ang).T
        out[:dim, idx + 1, :] = np.sin(ang).T
    return out


V_C, V_BM, V_GQ, V_GK, V_MQG, V_MKG, V_LB, V_HNG, V_FG = 0, 32, 176, 177, 178, 182, 184, 200, 201
NVEC = 217


def host_vec(inp, l, b):
    v = np.zeros((128, NVEC), np.float32)
    cc = np.stack([inp["c"][b], inp["c_ctx"]], axis=-1)
    v[:, V_C:V_C + 32] = cc.reshape(16, 128, 2).transpose(1, 0, 2).reshape(128, 32)
    v[:, V_BM:V_BM + 144] = inp["b_mod"][l].reshape(144, 128).T
    v[:, V_GQ] = inp["gqa_q_gain"][l]
    v[:, V_GK] = inp["gqa_k_gain"][l]
    v[:, V_MQG:V_MQG + 4] = inp["mla_q_gain"][l].reshape(4, 128).T
    v[:, V_MKG:V_MKG + 2] = inp["mla_kv_gain"][l].reshape(2, 128).T
    lg = inp["hgrn_lb_logits"]
    v[:, V_LB:V_LB + 16] = lg.reshape(2, 2, 4, 128).transpose(3, 0, 1, 2).reshape(128, 16)
    v[:, V_HNG] = inp["hgrn_norm_gain"][l]
    v[:, V_FG:V_FG + 16] = inp["final_gain"].reshape(16, 128).T
    return v


class Ctx:
    pass


def setup_common(nc, P, ar, CST, VEC, bf16_bank=True):
    g = Ctx()
    g.nc, g.P, g.ar = nc, P, ar
    g.ps = [nc.alloc_psum_tensor(f"ps{i}", [128, 512], F32) for i in range(7 if bf16_bank else 8)]
    if bf16_bank:
        g.psb = nc.alloc_psum_tensor("psb", [128, 1024], BF16)
    g.cst = ar.alloc("cst", [128, NCST], F32)
    g.vec = ar.alloc("vec", [128, NVEC], F32)
    g.idb = ar.alloc("idb", [128, 128], BF16)
    g.oneb = ar.alloc("oneb", [128, 128], BF16)
    g.rhb = ar.alloc("rhb", [128, 128], BF16)
    g.rrb = ar.alloc("rrb", [128, 64], BF16)
    g.mods = ar.alloc("mods", [128, 144, 2], F32)
    g.onep = ar.alloc("onep", [128, 144, 2], F32)
    g.hgate = ar.alloc("hgate", [128, 144, 2], F32)
    P.add(SP, lambda e: e.dma_start(out=g.cst[:], in_=CST), writes=["cst"], dma=True)
    P.add(SP, lambda e: e.dma_start(out=g.vec[:], in_=VEC), writes=["vec"], dma=True)
    P.add(DVE, lambda e: e.tensor_copy(out=g.idb[:], in_=g.cst[:, C_ID:C_ID + 128]), reads=["cst"], writes=["idb"])
    P.add(DVE, lambda e: e.tensor_copy(out=g.oneb[:], in_=g.cst[:, C_ONE:C_ONE + 128]), reads=["cst"], writes=["oneb"])
    P.add(DVE, lambda e: e.tensor_copy(out=g.rhb[:], in_=g.cst[:, C_RH:C_RH + 128]), reads=["cst"], writes=["rhb"])
    P.add(DVE, lambda e: e.tensor_copy(out=g.rrb[:], in_=g.cst[:, C_RR:C_RR + 64]), reads=["cst"], writes=["rrb"])
    g.eps = g.cst[:, C_EPS:C_EPS + 1]
    for i, t in enumerate(g.ps):
        P.psum_bank[t.name] = i
    if bf16_bank:
        P.psum_bank[g.psb.name] = 7
    return g


def derive_mods(g):
    P = g.P
    P.add(DVE, lambda e: e.tensor_scalar_add(out=g.onep[:], in0=g.mods[:], scalar1=1.0), reads=["mods"], writes=["onep"])
    P.add(DVE, lambda e: e.tensor_scalar_mul(out=g.hgate[:], in0=g.mods[:], scalar1=0.5), reads=["mods"], writes=["hgate"])


def col_rstd(g, srcs, n_feat, rstd_ap, t0, n, sq_bufs, keys_r, key_w, tmp):
    P = g.P
    psn = g.ps[0]
    nk = len(srcs)
    for i, (ap, rk) in enumerate(srcs):
        sb = sq_bufs[i % 2]
        P.add(ACT, lambda e, ap=ap, sb=sb: e.activation(out=sb[:, 0:n], in_=ap, func=AF.Square),
              reads=rk, writes=[("sq", i % 2)])
        P.add(PE, lambda e, sb=sb, i=i: e.matmul(psn[:, 0:n], lhsT=g.oneb[:], rhs=sb[:, 0:n], start=(i == 0), stop=(i == nk - 1)),
              reads=[("sq", i % 2), "oneb"], writes=[("ps", 0)])
    P.add(ACT, lambda e: e.activation(out=tmp[:, 0:n], in_=psn[:, 0:n], func=AF.Sqrt, bias=g.eps, scale=1.0 / n_feat),
          reads=["cst"], writes=[("ps", 0), "rstd_tmp"])
    P.add(DVE, lambda e: e.reciprocal(out=rstd_ap, in_=tmp[:, 0:n]), reads=["rstd_tmp"], writes=[key_w])


def norm_mod(g, xT, hT, i_shift, i_scale, bufs, tiles=TILES):
    P = g.P
    for ti, (t0, n) in enumerate(tiles):
        which = 0 if t0 < TL else 1
        srcs = [(xT[:, c, t0:t0 + n], [("x", c, ti)]) for c in range(KC)]
        col_rstd(g, srcs, D, bufs.rstd[:, t0:t0 + n], t0, n, bufs.sq, None, ("rstd", ti), bufs.rtmp)
        for c in range(KC):
            tb = bufs.tmp[c % 2]
            P.add(DVE, lambda e, c=c, tb=tb: e.tensor_tensor(out=tb[:, 0:n], in0=xT[:, c, t0:t0 + n], in1=bufs.rstd[:, t0:t0 + n], op=ALU.mult),
                  reads=[("x", c, ti), ("rstd", ti)], writes=[("nt", c % 2)])
            P.add(ACT, lambda e, c=c, tb=tb: e.activation(out=hT[:, c, t0:t0 + n], in_=tb[:, 0:n], func=AF.Identity,
                                                         bias=g.mods[:, i_shift * 16 + c, which:which + 1],
                                                         scale=g.onep[:, i_scale * 16 + c, which:which + 1]),
                  reads=[("nt", c % 2), "mods", "onep"], writes=[("h", c, ti)])


def load_w(g, eng, dst, src, key):
    return g.P.add(eng, lambda e: e.dma_start(out=dst, in_=src), writes=[key], dma=True)


def ffn(g, xT, hT, w_in, w_out, i_gate, bufs, tiles=TILES, GS=8):
    P = g.P
    ps = g.ps
    cnt = 0
    cnt2 = 0
    nld = 0
    for g0 in range(0, FC, GS):
        gs = min(GS, FC - g0)
        for jl in range(gs):
            j = g0 + jl
            b = nld % 2
            nld += 1
            load_w(g, POOL, bufs.wg[b][:], w_in[:, j * 128:(j + 1) * 128].rearrange("(c p) n -> p c n", p=128), ("wg", b))
            load_w(g, POOL, bufs.wu[b][:], w_in[:, FH + j * 128:FH + (j + 1) * 128].rearrange("(c p) n -> p c n", p=128), ("wu", b))
            for ti, (t0, n) in enumerate(tiles):
                q = cnt % 2
                cnt += 1
                pg, pu = ps[1 + q], ps[3 + q]
                hk = [("h", c, ti) for c in range(KC)]
                for c in range(KC):
                    P.add(PE, lambda e, c=c, pg=pg, b=b: e.matmul(pg[:, 0:n], lhsT=bufs.wg[b][:, c, :], rhs=hT[:, c, t0:t0 + n], start=(c == 0), stop=(c == KC - 1)),
                          reads=[("wg", b)] + (hk if c == 0 else []), writes=[("ps", 1 + q)])
                for c in range(KC):
                    P.add(PE, lambda e, c=c, pu=pu, b=b: e.matmul(pu[:, 0:n], lhsT=bufs.wu[b][:, c, :], rhs=hT[:, c, t0:t0 + n], start=(c == 0), stop=(c == KC - 1)),
                          reads=[("wu", b)], writes=[("ps", 3 + q)])
                sg = bufs.sgt[q]
                P.add(ACT, lambda e, pg=pg, sg=sg: e.activation(out=sg[:, 0:n], in_=pg[:, 0:n], func=AF.Silu),
                      writes=[("ps", 1 + q), ("sgt", q)])
                P.add(DVE, lambda e, pu=pu, sg=sg, jl=jl: e.tensor_tensor(out=bufs.aT[:, jl, t0:t0 + n], in0=pu[:, 0:n], in1=sg[:, 0:n], op=ALU.mult),
                      reads=[("sgt", q)], writes=[("ps", 3 + q), ("a", jl, ti)])
        for db in range(4):
            b = db % 2
            load_w(g, POOL, bufs.wo[b][:, 0:gs, :], w_out[g0 * 128:(g0 + gs) * 128, db * 512:(db + 1) * 512].rearrange("(c p) n -> p c n", p=128), ("wo", b))
            for dc in range(4):
                ch = db * 4 + dc
                for ti, (t0, n) in enumerate(tiles):
                    which = 0 if t0 < TL else 1
                    q = cnt2 % 2
                    cnt2 += 1
                    po = ps[5 + q]
                    for jl in range(gs):
                        P.add(PE, lambda e, jl=jl, po=po, b=b, dc=dc: e.matmul(po[:, 0:n], lhsT=bufs.wo[b][:, jl, dc * 128:(dc + 1) * 128], rhs=bufs.aT[:, jl, t0:t0 + n], start=(jl == 0), stop=(jl == gs - 1)),
                              reads=[("wo", b), ("a", jl, ti)], writes=[("ps", 5 + q)])
                    P.add(DVE, lambda e, po=po, ch=ch, which=which: e.scalar_tensor_tensor(out=xT[:, ch, t0:t0 + n], in0=po[:, 0:n], scalar=g.hgate[:, i_gate * 16 + ch, which:which + 1], in1=xT[:, ch, t0:t0 + n], op0=ALU.mult, op1=ALU.add),
                          reads=["hgate"], writes=[("ps", 5 + q), ("x", ch, ti)])


def alloc_ffn_bufs(ar):
    b = Ctx()
    b.rstd = ar.alloc("rstd", [128, T], F32)
    b.rtmp = ar.alloc("rtmp", [128, 512], F32)
    b.sq = [ar.alloc("sq", [128, 512], BF16) for _ in range(2)]
    b.tmp = [ar.alloc("ntmp", [128, 512], F32) for _ in range(2)]
    b.aT = ar.alloc("aT", [128, 8, T], BF16)
    b.wg = [ar.alloc("wg", [128, KC, 128], BF16) for _ in range(2)]
    b.wu = [ar.alloc("wu", [128, KC, 128], BF16) for _ in range(2)]
    b.wo = [ar.alloc("wo", [128, 8, 512], BF16) for _ in range(2)]
    b.sgt = [ar.alloc("sgt", [128, 512], F32) for _ in range(2)]
    return b


def compute_mods(g, ar, w_mod):
    P = g.P
    m = ar.mark()
    scb = ar.alloc("scb", [128, KC, 2], BF16)
    wm = [ar.alloc("wm", [128, KC, 512], BF16) for _ in range(2)]
    P.add(ACT, lambda e: e.activation(out=scb[:].rearrange("p c w -> p (c w)"), in_=g.vec[:, V_C:V_C + 32], func=AF.Silu),
          reads=["vec"], writes=["scb"])
    psm = g.ps[0]
    for blk in range(36):
        b = blk % 2
        load_w(g, POOL, wm[b][:], w_mod[:, blk * 512:(blk + 1) * 512].rearrange("(c p) n -> p c n", p=128), ("wm", b))
        for jj in range(4):
            j = blk * 4 + jj
            for k in range(KC):
                P.add(PE, lambda e, b=b, jj=jj, j=j, k=k: e.matmul(psm[:, 2 * j:2 * j + 2], lhsT=wm[b][:, k, jj * 128:(jj + 1) * 128], rhs=scb[:, k, :], start=(k == 0), stop=(k == KC - 1)),
                      reads=[("wm", b), "scb"], writes=[("ps", 0)])
    P.add(DVE, lambda e: e.tensor_tensor(out=g.mods[:], in0=psm[:, 0:288].rearrange("p (j w) -> p j w", w=2),
                                         in1=g.vec[:, V_BM:V_BM + 144].rearrange("p (j o) -> p j o", o=1).to_broadcast([128, 144, 2]), op=ALU.add),
          reads=["vec"], writes=[("ps", 0), "mods"])
    derive_mods(g)
    P.barrier()
    ar.release(m)


def rope_apply(g, src_bf, np_, rmat, cos, sin, dst, ps_i, t0, n, tmp1, tmp2, rk, wk):
    P = g.P
    pr = g.ps[ps_i]
    P.add(PE, lambda e: e.matmul(pr[0:np_, 0:n], lhsT=rmat, rhs=src_bf[0:np_, t0:t0 + n], start=True, stop=True),
          reads=rk + ["rhb", "rrb"], writes=[("ps", ps_i)])
    P.add(DVE, lambda e: e.tensor_tensor(out=tmp1[0:np_, 0:n], in0=pr[0:np_, 0:n], in1=sin[0:np_, t0:t0 + n], op=ALU.mult),
          reads=["rope"], writes=[("ps", ps_i), "rt1"])
    P.add(POOL, lambda e: e.tensor_tensor(out=tmp2[0:np_, 0:n], in0=src_bf[0:np_, t0:t0 + n], in1=cos[0:np_, t0:t0 + n], op=ALU.mult),
          reads=rk + ["rope"], writes=["rt2"])
    P.add(DVE, lambda e: e.tensor_tensor(out=dst[0:np_, t0:t0 + n], in0=tmp1[0:np_, 0:n], in1=tmp2[0:np_, 0:n], op=ALU.add),
          reads=["rt1", "rt2"], writes=wk)


STOP_AFTER = None


def build_A(l, last, F=None, half=0):
    if F is None:
        nc = bass.Bass("TRN2", target_bir_lowering=False)
        dt = lambda n, s, d=F32, k="ExternalInput": nc.dram_tensor(n, s, d, kind=k).ap()
    else:
        nc = F.nc
        dt = lambda n, s, d=F32, k="ExternalInput": F.tensor(n, s, d, k, l, half)
    XT = dt("xT", [D, T]); VEC = dt("vec", [128, NVEC]); CST = dt("cst", [128, NCST]); ROPE = dt("rope", [128, 4, TL])
    WMOD = dt("w_mod", [D, NMOD * D]); W1I = dt("w1i", [D, 2 * FH]); W1O = dt("w1o", [FH, D]); WIN = dt("w_in", [D, 4928])
    WUQ = dt("w_uq", [512, 768]); WUKV = dt("w_ukv", [256, 1024])
    o = "ExternalOutput"
    XO = dt("xo", [D, T], F32, o); MODS = dt("mods", [128, 288], F32, o)
    GQ = dt("gq", [128, 8, T], BF16, o); GK = dt("gk", [128, 2, T], BF16, o); GV = dt("gv", [T, 256], BF16, o)
    MQN = dt("mqn", [128, 4, T], BF16, o); MQR = dt("mqr", [64, 4, T], BF16, o); MKN = dt("mkn", [128, 4, T], BF16, o)
    MKR = dt("mkr", [64, T], BF16, o); MV = dt("mv", [T, 512], BF16, o)
    HQE = dt("hqe", [128, 2, 4, T], BF16, o); HKT = dt("hkt", [128, 2, 4, T], BF16, o); HKE = dt("hke", [T, 2, 4, 128], BF16, o)
    HV = dt("hv", [T, 512], BF16, o); HSC = dt("hsc", [128, 2, 3, 4, NCH], F32, o); HG = dt("hg", [128, 4, T], F32, o)

    if F is None:
        P = Prog(nc)
        ar = Arena(nc)
        g = setup_common(nc, P, ar, CST, VEC)
    else:
        P, ar, g = F.P, F.ar, F.g
    fin = []
    hT = ar.alloc("hT", [128, KC, T], BF16)
    m0 = ar.mark()
    xT = ar.alloc("xT", [128, KC, T], F32)
    for c in range(KC):
        P.add(SP, lambda e, c=c: e.dma_start(out=xT[:, c, :], in_=XT[c * 128:(c + 1) * 128, :]),
              writes=[("x", c, 0), ("x", c, 1), ("x", c, 2)], dma=True)
    if F is None:
        compute_mods(g, ar, WMOD)
        fin.append(P.add(SP, lambda e: e.dma_start(out=MODS, in_=g.mods[:].rearrange("p j w -> p (j w)")), reads=["mods"], dma=True))
    if STOP_AFTER == "mods":
        P.emit(final_wait_ops=fin)
        return nc
    fb = alloc_ffn_bufs(ar)
    norm_mod(g, xT, hT, 0, 1, fb)
    ffn(g, xT, hT, W1I, W1O, 2, fb)
    for c in range(KC):
        fin.append(P.add(SP, lambda e, c=c: e.dma_start(out=XO[c * 128:(c + 1) * 128, :], in_=xT[:, c, :]),
                         reads=[("x", c, 0), ("x", c, 1), ("x", c, 2)], dma=True))
    norm_mod(g, xT, hT, 3, 4, fb)
    P.barrier()
    ar.release(m0)

    rope = ar.alloc("rope", [128, 4, TL], F32)
    P.add(SP, lambda e: e.dma_start(out=rope[:], in_=ROPE), writes=["rope"], dma=True)
    cosH, sinH, cosR, sinR = rope[:, 0, :], rope[:, 1, :], rope[:, 2, :], rope[:, 3, :]
    wc = [ar.alloc("wc", [128, KC, 128], BF16) for _ in range(3)]
    rstd = ar.alloc("rstd2", [128, T], F32)
    rtmp = ar.alloc("rtmp2", [128, 512], F32)
    sq = [ar.alloc("sq2", [128, 512], BF16) for _ in range(2)]
    NS = 10
    stg = [ar.alloc("stg", [128, T], F32) for _ in range(NS)]
    sb16 = [ar.alloc("sb16", [128, T], BF16) for _ in range(6)]
    rt1 = ar.alloc("rt1", [128, 512], F32)
    rt2 = ar.alloc("rt2", [128, 512], F32)
    state = {"nw": 0, "nps": 0, "ndma": 0}

    def proj_fm(col0, ncols, sink):
        b = state["nw"] % 3
        state["nw"] += 1
        load_w(g, POOL, wc[b][:, :, 0:ncols], WIN[:, col0:col0 + ncols].rearrange("(c p) n -> p c n", p=128), ("wc", b))
        for ti, (t0, n) in enumerate(TILES):
            pi = 1 + state["nps"] % 4
            state["nps"] += 1
            pp = g.ps[pi]
            for c in range(KC):
                P.add(PE, lambda e, c=c, pp=pp, b=b: e.matmul(pp[0:ncols, 0:n], lhsT=wc[b][:, c, 0:ncols], rhs=hT[:, c, t0:t0 + n], start=(c == 0), stop=(c == KC - 1)),
                      reads=[("wc", b)] + ([("h", cc, ti) for cc in range(KC)] if c == 0 else []), writes=[("ps", pi)])
            sink(ti, t0, n, pp, pi)

    def evac_to(dst, key, np_=128, eng=ACT, func=None):
        def sink(ti, t0, n, pp, pi):
            if func is not None:
                P.add(ACT, lambda e: e.activation(out=dst[0:np_, t0:t0 + n], in_=pp[0:np_, 0:n], func=func), writes=[("ps", pi), (key, ti)])
            elif eng == ACT:
                P.add(ACT, lambda e: e.copy(out=dst[0:np_, t0:t0 + n], in_=pp[0:np_, 0:n]), writes=[("ps", pi), (key, ti)])
            else:
                P.add(DVE, lambda e: e.tensor_copy(out=dst[0:np_, t0:t0 + n], in_=pp[0:np_, 0:n]), writes=[("ps", pi), (key, ti)])
        return sink

    def k3(key):
        return [(key, 0), (key, 1), (key, 2)]

    def dma_out(dst, src, rk):
        q = (SP, ACT)[state["ndma"] % 1]
        state["ndma"] += 1
        fin.append(P.add(q, lambda e: e.dma_start(out=dst, in_=src), reads=rk, dma=True))

    for hh in range(10):
        s_raw = stg[hh % 2]
        kraw = f"raw{hh % 2}"
        proj_fm(hh * 128, 128, evac_to(s_raw, kraw))
        gcol = V_GQ if hh < 8 else V_GK
        qn = sb16[hh % 2]
        kqn = f"qn{hh % 2}"
        for ti, (t0, n) in enumerate(TILES):
            col_rstd(g, [(s_raw[:, t0:t0 + n], [(kraw, ti)])], 128, rstd[:, t0:t0 + n], t0, n, sq, None, ("rstd", ti), rtmp)
            P.add(DVE, lambda e, t0=t0, n=n: e.scalar_tensor_tensor(out=qn[:, t0:t0 + n], in0=s_raw[:, t0:t0 + n], scalar=g.vec[:, gcol:gcol + 1], in1=rstd[:, t0:t0 + n], op0=ALU.mult, op1=ALU.mult),
                  reads=[(kraw, ti), ("rstd", ti), "vec"], writes=[(kqn, ti)])
        ob = sb16[2 + hh % 2]
        kob = f"ob{hh % 2}"
        for ti, (t0, n) in enumerate(TILES[:2]):
            rope_apply(g, qn, 128, g.rhb[:], cosH, sinH, ob, 5 + ti % 2, t0, n, rt1, rt2, [(kqn, ti)], [(kob, ti)])
        P.add(POOL, lambda e: e.tensor_copy(out=ob[:, TL:T], in_=qn[:, TL:T]), reads=[(kqn, 2)], writes=[(kob, 2)])
        dma_out(GQ[:, hh, :] if hh < 8 else GK[:, hh - 8, :], ob[:], k3(kob))

    lbv = ar.alloc("lbv", [128, 2, 4], F32)
    oml = ar.alloc("oml", [128, 2, 4], F32)
    lg = g.vec[:, V_LB:V_LB + 16].rearrange("p (d l h) -> p d l h", d=2, l=2)
    if l == 0:
        P.add(DVE, lambda e: e.memset(lbv[:], 0.0), writes=["lbv"])
    else:
        P.add(DVE, lambda e: e.tensor_tensor(out=lbv[:], in0=lg[:, :, 1, :], in1=lg[:, :, 0, :], op=ALU.subtract), reads=["vec"], writes=["lbv"])
        P.add(ACT, lambda e: e.activation(out=lbv[:], in_=lbv[:], func=AF.Sigmoid), writes=["lbv"])
    P.add(DVE, lambda e: e.tensor_scalar(out=oml[:], in0=lbv[:], scalar1=-1.0, scalar2=1.0, op0=ALU.mult, op1=ALU.add), reads=["lbv"], writes=["oml"])
    hsc = ar.alloc("hsc", [128, 2, 3, 4, NCH], F32)
    smask = g.cst[:, C_SM:C_SM + T]
    ket = [ar.alloc("ket", [128, 10, 128], BF16) for _ in range(2)]
    nket = 0
    for h in range(4):
        sq_raw = stg[2]
        proj_fm(1536 + h * 128, 128, evac_to(sq_raw, "hq"))
        for d in range(2):
            z = stg[3]
            proj_fm((2560 if d == 0 else 3072) + h * 128, 128, evac_to(z, "z", func=AF.Sigmoid))
            f, lf, kk, cum, G = stg[4], stg[5], stg[6], stg[7], stg[8]
            A = lambda eng, fn, r, w: P.add(eng, fn, reads=r, writes=w)
            A(DVE, lambda e, d=d, h=h: e.tensor_scalar(out=f[:], in0=z[:], scalar1=oml[:, d, h:h + 1], scalar2=lbv[:, d, h:h + 1], op0=ALU.mult, op1=ALU.add),
              k3("z") + ["oml", "lbv"], ["f"])
            A(ACT, lambda e: e.activation(out=lf[:], in_=f[:], func=AF.Ln), ["f"], ["lf"])
            A(POOL, lambda e: e.tensor_scalar(out=kk[:], in0=f[:], scalar1=-1.0, scalar2=1.0, op0=ALU.mult, op1=ALU.add), ["f"], ["kk"])
            A(DVE, lambda e: e.tensor_tensor_scan(out=cum[:], data0=smask, data1=lf[:], initial=0.0, op0=ALU.mult, op1=ALU.add), ["lf", "cst"], ["cum"])
            cum3 = cum[:].rearrange("p (c t) -> p c t", t=CH)
            G3 = G[:].rearrange("p (c t) -> p c t", t=CH)
            if d == 0:
                Gs, G3s, kG = cum, cum3, "cum"
                last_ap = cum3[:, :, CH - 1:CH]
            else:
                A(POOL, lambda e: e.tensor_tensor(out=G[:], in0=lf[:], in1=cum[:], op=ALU.subtract), ["lf", "cum"], ["G"])
                A(DVE, lambda e: e.tensor_tensor(out=G3, in0=G3, in1=cum3[:, :, CH - 1:CH].to_broadcast([128, NCH, CH]), op=ALU.add), ["cum"], ["G"])
                Gs, G3s, kG = G, G3, "G"
                last_ap = G3[:, :, 0:1]
            mid_ap = G3s[:, :, CH // 2:CH // 2 + 1]
            sc0 = hsc[:, d, 0, h, :].rearrange("p (c o) -> p c o", o=1)
            sc1 = hsc[:, d, 1, h, :].rearrange("p (c o) -> p c o", o=1)
            sc2 = hsc[:, d, 2, h, :].rearrange("p (c o) -> p c o", o=1)
            A(ACT, lambda e, sc0=sc0, mid_ap=mid_ap: e.activation(out=sc0, in_=mid_ap, func=AF.Exp), [kG], [("hsc", d, 0, h)])
            A(ACT, lambda e, sc1=sc1, last_ap=last_ap: e.activation(out=sc1, in_=last_ap, func=AF.Exp), [kG], [("hsc", d, 1, h)])
            A(DVE, lambda e, sc2=sc2, last_ap=last_ap, mid_ap=mid_ap: e.tensor_tensor(out=sc2, in0=last_ap, in1=mid_ap, op=ALU.subtract), [kG], [("hsc", d, 2, h)])
            A(ACT, lambda e, sc2=sc2: e.activation(out=sc2, in_=sc2, func=AF.Exp), [], [("hsc", d, 2, h)])
            Gp = stg[9]
            Gp3 = Gp[:].rearrange("p (c t) -> p c t", t=CH)
            A(DVE, lambda e, G3s=G3s, mid_ap=mid_ap: e.tensor_tensor(out=Gp3, in0=G3s, in1=mid_ap.to_broadcast([128, NCH, CH]), op=ALU.subtract), [kG], ["Gp"])
            e1, e2 = stg[4], stg[5]
            A(ACT, lambda e: e.activation(out=e1[:], in_=Gp[:], func=AF.Exp), ["Gp"], ["f"])
            A(ACT, lambda e: e.activation(out=e2[:], in_=Gp[:], func=AF.Exp, scale=-1.0), ["Gp"], ["lf"])
            qe = sb16[0]
            keT = sb16[1]
            A(DVE, lambda e: e.scalar_tensor_tensor(out=qe[:], in0=sq_raw[:], scalar=float(128 ** -0.5), in1=e1[:], op0=ALU.mult, op1=ALU.mult),
              k3("hq") + ["f"], ["qe"])
            A(POOL, lambda e: e.tensor_tensor(out=keT[:], in0=kk[:], in1=e2[:], op=ALU.mult), ["kk", "lf"], ["keT"])
            dma_out(HQE[:, d, h, :], qe[:], ["qe"])
            dma_out(HKT[:, d, h, :], keT[:], ["keT"])
            kb = ket[nket % 2]
            kkb = ("ket", nket % 2)
            nket += 1
            for tt in range(10):
                P.add(PE, lambda e, tt=tt: e.transpose(out=g.psb[:, (tt % 8) * 128:(tt % 8 + 1) * 128], in_=keT[:, tt * 128:(tt + 1) * 128], identity=g.idb[:]),
                      reads=["keT", "idb"], writes=[("ps", 7)])
                P.add(DVE, lambda e, tt=tt, kb=kb: e.tensor_copy(out=kb[:, tt, :], in_=g.psb[:, (tt % 8) * 128:(tt % 8 + 1) * 128]),
                      writes=[("ps", 7), kkb])
            dma_out(HKE[:, d, h, :].rearrange("(tt p) k -> p tt k", p=128), kb[:], [kkb])
        og = stg[2 + 0]
    for h in range(4):
        gg = stg[h % 2]
        proj_fm(3584 + h * 128, 128, evac_to(gg, f"gg{h % 2}", func=AF.Silu))
        dma_out(HG[:, h, :], gg[:], k3(f"gg{h % 2}"))
    dma_out(HSC, hsc[:], [("hsc", d, i, h) for d in range(2) for i in range(3) for h in range(4)])

    wuq = ar.alloc("wuq", [128, 4, 768], BF16)
    wukv = ar.alloc("wukv", [128, 2, 1024], BF16)
    load_w(g, POOL, wuq[:], WUQ.rearrange("(c p) n -> p c n", p=128), "wuq")
    load_w(g, POOL, wukv[:], WUKV.rearrange("(c p) n -> p c n", p=128), "wukv")
    for nm, col0, nchk, gcol0, nfeat in (("cq", 4096, 4, V_MQG, 512), ("ckv", 4608, 2, V_MKG, 256)):
        raws = [stg[c] for c in range(nchk)]
        for c in range(nchk):
            proj_fm(col0 + c * 128, 128, evac_to(raws[c], f"{nm}{c}"))
        for ti, (t0, n) in enumerate(TILES):
            col_rstd(g, [(raws[c][:, t0:t0 + n], [(f"{nm}{c}", ti)]) for c in range(nchk)], nfeat, rstd[:, t0:t0 + n], t0, n, sq, None, ("rstd", ti), rtmp)
        cn = [sb16[c] for c in range(nchk)]
        for c in range(nchk):
            for ti, (t0, n) in enumerate(TILES):
                P.add(DVE, lambda e, c=c, t0=t0, n=n: e.scalar_tensor_tensor(out=cn[c][:, t0:t0 + n], in0=raws[c][:, t0:t0 + n], scalar=g.vec[:, gcol0 + c:gcol0 + c + 1], in1=rstd[:, t0:t0 + n], op0=ALU.mult, op1=ALU.mult),
                      reads=[(f"{nm}{c}", ti), ("rstd", ti), "vec"], writes=[(f"{nm}n{c}", ti)])
        if nm == "cq":
            for h in range(4):
                on = stg[4 + h % 2]
                onb = ket[h % 2][:].rearrange("p a b -> p (a b)")
                orb = ket[(h + 1) % 2][:].rearrange("p a b -> p (a b)")
                for ti, (t0, n) in enumerate(TILES):
                    pi = 1 + state["nps"] % 4
                    state["nps"] += 1
                    pp = g.ps[pi]
                    for c in range(4):
                        P.add(PE, lambda e, c=c, pp=pp, h=h: e.matmul(pp[:, 0:n], lhsT=wuq[:, c, h * 192:h * 192 + 128], rhs=cn[c][:, t0:t0 + n], start=(c == 0), stop=(c == 3)),
                              reads=["wuq"] + [(f"cqn{cc}", ti) for cc in range(4)], writes=[("ps", pi)])
                    P.add(ACT, lambda e, pp=pp, t0=t0, n=n, onb=onb: e.copy(out=onb[:, t0:t0 + n], in_=pp[:, 0:n]), writes=[("ps", pi), ("mqn_s", h % 2, ti)])
                dma_out(MQN[:, h, :], onb, [("mqn_s", h % 2, ti) for ti in range(3)])
                qr_raw = sb16[4]
                for ti, (t0, n) in enumerate(TILES):
                    pi = 1 + state["nps"] % 4
                    state["nps"] += 1
                    pp = g.ps[pi]
                    for c in range(4):
                        P.add(PE, lambda e, c=c, pp=pp, h=h: e.matmul(pp[0:64, 0:n], lhsT=wuq[:, c, h * 192 + 128:h * 192 + 192], rhs=cn[c][:, t0:t0 + n], start=(c == 0), stop=(c == 3)),
                              reads=["wuq"] + [(f"cqn{cc}", ti) for cc in range(4)], writes=[("ps", pi)])
                    P.add(ACT, lambda e, pp=pp, t0=t0, n=n: e.copy(out=qr_raw[0:64, t0:t0 + n], in_=pp[0:64, 0:n]), writes=[("ps", pi), ("qrraw", ti)])
                qro = sb16[5]
                for ti, (t0, n) in enumerate(TILES[:2]):
                    rope_apply(g, qr_raw, 64, g.rrb[0:64, :], cosR, sinR, qro, 5 + ti % 2, t0, n, rt1, rt2, [("qrraw", ti)], [("qro", ti)])
                P.add(POOL, lambda e: e.tensor_copy(out=qro[0:64, TL:T], in_=qr_raw[0:64, TL:T]), reads=[("qrraw", 2)], writes=[("qro", 2)])
                dma_out(MQR[:, h, :], qro[0:64, :], k3("qro"))
        else:
            for h in range(4):
                onb = ket[h % 2][:].rearrange("p a b -> p (a b)")
                for ti, (t0, n) in enumerate(TILES):
                    pi = 1 + state["nps"] % 4
                    state["nps"] += 1
                    pp = g.ps[pi]
                    for c in range(2):
                        P.add(PE, lambda e, c=c, pp=pp, h=h: e.matmul(pp[:, 0:n], lhsT=wukv[:, c, h * 256:h * 256 + 128], rhs=cn[c][:, t0:t0 + n], start=(c == 0), stop=(c == 1)),
                              reads=["wukv"] + [(f"ckvn{cc}", ti) for cc in range(2)], writes=[("ps", pi)])
                    P.add(ACT, lambda e, pp=pp, t0=t0, n=n, onb=onb: e.copy(out=onb[:, t0:t0 + n], in_=pp[:, 0:n]), writes=[("ps", pi), ("mkn_s", h % 2, ti)])
                dma_out(MKN[:, h, :], onb, [("mkn_s", h % 2, ti) for ti in range(3)])
            vst = [ar.alloc("vst", [128, 512], BF16) for _ in range(2)]
            wv4 = wukv[:].rearrange("p c (h x) -> p c h x", x=256)
            for tt in range(10):
                pi = 1 + state["nps"] % 4
                state["nps"] += 1
                pp = g.ps[pi]
                for h in range(4):
                    for c in range(2):
                        P.add(PE, lambda e, c=c, pp=pp, tt=tt, h=h: e.matmul(pp[:, h * 128:(h + 1) * 128], lhsT=cn[c][:, tt * 128:(tt + 1) * 128], rhs=wv4[:, c, h, 128:256], start=(c == 0), stop=(c == 1)),
                              reads=["wukv"] + [(f"ckvn{cc}", ti) for cc in range(2) for ti in range(3)], writes=[("ps", pi)])
                vb = vst[tt % 2]
                P.add(ACT, lambda e, pp=pp, vb=vb: e.copy(out=vb[:], in_=pp[:, 0:512]), writes=[("ps", pi), ("vst", tt % 2)])
                dma_out(MV[tt * 128:(tt + 1) * 128, :], vb[:], [("vst", tt % 2)])
    kr_raw = sb16[2]
    proj_fm(4864, 64, evac_to(kr_raw, "krraw", np_=64))
    kro = sb16[3]
    for ti, (t0, n) in enumerate(TILES[:2]):
        rope_apply(g, kr_raw, 64, g.rrb[0:64, :], cosR, sinR, kro, 5 + ti % 2, t0, n, rt1, rt2, [("krraw", ti)], [("kro", ti)])
    P.add(POOL, lambda e: e.tensor_copy(out=kro[0:64, TL:T], in_=kr_raw[0:64, TL:T]), reads=[("krraw", 2)], writes=[("kro", 2)])
    dma_out(MKR, kro[0:64, :], k3("kro"))

    wt = ar.alloc("wt", [128, KC, 512], BF16)
    tst = [ar.alloc("tst", [128, 512], BF16) for _ in range(2)]
    for col0, ncols, DST in ((1280, 256, GV), (2048, 512, HV)):
        load_w(g, POOL, wt[:, :, 0:ncols], WIN[:, col0:col0 + ncols].rearrange("(c p) n -> p c n", p=128), "wt")
        for tt in range(10):
            ti = 0 if tt < 4 else (1 if tt < 8 else 2)
            pi = 1 + state["nps"] % 4
            state["nps"] += 1
            pp = g.ps[pi]
            for c in range(KC):
                P.add(PE, lambda e, c=c, pp=pp, tt=tt: e.matmul(pp[:, 0:ncols], lhsT=hT[:, c, tt * 128:(tt + 1) * 128], rhs=wt[:, c, 0:ncols], start=(c == 0), stop=(c == KC - 1)),
                      reads=["wt"] + ([("h", cc, ti) for cc in range(KC)] if c == 0 else []), writes=[("ps", pi)])
            vb = tst[tt % 2]
            P.add(ACT, lambda e, pp=pp, vb=vb: e.copy(out=vb[:, 0:ncols], in_=pp[:, 0:ncols]), writes=[("ps", pi), ("tst", tt % 2)])
            dma_out(DST[tt * 128:(tt + 1) * 128, :], vb[:, 0:ncols], [("tst", tt % 2)])
    if F is not None:
        return fin
    P.emit(final_wait_ops=fin)
    return nc


DEBUG_MIX = False
NKEY = SEQ + TCX
NKC = NKEY // 128


def build_B(l, last, F=None, half=0):
    if F is None:
        nc = bass.Bass("TRN2", target_bir_lowering=False)
        dt = lambda n, s, d=F32, k="ExternalInput": nc.dram_tensor(n, s, d, kind=k).ap()
    else:
        nc = F.nc
        dt = lambda n, s, d=F32, k="ExternalInput": F.tensor_b(n, s, d, k, l, half)
    XT = dt("xT", [D, T]); VEC = dt("vec", [128, NVEC]); CST = dt("cst", [128, NCST]); MODS = dt("mods", [128, 288])
    GQ = dt("gq", [128, 8, T], BF16); GKA = dt("gk", [128, 2, NKEY], BF16); GVA = dt("gv", [NKEY, 256], BF16)
    MQN = dt("mqn", [128, 4, T], BF16); MQR = dt("mqr", [64, 4, T], BF16); MKN = dt("mkn", [128, 4, NKEY], BF16)
    MKR = dt("mkr", [64, NKEY], BF16); MVA = dt("mv", [NKEY, 512], BF16)
    HQE = dt("hqe", [128, 2, 4, T], BF16); HKT = dt("hkt", [128, 2, 4, T], BF16); HKE = dt("hke", [T, 2, 4, 128], BF16)
    HV = dt("hv", [T, 512], BF16); HSC = dt("hsc", [128, 2, 3, 4, NCH], F32); HG = dt("hg", [128, 4, T], F32)
    PKE = dt("pke", [TL, 2, 4, 128], BF16); PV = dt("pv", [TL, 512], BF16); PSC = dt("psc", [128, 2, 3, 4, 64], F32)
    WOUT = dt("w_out", [D, D]); W2I = dt("w2i", [D, 2 * FH]); W2O = dt("w2o", [FH, D])
    OUT = dt("out", [D, TL if last else T], F32, "ExternalOutput")
    MIXD = dt("mixdbg", [128, KC, T], BF16, "ExternalOutput") if DEBUG_MIX else None

    if F is None:
        P = Prog(nc)
        ar = Arena(nc)
        g = setup_common(nc, P, ar, CST, VEC)
        P.add(SP, lambda e: e.dma_start(out=g.mods[:].rearrange("p j w -> p (j w)"), in_=MODS), writes=["mods"], dma=True)
        derive_mods(g)
    else:
        P, ar, g = F.P, F.ar, F.g
    ps = g.ps
    fin = []
    pdirs = (0, 1) if F is None else ((1,) if half == 0 else (0,))

    SEGS = ((0, 0, 0, TL), (TL, 1, 0, TL), (2 * TL, 0, TL, TCX))

    def ld_feat(dst_tile, np_, name, unf_ap, idx):
        if F is None:
            P.add(SP, lambda e: e.dma_start(out=dst_tile[0:np_, :], in_=unf_ap), dma=True)
            return
        for d0, hf, s0, n in SEGS:
            src = F.mid[(name, l, hf)]
            sap = src[:, idx, s0:s0 + n] if idx is not None else src[:, s0:s0 + n]
            P.add(SP, lambda e, sap=sap, d0=d0, n=n: e.dma_start(out=dst_tile[0:np_, d0:d0 + n], in_=sap), dma=True)

    def ld_tok(dst_tile, name, unf_t, c0, c1):
        if F is None:
            P.add(SP, lambda e: e.dma_start(out=dst_tile[:], in_=unf_t[:, c0:c1].rearrange("(c p) n -> p c n", p=128)), dma=True)
            return
        for d0, hf, s0, n in SEGS:
            sap = F.mid[(name, l, hf)][s0:s0 + n, c0:c1].rearrange("(c p) n -> p c n", p=128)
            P.add(SP, lambda e, sap=sap, d0=d0, n=n: e.dma_start(out=dst_tile[:, d0 // 128:(d0 + n) // 128, :], in_=sap), dma=True)

    qtiles = TILES[:2] if last else TILES
    mixT = ar.alloc("mixT", [128, KC, T], BF16)
    m0 = ar.mark()

    oacc = ar.alloc("oacc", [128, 4, T], F32)
    m_scan = ar.mark()
    hqe = ar.alloc("hqe", [128, 4, T], BF16)
    hkt = ar.alloc("hkt", [128, 4, T], BF16)
    hke = ar.alloc("hke", [64, 20, 4, 128], BF16)
    hv = ar.alloc("hv", [64, 20, 512], BF16)
    pke = ar.alloc("pke", [64, 16, 4, 128], BF16)
    pv = ar.alloc("pv", [64, 16, 512], BF16)
    hsc = ar.alloc("hsc", [128, 2, 3, 4, NCH], F32)
    psc = ar.alloc("psc", [128, 2, 3, 4, 64], F32)
    S = ar.alloc("S", [128, 4, 128], F32)
    Sbs = [ar.alloc("Sb", [128, 4, 128], BF16) for _ in range(2)]
    stmp = ar.alloc("stmp", [128, 4, 128], F32)
    sc = ar.alloc("sc", [64, 4, 64], BF16)
    keMs = [ar.alloc("keM", [64, 4, 4, 128], BF16) for _ in range(2)]
    P.add(SP, lambda e: e.dma_start(out=hv[:], in_=HV.rearrange("(c p) n -> p c n", p=64)), dma=True)
    PVs = PV if F is None else F.mid[("hv", l, 1 - half)][0:TL, :]
    PSCs = PSC if F is None else F.mid[("hsc", l, 1 - half)][:, :, :, :, 0:64]
    P.add(SP, lambda e: e.dma_start(out=pv[:], in_=PVs.rearrange("(c p) n -> p c n", p=64)), dma=True)
    P.add(SP, lambda e: e.dma_start(out=hsc[:], in_=HSC), dma=True)
    P.add(SP, lambda e: e.dma_start(out=psc[:], in_=PSCs), dma=True)
    P.add(POOL, lambda e: e.memset(oacc[:], 0.0))
    bc = lambda ap: ap.rearrange("p (h o) -> p h o", o=1).to_broadcast([128, 4, 128])
    nstep = 0
    for d in range(2):
        for h in range(4):
            P.add(SP, lambda e, h=h: e.dma_start(out=hqe[:, h, :], in_=HQE[:, d, h, :]), dma=True)
            P.add(SP, lambda e, h=h: e.dma_start(out=hkt[:, h, :], in_=HKT[:, d, h, :]), dma=True)
        P.add(SP, lambda e: e.dma_start(out=hke[:], in_=HKE[:, d, :, :].rearrange("(c p) h k -> p c h k", p=64)), dma=True)
        PKEs = PKE[:, d, :, :] if F is None else F.mid[("hke", l, 1 - half)][0:TL, d, :, :]
        if d in pdirs:
            P.add(SP, lambda e: e.dma_start(out=pke[:], in_=PKEs.rearrange("(c p) h k -> p c h k", p=64)), dma=True)
        P.add(DVE, lambda e: e.memset(S[:], 0.0))
        mcol = C_MF if d == 0 else C_MB
        mask = g.cst[0:64, mcol:mcol + 64]
        ctx_t = [16, 17, 18, 19]
        lat_t = list(range(16))
        jord = [0, 1, 2, 3]
        if d == 1:
            ctx_t, lat_t, jord = ctx_t[::-1], lat_t[::-1], jord[::-1]
        steps = [("own", c) for c in ctx_t] + ([("par", c) for c in lat_t] if d in pdirs else []) + [("own", c) for c in lat_t]
        for kind, tt in steps:
            KE, V, SCL = (hke, hv, hsc) if kind == "own" else (pke, pv, psc)
            t0 = tt * 64
            want_out = kind == "own" and not (last and tt >= 16)
            keM = keMs[nstep % 2]
            nstep += 1
            pa, po = ps[1], ps[2]
            for j in range(4):
                P.add(POOL, lambda e, j=j: e.tensor_scalar(out=keM[:, j, :, :], in0=KE[0:64, tt, :, :], scalar1=g.cst[0:64, C_RM + j:C_RM + j + 1], scalar2=None, op0=ALU.mult))
            if want_out:
                for h in range(4):
                    P.add(PE, lambda e, h=h: e.matmul(pa[0:64, h * 64:(h + 1) * 64], lhsT=hkt[:, h, t0:t0 + 64], rhs=hqe[:, h, t0:t0 + 64], start=True, stop=True))
                P.add(DVE, lambda e: e.tensor_tensor(out=sc[:], in0=pa[0:64, 0:256].rearrange("p (h t) -> p h t", h=4),
                                                     in1=mask.rearrange("p (o t) -> p o t", o=1).to_broadcast([64, 4, 64]), op=ALU.mult))
                for h in range(4):
                    P.add(PE, lambda e, h=h: e.matmul(po[:, h * 64:(h + 1) * 64], lhsT=V[0:64, tt, h * 128:(h + 1) * 128], rhs=sc[0:64, h, :], start=(h == 0), stop=False, skip_group_check=True))
            for ji, j in enumerate(jord):
                sci = tt * 4 + j
                if want_out:
                    Sb = Sbs[(nstep * 4 + ji) % 2]
                    P.add(DVE, lambda e, Sb=Sb: e.tensor_tensor(out=Sb[:], in0=S[:], in1=bc(SCL[:, d, 0, :, sci]), op=ALU.mult))
                    for h in range(4):
                        P.add(PE, lambda e, h=h, Sb=Sb: e.matmul(po[:, h * 64 + j * CH:h * 64 + (j + 1) * CH], lhsT=Sb[:, h, :], rhs=hqe[:, h, t0 + j * CH:t0 + (j + 1) * CH],
                                                                 start=False, stop=(ji == 3 and h == 3), skip_group_check=True))
                pS = ps[3 + ji % 2]
                for h in range(4):
                    P.add(PE, lambda e, h=h: e.matmul(pS[:, h * 128:(h + 1) * 128], lhsT=keM[0:64, j, h, :], rhs=V[0:64, tt, h * 128:(h + 1) * 128], start=True, stop=True))
                P.add(POOL, lambda e: e.tensor_tensor(out=S[:], in0=S[:], in1=bc(SCL[:, d, 1, :, sci]), op=ALU.mult))
                P.add(DVE, lambda e: e.tensor_tensor(out=stmp[:], in0=pS[:].rearrange("p (h v) -> p h v", h=4), in1=bc(SCL[:, d, 2, :, sci]), op=ALU.mult))
                P.add(POOL, lambda e: e.tensor_tensor(out=S[:], in0=S[:], in1=stmp[:], op=ALU.add))
            if want_out:
                P.add(DVE, lambda e: e.tensor_tensor(out=oacc[:, :, t0:t0 + 64], in0=po[:, 0:256].rearrange("p (h t) -> p h t", h=4), in1=oacc[:, :, t0:t0 + 64], op=ALU.add))
    ar.release(m_scan)
    hgt = ar.alloc("hgt", [128, T], F32)
    rstd = ar.alloc("rstd", [128, T], F32)
    rtmp = ar.alloc("rtmp", [128, 512], F32)
    sq = [ar.alloc("sq", [128, 512], BF16) for _ in range(2)]
    ytmp = ar.alloc("ytmp", [128, 512], F32)
    okeys = []
    for h in range(4):
        P.add(SP, lambda e, h=h: e.dma_start(out=hgt[:], in_=HG[:, h, :]), writes=["hgt"], dma=True)
        for ti, (t0, n) in enumerate(qtiles):
            col_rstd(g, [(oacc[:, h, t0:t0 + n], okeys)], 128, rstd[:, t0:t0 + n], t0, n, sq, None, ("rstd", ti), rtmp)
            P.add(DVE, lambda e, h=h: e.scalar_tensor_tensor(out=ytmp[:, 0:n], in0=oacc[:, h, t0:t0 + n], scalar=g.vec[:, V_HNG:V_HNG + 1], in1=rstd[:, t0:t0 + n], op0=ALU.mult, op1=ALU.mult),
                  reads=okeys + [("rstd", ti), "vec"], writes=["ytmp"])
            P.add(POOL, lambda e, h=h: e.tensor_tensor(out=mixT[:, 8 + h, t0:t0 + n], in0=ytmp[:, 0:n], in1=hgt[:, t0:t0 + n], op=ALU.mult),
                  reads=["ytmp", "hgt"], writes=[("mix", 8 + h, ti)])
    P.barrier()
    ar.release(m0)

    qh = [ar.alloc("qh", [128, T], BF16) for _ in range(2)]
    qr = [ar.alloc("qr", [64, T], BF16) for _ in range(2)]
    kT = [ar.alloc("kT", [128, NKEY], BF16) for _ in range(2)]
    kr = ar.alloc("kr", [64, NKEY], BF16)
    vv = [ar.alloc("vv", [128, NKC, 128], BF16) for _ in range(2)]
    pT = [ar.alloc("pT", [128, 512], BF16) for _ in range(2)]
    rec = ar.alloc("rec", [128, 512], F32)
    sqb = [ar.alloc("sqa", [128, 512], BF16) for _ in range(2)]
    mx = ar.alloc("mx", [128, 16], F32)
    bias = ar.alloc("bias", [128, 2], F32)
    ld_feat(kr, 64, "mkr", MKR, None)
    cnt = {"s": 0, "o": 0, "sq": 0}

    def max_sq(srcs, ntok, dst_col, tiles):
        cols = []
        for ti, (t0, n) in enumerate(tiles):
            for i, (ap, np_, rk) in enumerate(srcs):
                b = cnt["sq"] % 2
                cnt["sq"] += 1
                P.add(ACT, lambda e, ap=ap, b=b, np_=np_: e.activation(out=sqb[b][0:np_, 0:n], in_=ap[0:np_, t0:t0 + n], func=AF.Square), reads=rk, writes=[("sqa", b)])
                P.add(PE, lambda e, b=b, i=i, np_=np_: e.matmul(ps[0][:, 0:n], lhsT=g.oneb[0:np_, :], rhs=sqb[b][0:np_, 0:n], start=(i == 0), stop=(i == len(srcs) - 1)),
                      reads=[("sqa", b), "oneb"], writes=[("ps", 0)])
            P.add(DVE, lambda e, ti=ti: e.reduce_max(out=mx[:, 8 + ti:9 + ti], in_=ps[0][:, 0:n], axis=AX.X), writes=[("ps", 0), ("mxp", ti)])
            cols.append(ti)
        P.add(DVE, lambda e: e.reduce_max(out=mx[:, dst_col:dst_col + 1], in_=mx[:, 8:8 + len(cols)], axis=AX.X),
              reads=[("mxp", ti) for ti in cols], writes=[("mx", dst_col)])

    KT5 = ((0, 512), (512, 512), (1024, 512), (1536, 512), (2048, 256))
    heads = [("g", h) for h in range(8)] + [("m", h) for h in range(4)]
    for hi_, (kind, h) in enumerate(heads):
        b = hi_ % 2
        if kind == "g":
            kv = h // 4
            scale = 128 ** -0.5
            mixi = h
            P.add(SP, lambda e: e.dma_start(out=qh[b][:], in_=GQ[:, h, :]), writes=[("qh", b)], dma=True)
            if h % 4 == 0:
                kb = kv % 2
                ld_feat(kT[kb], 128, "gk", GKA[:, kv, :] if F is None else None, kv)
                ld_tok(vv[kb], "gv", GVA, kv * 128, (kv + 1) * 128)
                max_sq([(kT[kb], 128, [("kT", kb)])], NKEY, 1, KT5)
            qs = [(qh[b], 128, [("qh", b)])]
        else:
            scale = 192 ** -0.5
            mixi = 12 + h
            kb = h % 2
            P.add(SP, lambda e: e.dma_start(out=qh[b][:], in_=MQN[:, h, :]), writes=[("qh", b)], dma=True)
            P.add(SP, lambda e: e.dma_start(out=qr[b][:], in_=MQR[:, h, :]), writes=[("qr", b)], dma=True)
            ld_feat(kT[kb], 128, "mkn", MKN[:, h, :] if F is None else None, h)
            ld_tok(vv[kb], "mv", MVA, h * 128, (h + 1) * 128)
            max_sq([(kT[kb], 128, [("kT", kb)]), (kr, 64, ["kr"])], NKEY, 1, KT5)
            qs = [(qh[b], 128, [("qh", b)]), (qr[b], 64, [("qr", b)])]
        max_sq(qs, T, 0, TILES)
        P.add(DVE, lambda e: e.tensor_tensor(out=mx[:, 2:3], in0=mx[:, 0:1], in1=mx[:, 1:2], op=ALU.mult), reads=[("mx", 0), ("mx", 1)], writes=[("mx", 2)])
        P.add(ACT, lambda e: e.activation(out=mx[:, 3:4], in_=mx[:, 2:3], func=AF.Sqrt, scale=float(scale * scale)), reads=[("mx", 2)], writes=[("mx", 3)])
        bcol = hi_ % 2
        P.add(DVE, lambda e: e.tensor_scalar_mul(out=bias[:, bcol:bcol + 1], in0=mx[:, 3:4], scalar1=-1.0), reads=[("mx", 3)], writes=[("bias", bcol)])
        for ti, (t0, n) in enumerate(qtiles):
            chunks = list(range(NKC)) if t0 < TL else [16, 17]
            oi = cnt["o"] % 2
            cnt["o"] += 1
            po, psm = ps[3 + oi], ps[5 + oi]
            for ci, c in enumerate(chunks):
                si = 1 + cnt["s"] % 2
                pb = cnt["s"] % 2
                cnt["s"] += 1
                pss = ps[si]
                if kind == "g":
                    P.add(PE, lambda e: e.matmul(pss[:, 0:n], lhsT=kT[kb][:, c * 128:(c + 1) * 128], rhs=qh[b][:, t0:t0 + n], start=True, stop=True),
                          reads=[("kT", kb), ("qh", b)], writes=[("ps", si)])
                else:
                    P.add(PE, lambda e: e.matmul(pss[:, 0:n], lhsT=kT[kb][:, c * 128:(c + 1) * 128], rhs=qh[b][:, t0:t0 + n], start=True, stop=False),
                          reads=[("kT", kb), ("qh", b)], writes=[("ps", si)])
                    P.add(PE, lambda e: e.matmul(pss[:, 0:n], lhsT=kr[0:64, c * 128:(c + 1) * 128], rhs=qr[b][0:64, t0:t0 + n], start=False, stop=True),
                          reads=["kr", ("qr", b)], writes=[("ps", si)])
                P.add(ACT, lambda e: e.activation(out=pT[pb][:, 0:n], in_=pss[:, 0:n], func=AF.Exp, bias=bias[:, bcol:bcol + 1], scale=float(scale)),
                      reads=[("bias", bcol)], writes=[("ps", si), ("pT", pb)])
                P.add(PE, lambda e: e.matmul(po[:, 0:n], lhsT=vv[kb][:, c, :], rhs=pT[pb][:, 0:n], start=(ci == 0), stop=(ci == len(chunks) - 1)),
                      reads=[("vv", kb), ("pT", pb)], writes=[("ps", 3 + oi)])
                P.add(PE, lambda e: e.matmul(psm[:, 0:n], lhsT=g.oneb[:], rhs=pT[pb][:, 0:n], start=(ci == 0), stop=(ci == len(chunks) - 1)),
                      reads=["oneb", ("pT", pb)], writes=[("ps", 5 + oi)])
            P.add(DVE, lambda e: e.reciprocal(out=rec[:, 0:n], in_=psm[:, 0:n]), writes=[("ps", 5 + oi), "rec"])
            P.add(DVE, lambda e: e.tensor_tensor(out=mixT[:, mixi, t0:t0 + n], in0=po[:, 0:n], in1=rec[:, 0:n], op=ALU.mult),
                  reads=["rec"], writes=[("ps", 3 + oi), ("mix", mixi, ti)])
    P.barrier()
    ar.release(m0)

    if DEBUG_MIX:
        nm = TL if last else T
        fin.append(P.add(SP, lambda e: e.dma_start(out=MIXD[:, :, 0:nm], in_=mixT[:, :, 0:nm]), dma=True))
        P.barrier()
    xT = ar.alloc("xT", [128, KC, T], F32)
    for c in range(KC):
        P.add(SP, lambda e, c=c: e.dma_start(out=xT[:, c, :], in_=XT[c * 128:(c + 1) * 128, :]),
              writes=[("x", c, 0), ("x", c, 1), ("x", c, 2)], dma=True)
    fb = alloc_ffn_bufs(ar)
    nps = 0
    for ch in range(KC):
        b = ch % 2
        load_w(g, POOL, fb.wg[b][:], WOUT[:, ch * 128:(ch + 1) * 128].rearrange("(c p) n -> p c n", p=128), ("wg", b))
        for ti, (t0, n) in enumerate(qtiles):
            which = 0 if t0 < TL else 1
            q = nps % 2
            nps += 1
            pp = ps[5 + q]
            for c in range(KC):
                P.add(PE, lambda e, c=c: e.matmul(pp[:, 0:n], lhsT=fb.wg[b][:, c, :], rhs=mixT[:, c, t0:t0 + n], start=(c == 0), stop=(c == KC - 1)),
                      reads=[("wg", b)], writes=[("ps", 5 + q)])
            P.add(DVE, lambda e: e.scalar_tensor_tensor(out=xT[:, ch, t0:t0 + n], in0=pp[:, 0:n], scalar=g.mods[:, 5 * 16 + ch, which:which + 1], in1=xT[:, ch, t0:t0 + n], op0=ALU.mult, op1=ALU.add),
                  reads=["mods"], writes=[("ps", 5 + q), ("x", ch, ti)])
    P.barrier()
    hT = mixT
    norm_mod(g, xT, hT, 6, 7, fb, tiles=qtiles)
    ffn(g, xT, hT, W2I, W2O, 8, fb, tiles=qtiles)
    if last:
        for ti, (t0, n) in enumerate(qtiles):
            col_rstd(g, [(xT[:, c, t0:t0 + n], [("x", c, ti)]) for c in range(KC)], D, fb.rstd[:, t0:t0 + n], t0, n, fb.sq, None, ("rstd", ti), fb.rtmp)
            for c in range(KC):
                P.add(DVE, lambda e, c=c: e.scalar_tensor_tensor(out=xT[:, c, t0:t0 + n], in0=xT[:, c, t0:t0 + n], scalar=g.vec[:, V_FG + c:V_FG + c + 1], in1=fb.rstd[:, t0:t0 + n], op0=ALU.mult, op1=ALU.mult),
                      reads=[("rstd", ti), "vec"], writes=[("x", c, ti)])
    ncols = TL if last else T
    for c in range(KC):
        fin.append(P.add(SP, lambda e, c=c: e.dma_start(out=OUT[c * 128:(c + 1) * 128, :], in_=xT[:, c, 0:ncols]),
                         reads=[("x", c, 0), ("x", c, 1), ("x", c, 2)], dma=True))
    if F is not None:
        return fin
    P.emit(final_wait_ops=fin)
    return nc


class Fused:
    def __init__(self, nc, P, ar, g):
        self.nc, self.P, self.ar, self.g = nc, P, ar, g
        self.mid, self.xin, self.w, self.rope, self.outs = {}, {}, {}, {}, {}

    def tensor(self, n, s, d, k, l, half):
        if n == "xT":
            return self.xin[(l, half)]
        if n == "rope":
            return self.rope[half]
        if n in ("vec", "cst", "mods", "w_mod"):
            return None
        if k == "ExternalInput":
            return self.w[(n, l)]
        t = self.nc.dram_tensor(f"{n}_{l}_{half}", s, d, kind="Internal").ap()
        self.mid[(n, l, half)] = t
        return t

    def tensor_b(self, n, s, d, k, l, half):
        if n == "xT":
            return self.mid[("xo", l, half)]
        if n in ("vec", "cst", "mods", "gk", "gv", "mkn", "mkr", "mv", "pke", "pv", "psc"):
            return None
        if n in ("gq", "mqn", "mqr", "hqe", "hkt", "hke", "hv", "hsc", "hg"):
            return self.mid[(n, l, half)]
        if n == "out":
            return self.xin[(l + 1, half)] if (l + 1, half) in self.xin else self.outs[half]
        if n == "mixdbg":
            return self.nc.dram_tensor(f"mixdbg_{l}_{half}", s, d, kind="ExternalOutput").ap()
        return self.w[(n, l)]


W_A = (("w1i", [D, 2 * FH]), ("w1o", [FH, D]), ("w_in", [D, 4928]), ("w_uq", [512, 768]), ("w_ukv", [256, 1024]))
W_B = (("w_out", [D, D]), ("w2i", [D, 2 * FH]), ("w2o", [FH, D]))
DEPTH = 2


def build_fused():
    nc = bass.Bass("TRN2", target_bir_lowering=False)
    ein = lambda n, s, d=F32: nc.dram_tensor(n, s, d, kind="ExternalInput").ap()
    CST = ein("cst", [128, NCST])
    VECS = [ein(f"vec{l}", [128, NVEC]) for l in range(DEPTH)]
    WMODS = [ein(f"w_mod{l}", [D, NMOD * D]) for l in range(DEPTH)]
    P = Prog(nc)
    ar = Arena(nc)
    g = setup_common(nc, P, ar, CST, VECS[0])
    F = Fused(nc, P, ar, g)
    for h in range(2):
        F.xin[(0, h)] = ein(f"xT{h}", [D, T])
        F.rope[h] = ein(f"rope{h}", [128, 4, TL])
        F.outs[h] = nc.dram_tensor(f"out{h}", [D, TL], F32, kind="ExternalOutput").ap()
        for l in range(1, DEPTH):
            F.xin[(l, h)] = nc.dram_tensor(f"x{l}_{h}", [D, T], F32, kind="Internal").ap()
    for l in range(DEPTH):
        for n, shp in W_A + W_B:
            F.w[(n, l)] = ein(f"{n}{l}", shp)
    base = ar.mark()
    fin = []
    for l in range(DEPTH):
        last = l == DEPTH - 1
        if l > 0:
            P.add(SP, lambda e: e.dma_start(out=g.vec[:], in_=VECS[l]), dma=True)
        ar.release(base)
        compute_mods(g, ar, WMODS[l])
        for half in range(2):
            ar.release(base)
            build_A(l, last, F, half)
        for half in range(2):
            ar.release(base)
            f = build_B(l, last, F, half)
            if last:
                fin += f
    P.emit(final_wait_ops=fin)
    return nc


NCORES = 4
_CACHE = {}


def fused_inputs(inp, b, cst, ropes):
    m = {"cst": cst}
    for h in range(2):
        xl = inp["x"][b, h * TL:(h + 1) * TL]
        m[f"xT{h}"] = np.ascontiguousarray(np.concatenate([xl, inp["ctx"][b]], axis=0).T.astype(np.float32))
        m[f"rope{h}"] = ropes[h]
    for l in range(DEPTH):
        m[f"vec{l}"] = host_vec(inp, l, b)
        m[f"w_mod{l}"] = inp["w_mod"][l]
        m[f"w1i{l}"] = inp["w_ffn1_in"][l]
        m[f"w1o{l}"] = inp["w_ffn1_out"][l]
        m[f"w_in{l}"] = inp["w_in"][l]
        m[f"w_uq{l}"] = inp["w_uq"][l]
        m[f"w_ukv{l}"] = inp["w_ukv"][l]
        m[f"w_out{l}"] = inp["w_out"][l]
        m[f"w2i{l}"] = inp["w_ffn2_in"][l]
        m[f"w2o{l}"] = inp["w_ffn2_out"][l]
    return m


def kernel(**inputs):
    inp = {k: np.asarray(v) for k, v in inputs.items()}
    cst = host_consts()
    ropes = [host_rope(0), host_rope(1)]
    if "nc" not in _CACHE:
        _CACHE["nc"] = build_fused()
    cores = list(range(NCORES))
    res = run_bass_kernel_spmd(_CACHE["nc"], [fused_inputs(inp, b, cst, ropes) for b in cores], core_ids=cores)
    out = np.zeros((4, SEQ, D), np.float32)
    for b in cores:
        for h in range(2):
            out[b, h * TL:(h + 1) * TL, :] = res.results[b][f"out{h}"].T
    return out


def _a_inputs(inp, l, core, xT, cst, ropes):
    b, half = core // 2, core % 2
    return {"xT": xT, "vec": host_vec(inp, l, b), "cst": cst, "rope": ropes[half],
            "w_mod": inp["w_mod"][l], "w1i": inp["w_ffn1_in"][l], "w1o": inp["w_ffn1_out"][l], "w_in": inp["w_in"][l],
            "w_uq": inp["w_uq"][l], "w_ukv": inp["w_ukv"][l]}


def _b_inputs(inp, l, core, ra, cst):
    b, half = core // 2, core % 2
    r = ra[core]
    r0, r1 = ra[2 * b], ra[2 * b + 1]
    rp = ra[core ^ 1]
    catk = lambda k: np.ascontiguousarray(np.concatenate([r0[k][..., :TL], r1[k][..., :TL], r0[k][..., TL:]], axis=-1))
    catv = lambda k: np.ascontiguousarray(np.concatenate([r0[k][:TL], r1[k][:TL], r0[k][TL:]], axis=0))
    pdir = 1 if half == 0 else 0
    pke = np.zeros((TL, 2, 4, 128), ml_dtypes.bfloat16)
    pke[:, pdir] = rp["hke"][:TL, pdir]
    psc = np.ones((128, 2, 3, 4, 64), np.float32)
    psc[:, pdir] = rp["hsc"][:, pdir, :, :, :64]
    psc[:, 1 - pdir, 2] = 0.0
    return {"xT": r["xo"], "vec": host_vec(inp, l, b), "cst": cst, "mods": r["mods"],
            "gq": r["gq"], "gk": catk("gk"), "gv": catv("gv"),
            "mqn": r["mqn"], "mqr": r["mqr"], "mkn": catk("mkn"), "mkr": catk("mkr"), "mv": catv("mv"),
            "hqe": r["hqe"], "hkt": r["hkt"], "hke": r["hke"], "hv": r["hv"], "hsc": r["hsc"], "hg": r["hg"],
            "pke": pke, "pv": np.ascontiguousarray(rp["hv"][:TL]), "psc": psc,
            "w_out": inp["w_out"][l], "w2i": inp["w_ffn2_in"][l], "w2o": inp["w_ffn2_out"][l]}
```

```python
import numpy as np
import ml_dtypes
import concourse.bass as bass
import concourse.mybir as mybir
from concourse.bass_utils import run_bass_kernel_spmd

F32 = mybir.dt.float32
BF16 = mybir.dt.bfloat16
AF = mybir.ActivationFunctionType
ALU = mybir.AluOpType
AX = mybir.AxisListType

PE, ACT, DVE, POOL, SP = "pe", "act", "dve", "pool", "sp"
ENGS = (PE, ACT, DVE, POOL, SP)
N_DMA_SEMS = 8
ANNOTATE = False
EMBED_WAIT = False

D = 2048
KC = 16
TL = 1024
TCX = 256
T = TL + TCX
SEQ = 2048
FH = 5504
FC = 43
NMOD = 9
EPS = 1e-6
CH = 16
NCH = T // CH
TILES = ((0, 512), (512, 512), (1024, 256))


class Op:
    __slots__ = ("eng", "fn", "idx", "deps", "is_dma", "signal", "seq", "dsem", "dval", "nd", "tag")

    def __init__(self, eng, fn, is_dma):
        self.eng = eng
        self.fn = fn
        self.is_dma = is_dma
        self.signal = False
        self.seq = None
        self.deps = []
        self.dsem = None
        self.dval = None
        self.nd = None


class _Rec:
    def __getattr__(self, name):
        def f(*a, **k):
            self.call = (name, a, k)
            return self
        return f


_ESZ = {}


def _esize(dtype):
    k = str(dtype)
    if k not in _ESZ:
        _ESZ[k] = 4 if "32" in k else (2 if "16" in k else (1 if "8" in k else 4))
    return _ESZ[k]


def _footprint(ap, psum_bank):
    sp = str(ap.space)
    t = ap.tensor
    if "PSUM" in sp:
        return ("P", psum_bank[t.name], 0, 1 << 30, 0, 128)
    if "DRAM" in sp:
        return ("D", t.name, 0, 1 << 30, 0, 128)
    apl = ap.ap
    pstep, pcount = apl[0]
    off = ap.offset
    if pstep:
        p0 = off // pstep
        foff = off - p0 * pstep
    else:
        p0, foff = 0, off
    ext = 0
    for st, c in apl[1:]:
        ext += (c - 1) * abs(st)
    esz = _esize(ap.dtype)
    base = t.manual_sbuf_range[0]
    return ("S", None, base + foff * esz, base + (foff + ext + 1) * esz, p0, p0 + pcount)


def _ov(a, b):
    return a[0] == b[0] and a[1] == b[1] and a[2] < b[3] and b[2] < a[3] and a[4] < b[5] and b[4] < a[5]


def _covers(a, b):
    return a[0] == b[0] and a[1] == b[1] and a[2] <= b[2] and a[3] >= b[3] and a[4] <= b[4] and a[5] >= b[5]


def _is_ap(x):
    return hasattr(x, "tensor") and hasattr(x, "ap") and hasattr(x, "offset")


class Prog:
    def __init__(self, nc):
        self.nc = nc
        self.ops = {e: [] for e in ENGS}
        self.last_writer = {}
        self.readers = {}
        self.barrier_ops = []
        self.psum_bank = {}
        self.wlog = []
        self.rlog = []

    def _addr_deps(self, op, call, deps):
        name, a, k = call
        outs, ins = [], []
        if "out" in k:
            outs.append(k["out"])
        elif a and _is_ap(a[0]):
            outs.append(a[0])
        for i, x in enumerate(a):
            if _is_ap(x) and not (i == 0 and "out" not in k):
                ins.append(x)
        for kk_, x in k.items():
            if kk_ != "out" and _is_ap(x):
                ins.append(x)
        wf = [_footprint(x, self.psum_bank) for x in outs]
        rf = []
        for x in ins:
            f = _footprint(x, self.psum_bank)
            (wf if f[0] == "P" else rf).append(f)
        for f in rf:
            for f2, o2 in self.wlog:
                if _ov(f, f2):
                    deps[id(o2)] = o2
        for f in wf:
            for f2, o2 in self.wlog:
                if _ov(f, f2):
                    deps[id(o2)] = o2
            for f2, o2 in self.rlog:
                if _ov(f, f2):
                    deps[id(o2)] = o2
        for f in wf:
            self.wlog = [(f2, o2) for f2, o2 in self.wlog if not _covers(f, f2)]
            self.rlog = [(f2, o2) for f2, o2 in self.rlog if not _covers(f, f2)]
            self.wlog.append((f, op))
        for f in rf:
            if f[0] == "D":
                continue
            done = False
            if not op.is_dma:
                for i, (f2, o2) in enumerate(self.rlog):
                    if f2 == f and o2.eng == op.eng and not o2.is_dma:
                        self.rlog[i] = (f, op)
                        done = True
                        break
            if not done:
                self.rlog.append((f, op))

    def add(self, eng, fn, reads=(), writes=(), dma=False):
        rec = _Rec()
        fn(rec)
        op = Op(eng, rec.call, dma)
        import sys as _sys
        f = _sys._getframe(1)
        op.tag = f"L{f.f_lineno}"
        if f.f_back is not None and f.f_code.co_name in ("<lambda>", "A", "load_w", "dma_out", "sink"):
            op.tag += f"<L{f.f_back.f_lineno}"
            if f.f_back.f_back is not None:
                op.tag += f"<L{f.f_back.f_back.f_lineno}"
        op.idx = len(self.ops[eng])
        deps = {}
        for r in reads:
            w = self.last_writer.get(r)
            if w is not None:
                deps[id(w)] = w
        for r in writes:
            w = self.last_writer.get(r)
            if w is not None:
                deps[id(w)] = w
            for rd in self.readers.get(r, ()):
                deps[id(rd)] = rd
        for b in self.barrier_ops:
            deps[id(b)] = b
        self._addr_deps(op, rec.call, deps)
        newest = {}
        for d in deps.values():
            if d.is_dma:
                op.deps.append(d)
                continue
            if d.eng == eng and not dma and (eng == PE or eng == SP):
                continue
            cur = newest.get(d.eng)
            if cur is None or d.idx > cur.idx:
                newest[d.eng] = d
        for d in newest.values():
            op.deps.append(d)
            d.signal = True
        for r in writes:
            self.last_writer[r] = op
            self.readers[r] = []
        for r in reads:
            if r in writes:
                continue
            self.readers.setdefault(r, []).append(op)
        if dma:
            op.signal = True
        self.ops[eng].append(op)
        return op

    def barrier(self):
        bl = []
        for e in ENGS:
            ops = self.ops[e]
            last_c = None
            for o in reversed(ops):
                if not o.is_dma:
                    last_c = o
                    break
            if last_c is not None:
                bl.append(last_c)
            n = 0
            for o in reversed(ops):
                if o.is_dma:
                    bl.append(o)
                    n += 1
                    if n >= N_DMA_SEMS:
                        break
        self.barrier_ops = bl
        self.last_writer = {}
        self.readers = {}
        self.wlog = []
        self.rlog = []

    def emit(self, final_wait_ops=()):
        nc = self.nc
        sems = {e: nc.alloc_semaphore(name=f"s_{e}") for e in ENGS}
        dsems = {e: [nc.alloc_semaphore(name=f"d_{e}{i}") for i in range(N_DMA_SEMS)]
                 for e in (SP, POOL, ACT)}
        for e in ENGS:
            seq = 0
            nd = 0
            for op in self.ops[e]:
                if op.is_dma:
                    op.dsem = dsems[e][nd % N_DMA_SEMS]
                    op.dval = 16 * (nd // N_DMA_SEMS + 1)
                    op.nd = nd
                    nd += 1
                elif op.signal:
                    seq += 1
                    op.seq = seq
        engobj = {PE: "tensor", ACT: "scalar", DVE: "vector", POOL: "gpsimd", SP: "sync"}
        with nc.Block() as block:
            for e in ENGS:
                ops = self.ops[e]

                def body(eng, ops=ops, e=e):
                    waited = {}
                    for op in ops:
                        need = {}
                        for d in op.deps:
                            if d.is_dma:
                                key = ("d", d.eng, d.nd % N_DMA_SEMS)
                                val = d.dval
                                sem = d.dsem
                            else:
                                key = ("e", d.eng)
                                val = d.seq
                                sem = sems[d.eng]
                            if need.get(key, (None, 0))[1] < val:
                                need[key] = (sem, val)
                        if op.is_dma and op.nd >= N_DMA_SEMS:
                            key = ("d", e, op.nd % N_DMA_SEMS)
                            val = op.dval - 16
                            if need.get(key, (None, 0))[1] < val:
                                need[key] = (op.dsem, val)
                        todo = []
                        for key, (sem, val) in need.items():
                            if waited.get(key, 0) >= val:
                                continue
                            todo.append((sem, val))
                            waited[key] = val
                        emb = todo.pop() if (todo and EMBED_WAIT) else None
                        for sem, val in todo:
                            eng.wait_ge(sem, val)
                        name, a, k = op.fn
                        ins = getattr(eng, name)(*a, **k)
                        if emb is not None:
                            ins._wait_ge(emb[0], emb[1])
                        if ANNOTATE:
                            ins.annotate(op.tag)
                        if op.is_dma:
                            ins.then_inc(op.dsem, 16)
                        elif op.signal:
                            ins.then_inc(sems[e], 1)
                    if e == SP:
                        for fo in final_wait_ops:
                            eng.wait_ge(fo.dsem, fo.dval)

                getattr(block, engobj[e])(body)


class Arena:
    def __init__(self, nc, lo=16640, hi=229000):
        self.nc = nc
        self.lo = lo
        self.hi = hi
        self.top = lo
        self.n = 0

    def alloc(self, name, shape, dtype):
        per = 1
        for s in shape[1:]:
            per *= s
        nbytes = per * (4 if dtype == F32 else 2)
        nbytes = (nbytes + 63) // 64 * 64
        assert self.top + nbytes <= self.hi, f"SBUF arena overflow at {name}: {self.top}+{nbytes}"
        h = self.nc.alloc_sbuf_tensor_at(f"{name}_{self.n}", list(shape), dtype, offset=self.top)
        self.n += 1
        self.top += nbytes
        return h

    def mark(self):
        return self.top

    def release(self, m):
        self.top = m


C_ID, C_ONE, C_MF, C_MB, C_RH, C_RR, C_EPS, C_RM, C_SM = 0, 128, 256, 320, 384, 512, 576, 577, 581
NCST = 581 + T


def host_consts():
    c = np.zeros((128, NCST), np.float32)
    c[:, C_ID:C_ID + 128] = np.eye(128, dtype=np.float32)
    c[:, C_ONE:C_ONE + 128] = 1.0
    s = np.arange(64)[:, None]
    t = np.arange(64)[None, :]
    c[:64, C_MF:C_MF + 64] = (s <= t) & (s // CH == t // CH)
    c[:64, C_MB:C_MB + 64] = (s >= t) & (s // CH == t // CH)
    for j in range(4):
        c[:64, C_RM + j] = (np.arange(64) // CH == j)
    def rmat(dim):
        r = np.zeros((128, 128), np.float32)
        q = dim // 4
        for i in range(q):
            r[q + i, i] = -1.0
            r[i, q + i] = 1.0
            r[3 * q + i, 2 * q + i] = -1.0
            r[2 * q + i, 3 * q + i] = 1.0
        return r
    c[:, C_RH:C_RH + 128] = rmat(128)
    c[:, C_RR:C_RR + 64] = rmat(64)[:, :64]
    c[:, C_EPS] = EPS
    sm = np.ones(T, np.float32)
    sm[::CH] = 0.0
    c[:, C_SM:C_SM + T] = sm[None, :]
    return c


def host_rope(half):
    pos = np.arange(half * TL, (half + 1) * TL)
    row = (pos // 64).astype(np.float32)
    col = (pos % 64).astype(np.float32)
    out = np.zeros((128, 4, TL), np.float32)
    for idx, dim in ((0, 128), (2, 64)):
        ad = dim // 2
        inv = (10000.0 ** (-np.arange(0, ad, 2, dtype=np.float32) / ad)).astype(np.float32)
        ar = row[:, None] * inv
        ac = col[:, None] * inv
        ang = np.concatenate([ar, ar, ac, ac], axis=-1).astype(np.float32)
        out[:dim, idx, :] = np.cos(ang).T
        out[:dim, idx + 1, :] = np.sin(ang).T
    return out


V_C, V_BM, V_GQ, V_GK, V_MQG, V_MKG, V_LB, V_HNG, V_FG = 0, 32, 176, 177, 178, 182, 184, 200, 201
NVEC = 217


def host_vec(inp, l, b):
    v = np.zeros((128, NVEC), np.float32)
    cc = np.stack([inp["c"][b], inp["c_ctx"]], axis=-1)
    v[:, V_C:V_C + 32] = cc.reshape(16, 128, 2).transpose(1, 0, 2).reshape(128, 32)
    v[:, V_BM:V_BM + 144] = inp["b_mod"][l].reshape(144, 128).T
    v[:, V_GQ] = inp["gqa_q_gain"][l]
    v[:, V_GK] = inp["gqa_k_gain"][l]
    v[:, V_MQG:V_MQG + 4] = inp["mla_q_gain"][l].reshape(4, 128).T
    v[:, V_MKG:V_MKG + 2] = inp["mla_kv_gain"][l].reshape(2, 128).T
    lg = inp["hgrn_lb_logits"]
    v[:, V_LB:V_LB + 16] = lg.reshape(2, 2, 4, 128).transpose(3, 0, 1, 2).reshape(128, 16)
    v[:, V_HNG] = inp["hgrn_norm_gain"][l]
    v[:, V_FG:V_FG + 16] = inp["final_gain"].reshape(16, 128).T
    return v


class Ctx:
    pass


def setup_common(nc, P, ar, CST, VEC, bf16_bank=True):
    g = Ctx()
    g.nc, g.P, g.ar = nc, P, ar
    g.ps = [nc.alloc_psum_tensor(f"ps{i}", [128, 512], F32) for i in range(7 if bf16_bank else 8)]
    if bf16_bank:
        g.psb = nc.alloc_psum_tensor("psb", [128, 1024], BF16)
    g.cst = ar.alloc("cst", [128, NCST], F32)
    g.vec = ar.alloc("vec", [128, NVEC], F32)
    g.idb = ar.alloc("idb", [128, 128], BF16)
    g.oneb = ar.alloc("oneb", [128, 128], BF16)
    g.rhb = ar.alloc("rhb", [128, 128], BF16)
    g.rrb = ar.alloc("rrb", [128, 64], BF16)
    g.mods = ar.alloc("mods", [128, 144, 2], F32)
    g.onep = ar.alloc("onep", [128, 144, 2], F32)
    g.hgate = ar.alloc("hgate", [128, 144, 2], F32)
    P.add(SP, lambda e: e.dma_start(out=g.cst[:], in_=CST), writes=["cst"], dma=True)
    P.add(SP, lambda e: e.dma_start(out=g.vec[:], in_=VEC), writes=["vec"], dma=True)
    P.add(DVE, lambda e: e.tensor_copy(out=g.idb[:], in_=g.cst[:, C_ID:C_ID + 128]), reads=["cst"], writes=["idb"])
    P.add(DVE, lambda e: e.tensor_copy(out=g.oneb[:], in_=g.cst[:, C_ONE:C_ONE + 128]), reads=["cst"], writes=["oneb"])
    P.add(DVE, lambda e: e.tensor_copy(out=g.rhb[:], in_=g.cst[:, C_RH:C_RH + 128]), reads=["cst"], writes=["rhb"])
    P.add(DVE, lambda e: e.tensor_copy(out=g.rrb[:], in_=g.cst[:, C_RR:C_RR + 64]), reads=["cst"], writes=["rrb"])
    g.eps = g.cst[:, C_EPS:C_EPS + 1]
    for i, t in enumerate(g.ps):
        P.psum_bank[t.name] = i
    if bf16_bank:
        P.psum_bank[g.psb.name] = 7
    return g


def derive_mods(g):
    P = g.P
    P.add(DVE, lambda e: e.tensor_scalar_add(out=g.onep[:], in0=g.mods[:], scalar1=1.0), reads=["mods"], writes=["onep"])
    P.add(DVE, lambda e: e.tensor_scalar_mul(out=g.hgate[:], in0=g.mods[:], scalar1=0.5), reads=["mods"], writes=["hgate"])


def col_rstd(g, srcs, n_feat, rstd_ap, t0, n, sq_bufs, keys_r, key_w, tmp):
    P = g.P
    psn = g.ps[0]
    nk = len(srcs)
    for i, (ap, rk) in enumerate(srcs):
        sb = sq_bufs[i % 2]
        P.add(ACT, lambda e, ap=ap, sb=sb: e.activation(out=sb[:, 0:n], in_=ap, func=AF.Square),
              reads=rk, writes=[("sq", i % 2)])
        P.add(PE, lambda e, sb=sb, i=i: e.matmul(psn[:, 0:n], lhsT=g.oneb[:], rhs=sb[:, 0:n], start=(i == 0), stop=(i == nk - 1)),
              reads=[("sq", i % 2), "oneb"], writes=[("ps", 0)])
    P.add(ACT, lambda e: e.activation(out=tmp[:, 0:n], in_=psn[:, 0:n], func=AF.Sqrt, bias=g.eps, scale=1.0 / n_feat),
          reads=["cst"], writes=[("ps", 0), "rstd_tmp"])
    P.add(DVE, lambda e: e.reciprocal(out=rstd_ap, in_=tmp[:, 0:n]), reads=["rstd_tmp"], writes=[key_w])


def norm_mod(g, xT, hT, i_shift, i_scale, bufs, tiles=TILES):
    P = g.P
    for ti, (t0, n) in enumerate(tiles):
        which = 0 if t0 < TL else 1
        srcs = [(xT[:, c, t0:t0 + n], [("x", c, ti)]) for c in range(KC)]
        col_rstd(g, srcs, D, bufs.rstd[:, t0:t0 + n], t0, n, bufs.sq, None, ("rstd", ti), bufs.rtmp)
        for c in range(KC):
            tb = bufs.tmp[c % 2]
            P.add(DVE, lambda e, c=c, tb=tb: e.tensor_tensor(out=tb[:, 0:n], in0=xT[:, c, t0:t0 + n], in1=bufs.rstd[:, t0:t0 + n], op=ALU.mult),
                  reads=[("x", c, ti), ("rstd", ti)], writes=[("nt", c % 2)])
            P.add(ACT, lambda e, c=c, tb=tb: e.activation(out=hT[:, c, t0:t0 + n], in_=tb[:, 0:n], func=AF.Identity,
                                                         bias=g.mods[:, i_shift * 16 + c, which:which + 1],
                                                         scale=g.onep[:, i_scale * 16 + c, which:which + 1]),
                  reads=[("nt", c % 2), "mods", "onep"], writes=[("h", c, ti)])


def load_w(g, eng, dst, src, key):
    return g.P.add(eng, lambda e: e.dma_start(out=dst, in_=src), writes=[key], dma=True)


def ffn(g, xT, hT, w_in, w_out, i_gate, bufs, tiles=TILES, GS=8):
    P = g.P
    ps = g.ps
    cnt = 0
    cnt2 = 0
    nld = 0
    for g0 in range(0, FC, GS):
        gs = min(GS, FC - g0)
        for jl in range(gs):
            j = g0 + jl
            b = nld % 2
            nld += 1
            load_w(g, POOL, bufs.wg[b][:], w_in[:, j * 128:(j + 1) * 128].rearrange("(c p) n -> p c n", p=128), ("wg", b))
            load_w(g, POOL, bufs.wu[b][:], w_in[:, FH + j * 128:FH + (j + 1) * 128].rearrange("(c p) n -> p c n", p=128), ("wu", b))
            for ti, (t0, n) in enumerate(tiles):
                q = cnt % 2
                cnt += 1
                pg, pu = ps[1 + q], ps[3 + q]
                hk = [("h", c, ti) for c in range(KC)]
                for c in range(KC):
                    P.add(PE, lambda e, c=c, pg=pg, b=b: e.matmul(pg[:, 0:n], lhsT=bufs.wg[b][:, c, :], rhs=hT[:, c, t0:t0 + n], start=(c == 0), stop=(c == KC - 1)),
                          reads=[("wg", b)] + (hk if c == 0 else []), writes=[("ps", 1 + q)])
                for c in range(KC):
                    P.add(PE, lambda e, c=c, pu=pu, b=b: e.matmul(pu[:, 0:n], lhsT=bufs.wu[b][:, c, :], rhs=hT[:, c, t0:t0 + n], start=(c == 0), stop=(c == KC - 1)),
                          reads=[("wu", b)], writes=[("ps", 3 + q)])
                sg = bufs.sgt[q]
                P.add(ACT, lambda e, pg=pg, sg=sg: e.activation(out=sg[:, 0:n], in_=pg[:, 0:n], func=AF.Silu),
                      writes=[("ps", 1 + q), ("sgt", q)])
                P.add(DVE, lambda e, pu=pu, sg=sg, jl=jl: e.tensor_tensor(out=bufs.aT[:, jl, t0:t0 + n], in0=pu[:, 0:n], in1=sg[:, 0:n], op=ALU.mult),
                      reads=[("sgt", q)], writes=[("ps", 3 + q), ("a", jl, ti)])
        for db in range(4):
            b = db % 2
            load_w(g, POOL, bufs.wo[b][:, 0:gs, :], w_out[g0 * 128:(g0 + gs) * 128, db * 512:(db + 1) * 512].rearrange("(c p) n -> p c n", p=128), ("wo", b))
            for dc in range(4):
                ch = db * 4 + dc
                for ti, (t0, n) in enumerate(tiles):
                    which = 0 if t0 < TL else 1
                    q = cnt2 % 2
                    cnt2 += 1
                    po = ps[5 + q]
                    for jl in range(gs):
                        P.add(PE, lambda e, jl=jl, po=po, b=b, dc=dc: e.matmul(po[:, 0:n], lhsT=bufs.wo[b][:, jl, dc * 128:(dc + 1) * 128], rhs=bufs.aT[:, jl, t0:t0 + n], start=(jl == 0), stop=(jl == gs - 1)),
                              reads=[("wo", b), ("a", jl, ti)], writes=[("ps", 5 + q)])
                    P.add(DVE, lambda e, po=po, ch=ch, which=which: e.scalar_tensor_tensor(out=xT[:, ch, t0:t0 + n], in0=po[:, 0:n], scalar=g.hgate[:, i_gate * 16 + ch, which:which + 1], in1=xT[:, ch, t0:t0 + n], op0=ALU.mult, op1=ALU.add),
                          reads=["hgate"], writes=[("ps", 5 + q), ("x", ch, ti)])


def alloc_ffn_bufs(ar):
    b = Ctx()
    b.rstd = ar.alloc("rstd", [128, T], F32)
    b.rtmp = ar.alloc("rtmp", [128, 512], F32)
    b.sq = [ar.alloc("sq", [128, 512], BF16) for _ in range(2)]
    b.tmp = [ar.alloc("ntmp", [128, 512], F32) for _ in range(2)]
    b.aT = ar.alloc("aT", [128, 8, T], BF16)
    b.wg = [ar.alloc("wg", [128, KC, 128], BF16) for _ in range(2)]
    b.wu = [ar.alloc("wu", [128, KC, 128], BF16) for _ in range(2)]
    b.wo = [ar.alloc("wo", [128, 8, 512], BF16) for _ in range(2)]
    b.sgt = [ar.alloc("sgt", [128, 512], F32) for _ in range(2)]
    return b


def compute_mods(g, ar, w_mod):
    P = g.P
    m = ar.mark()
    scb = ar.alloc("scb", [128, KC, 2], BF16)
    wm = [ar.alloc("wm", [128, KC, 512], BF16) for _ in range(2)]
    P.add(ACT, lambda e: e.activation(out=scb[:].rearrange("p c w -> p (c w)"), in_=g.vec[:, V_C:V_C + 32], func=AF.Silu),
          reads=["vec"], writes=["scb"])
    psm = g.ps[0]
    for blk in range(36):
        b = blk % 2
        load_w(g, POOL, wm[b][:], w_mod[:, blk * 512:(blk + 1) * 512].rearrange("(c p) n -> p c n", p=128), ("wm", b))
        for jj in range(4):
            j = blk * 4 + jj
            for k in range(KC):
                P.add(PE, lambda e, b=b, jj=jj, j=j, k=k: e.matmul(psm[:, 2 * j:2 * j + 2], lhsT=wm[b][:, k, jj * 128:(jj + 1) * 128], rhs=scb[:, k, :], start=(k == 0), stop=(k == KC - 1)),
                      reads=[("wm", b), "scb"], writes=[("ps", 0)])
    P.add(DVE, lambda e: e.tensor_tensor(out=g.mods[:], in0=psm[:, 0:288].rearrange("p (j w) -> p j w", w=2),
                                         in1=g.vec[:, V_BM:V_BM + 144].rearrange("p (j o) -> p j o", o=1).to_broadcast([128, 144, 2]), op=ALU.add),
          reads=["vec"], writes=[("ps", 0), "mods"])
    derive_mods(g)
    pass
    ar.release(m)


def rope_apply(g, src_bf, np_, rmat, cos, sin, dst, ps_i, t0, n, tmp1, tmp2, rk, wk):
    P = g.P
    pr = g.ps[ps_i]
    P.add(PE, lambda e: e.matmul(pr[0:np_, 0:n], lhsT=rmat, rhs=src_bf[0:np_, t0:t0 + n], start=True, stop=True),
          reads=rk + ["rhb", "rrb"], writes=[("ps", ps_i)])
    P.add(DVE, lambda e: e.tensor_tensor(out=tmp1[0:np_, 0:n], in0=pr[0:np_, 0:n], in1=sin[0:np_, t0:t0 + n], op=ALU.mult),
          reads=["rope"], writes=[("ps", ps_i), "rt1"])
    P.add(POOL, lambda e: e.tensor_tensor(out=tmp2[0:np_, 0:n], in0=src_bf[0:np_, t0:t0 + n], in1=cos[0:np_, t0:t0 + n], op=ALU.mult),
          reads=rk + ["rope"], writes=["rt2"])
    P.add(DVE, lambda e: e.tensor_tensor(out=dst[0:np_, t0:t0 + n], in0=tmp1[0:np_, 0:n], in1=tmp2[0:np_, 0:n], op=ALU.add),
          reads=["rt1", "rt2"], writes=wk)


STOP_AFTER = None


def build_A(l, last, F=None, half=0):
    if F is None:
        nc = bass.Bass("TRN2", target_bir_lowering=False)
        dt = lambda n, s, d=F32, k="ExternalInput": nc.dram_tensor(n, s, d, kind=k).ap()
    else:
        nc = F.nc
        dt = lambda n, s, d=F32, k="ExternalInput": F.tensor(n, s, d, k, l, half)
    XT = dt("xT", [D, T]); VEC = dt("vec", [128, NVEC]); CST = dt("cst", [128, NCST]); ROPE = dt("rope", [128, 4, TL])
    WMOD = dt("w_mod", [D, NMOD * D]); W1I = dt("w1i", [D, 2 * FH]); W1O = dt("w1o", [FH, D]); WIN = dt("w_in", [D, 4928])
    WUQ = dt("w_uq", [512, 768]); WUKV = dt("w_ukv", [256, 1024])
    o = "ExternalOutput"
    XO = dt("xo", [D, T], F32, o); MODS = dt("mods", [128, 288], F32, o)
    GQ = dt("gq", [128, 8, T], BF16, o); GK = dt("gk", [128, 2, T], BF16, o); GV = dt("gv", [T, 256], BF16, o)
    MQN = dt("mqn", [128, 4, T], BF16, o); MQR = dt("mqr", [64, 4, T], BF16, o); MKN = dt("mkn", [128, 4, T], BF16, o)
    MKR = dt("mkr", [64, T], BF16, o); MV = dt("mv", [T, 512], BF16, o)
    HQE = dt("hqe", [128, 2, 4, T], BF16, o); HKT = dt("hkt", [128, 2, 4, T], BF16, o); HKE = dt("hke", [T, 2, 4, 128], BF16, o)
    HV = dt("hv", [T, 512], BF16, o); HSC = dt("hsc", [128, 2, 3, 4, NCH], F32, o); HG = dt("hg", [128, 4, T], F32, o)

    if F is None:
        P = Prog(nc)
        ar = Arena(nc)
        g = setup_common(nc, P, ar, CST, VEC)
    else:
        P, ar, g = F.P, F.ar, F.g
    fin = []
    hT = ar.alloc("hT", [128, KC, T], BF16)
    m0 = ar.mark()
    xT = ar.alloc("xT", [128, KC, T], F32)
    for c in range(KC):
        P.add(SP, lambda e, c=c: e.dma_start(out=xT[:, c, :], in_=XT[c * 128:(c + 1) * 128, :]),
              writes=[("x", c, 0), ("x", c, 1), ("x", c, 2)], dma=True)
    if F is None:
        compute_mods(g, ar, WMOD)
        fin.append(P.add(SP, lambda e: e.dma_start(out=MODS, in_=g.mods[:].rearrange("p j w -> p (j w)")), reads=["mods"], dma=True))
    if STOP_AFTER == "mods":
        P.emit(final_wait_ops=fin)
        return nc
    fb = alloc_ffn_bufs(ar)
    norm_mod(g, xT, hT, 0, 1, fb)
    ffn(g, xT, hT, W1I, W1O, 2, fb)
    for c in range(KC):
        fin.append(P.add(SP, lambda e, c=c: e.dma_start(out=XO[c * 128:(c + 1) * 128, :], in_=xT[:, c, :]),
                         reads=[("x", c, 0), ("x", c, 1), ("x", c, 2)], dma=True))
    norm_mod(g, xT, hT, 3, 4, fb)
    pass
    ar.release(m0)

    rope = ar.alloc("rope", [128, 4, TL], F32)
    P.add(SP, lambda e: e.dma_start(out=rope[:], in_=ROPE), writes=["rope"], dma=True)
    cosH, sinH, cosR, sinR = rope[:, 0, :], rope[:, 1, :], rope[:, 2, :], rope[:, 3, :]
    wc = [ar.alloc("wc", [128, KC, 128], BF16) for _ in range(3)]
    rstd = ar.alloc("rstd2", [128, T], F32)
    rtmp = ar.alloc("rtmp2", [128, 512], F32)
    sq = [ar.alloc("sq2", [128, 512], BF16) for _ in range(2)]
    NS = 10
    stg = [ar.alloc("stg", [128, T], F32) for _ in range(NS)]
    sb16 = [ar.alloc("sb16", [128, T], BF16) for _ in range(6)]
    rt1 = ar.alloc("rt1", [128, 512], F32)
    rt2 = ar.alloc("rt2", [128, 512], F32)
    state = {"nw": 0, "nps": 0, "ndma": 0}

    def proj_fm(col0, ncols, sink):
        b = state["nw"] % 3
        state["nw"] += 1
        load_w(g, POOL, wc[b][:, :, 0:ncols], WIN[:, col0:col0 + ncols].rearrange("(c p) n -> p c n", p=128), ("wc", b))
        for ti, (t0, n) in enumerate(TILES):
            pi = 1 + state["nps"] % 4
            state["nps"] += 1
            pp = g.ps[pi]
            for c in range(KC):
                P.add(PE, lambda e, c=c, pp=pp, b=b: e.matmul(pp[0:ncols, 0:n], lhsT=wc[b][:, c, 0:ncols], rhs=hT[:, c, t0:t0 + n], start=(c == 0), stop=(c == KC - 1)),
                      reads=[("wc", b)] + ([("h", cc, ti) for cc in range(KC)] if c == 0 else []), writes=[("ps", pi)])
            sink(ti, t0, n, pp, pi)

    def evac_to(dst, key, np_=128, eng=ACT, func=None):
        def sink(ti, t0, n, pp, pi):
            if func is not None:
                P.add(ACT, lambda e: e.activation(out=dst[0:np_, t0:t0 + n], in_=pp[0:np_, 0:n], func=func), writes=[("ps", pi), (key, ti)])
            elif eng == ACT:
                P.add(ACT, lambda e: e.copy(out=dst[0:np_, t0:t0 + n], in_=pp[0:np_, 0:n]), writes=[("ps", pi), (key, ti)])
            else:
                P.add(DVE, lambda e: e.tensor_copy(out=dst[0:np_, t0:t0 + n], in_=pp[0:np_, 0:n]), writes=[("ps", pi), (key, ti)])
        return sink

    def k3(key):
        return [(key, 0), (key, 1), (key, 2)]

    def dma_out(dst, src, rk):
        q = (SP, ACT)[state["ndma"] % 1]
        state["ndma"] += 1
        fin.append(P.add(q, lambda e: e.dma_start(out=dst, in_=src), reads=rk, dma=True))

    for hh in range(10):
        s_raw = stg[hh % 2]
        kraw = f"raw{hh % 2}"
        proj_fm(hh * 128, 128, evac_to(s_raw, kraw))
        gcol = V_GQ if hh < 8 else V_GK
        qn = sb16[hh % 2]
        kqn = f"qn{hh % 2}"
        for ti, (t0, n) in enumerate(TILES):
            col_rstd(g, [(s_raw[:, t0:t0 + n], [(kraw, ti)])], 128, rstd[:, t0:t0 + n], t0, n, sq, None, ("rstd", ti), rtmp)
            P.add(DVE, lambda e, t0=t0, n=n: e.scalar_tensor_tensor(out=qn[:, t0:t0 + n], in0=s_raw[:, t0:t0 + n], scalar=g.vec[:, gcol:gcol + 1], in1=rstd[:, t0:t0 + n], op0=ALU.mult, op1=ALU.mult),
                  reads=[(kraw, ti), ("rstd", ti), "vec"], writes=[(kqn, ti)])
        ob = sb16[2 + hh % 2]
        kob = f"ob{hh % 2}"
        for ti, (t0, n) in enumerate(TILES[:2]):
            rope_apply(g, qn, 128, g.rhb[:], cosH, sinH, ob, 5 + ti % 2, t0, n, rt1, rt2, [(kqn, ti)], [(kob, ti)])
        P.add(POOL, lambda e: e.tensor_copy(out=ob[:, TL:T], in_=qn[:, TL:T]), reads=[(kqn, 2)], writes=[(kob, 2)])
        dma_out(GQ[:, hh, :] if hh < 8 else GK[:, hh - 8, :], ob[:], k3(kob))

    lbv = ar.alloc("lbv", [128, 2, 4], F32)
    oml = ar.alloc("oml", [128, 2, 4], F32)
    lg = g.vec[:, V_LB:V_LB + 16].rearrange("p (d l h) -> p d l h", d=2, l=2)
    if l == 0:
        P.add(DVE, lambda e: e.memset(lbv[:], 0.0), writes=["lbv"])
    else:
        P.add(DVE, lambda e: e.tensor_tensor(out=lbv[:], in0=lg[:, :, 1, :], in1=lg[:, :, 0, :], op=ALU.subtract), reads=["vec"], writes=["lbv"])
        P.add(ACT, lambda e: e.activation(out=lbv[:], in_=lbv[:], func=AF.Sigmoid), writes=["lbv"])
    P.add(DVE, lambda e: e.tensor_scalar(out=oml[:], in0=lbv[:], scalar1=-1.0, scalar2=1.0, op0=ALU.mult, op1=ALU.add), reads=["lbv"], writes=["oml"])
    hsc = ar.alloc("hsc", [128, 2, 3, 4, NCH], F32)
    smask = g.cst[:, C_SM:C_SM + T]
    ket = [ar.alloc("ket", [128, 10, 128], BF16) for _ in range(2)]
    nket = 0
    for h in range(4):
        sq_raw = stg[2]
        proj_fm(1536 + h * 128, 128, evac_to(sq_raw, "hq"))
        for d in range(2):
            z = stg[3]
            proj_fm((2560 if d == 0 else 3072) + h * 128, 128, evac_to(z, "z", func=AF.Sigmoid))
            f, lf, kk, cum, G = stg[4], stg[5], stg[6], stg[7], stg[8]
            A = lambda eng, fn, r, w: P.add(eng, fn, reads=r, writes=w)
            A(DVE, lambda e, d=d, h=h: e.tensor_scalar(out=f[:], in0=z[:], scalar1=oml[:, d, h:h + 1], scalar2=lbv[:, d, h:h + 1], op0=ALU.mult, op1=ALU.add),
              k3("z") + ["oml", "lbv"], ["f"])
            A(ACT, lambda e: e.activation(out=lf[:], in_=f[:], func=AF.Ln), ["f"], ["lf"])
            A(POOL, lambda e: e.tensor_scalar(out=kk[:], in0=f[:], scalar1=-1.0, scalar2=1.0, op0=ALU.mult, op1=ALU.add), ["f"], ["kk"])
            A(DVE, lambda e: e.tensor_tensor_scan(out=cum[:], data0=smask, data1=lf[:], initial=0.0, op0=ALU.mult, op1=ALU.add), ["lf", "cst"], ["cum"])
            cum3 = cum[:].rearrange("p (c t) -> p c t", t=CH)
            G3 = G[:].rearrange("p (c t) -> p c t", t=CH)
            if d == 0:
                Gs, G3s, kG = cum, cum3, "cum"
                last_ap = cum3[:, :, CH - 1:CH]
            else:
                A(POOL, lambda e: e.tensor_tensor(out=G[:], in0=lf[:], in1=cum[:], op=ALU.subtract), ["lf", "cum"], ["G"])
                A(DVE, lambda e: e.tensor_tensor(out=G3, in0=G3, in1=cum3[:, :, CH - 1:CH].to_broadcast([128, NCH, CH]), op=ALU.add), ["cum"], ["G"])
                Gs, G3s, kG = G, G3, "G"
                last_ap = G3[:, :, 0:1]
            mid_ap = G3s[:, :, CH // 2:CH // 2 + 1]
            sc0 = hsc[:, d, 0, h, :].rearrange("p (c o) -> p c o", o=1)
            sc1 = hsc[:, d, 1, h, :].rearrange("p (c o) -> p c o", o=1)
            sc2 = hsc[:, d, 2, h, :].rearrange("p (c o) -> p c o", o=1)
            A(ACT, lambda e, sc0=sc0, mid_ap=mid_ap: e.activation(out=sc0, in_=mid_ap, func=AF.Exp), [kG], [("hsc", d, 0, h)])
            A(ACT, lambda e, sc1=sc1, last_ap=last_ap: e.activation(out=sc1, in_=last_ap, func=AF.Exp), [kG], [("hsc", d, 1, h)])
            A(DVE, lambda e, sc2=sc2, last_ap=last_ap, mid_ap=mid_ap: e.tensor_tensor(out=sc2, in0=last_ap, in1=mid_ap, op=ALU.subtract), [kG], [("hsc", d, 2, h)])
            A(ACT, lambda e, sc2=sc2: e.activation(out=sc2, in_=sc2, func=AF.Exp), [], [("hsc", d, 2, h)])
            Gp = stg[9]
            Gp3 = Gp[:].rearrange("p (c t) -> p c t", t=CH)
            A(DVE, lambda e, G3s=G3s, mid_ap=mid_ap: e.tensor_tensor(out=Gp3, in0=G3s, in1=mid_ap.to_broadcast([128, NCH, CH]), op=ALU.subtract), [kG], ["Gp"])
            e1, e2 = stg[4], stg[5]
            A(ACT, lambda e: e.activation(out=e1[:], in_=Gp[:], func=AF.Exp), ["Gp"], ["f"])
            A(ACT, lambda e: e.activation(out=e2[:], in_=Gp[:], func=AF.Exp, scale=-1.0), ["Gp"], ["lf"])
            qe = sb16[0]
            keT = sb16[1]
            A(DVE, lambda e: e.scalar_tensor_tensor(out=qe[:], in0=sq_raw[:], scalar=float(128 ** -0.5), in1=e1[:], op0=ALU.mult, op1=ALU.mult),
              k3("hq") + ["f"], ["qe"])
            A(POOL, lambda e: e.tensor_tensor(out=keT[:], in0=kk[:], in1=e2[:], op=ALU.mult), ["kk", "lf"], ["keT"])
            dma_out(HQE[:, d, h, :], qe[:], ["qe"])
            dma_out(HKT[:, d, h, :], keT[:], ["keT"])
            kb = ket[nket % 2]
            kkb = ("ket", nket % 2)
            nket += 1
            for tt in range(10):
                P.add(PE, lambda e, tt=tt: e.transpose(out=g.psb[:, (tt % 8) * 128:(tt % 8 + 1) * 128], in_=keT[:, tt * 128:(tt + 1) * 128], identity=g.idb[:]),
                      reads=["keT", "idb"], writes=[("ps", 7)])
                P.add(DVE, lambda e, tt=tt, kb=kb: e.tensor_copy(out=kb[:, tt, :], in_=g.psb[:, (tt % 8) * 128:(tt % 8 + 1) * 128]),
                      writes=[("ps", 7), kkb])
            dma_out(HKE[:, d, h, :].rearrange("(tt p) k -> p tt k", p=128), kb[:], [kkb])
        og = stg[2 + 0]
    for h in range(4):
        gg = stg[h % 2]
        proj_fm(3584 + h * 128, 128, evac_to(gg, f"gg{h % 2}", func=AF.Silu))
        dma_out(HG[:, h, :], gg[:], k3(f"gg{h % 2}"))
    dma_out(HSC, hsc[:], [("hsc", d, i, h) for d in range(2) for i in range(3) for h in range(4)])

    wuq = ar.alloc("wuq", [128, 4, 768], BF16)
    wukv = ar.alloc("wukv", [128, 2, 1024], BF16)
    load_w(g, POOL, wuq[:], WUQ.rearrange("(c p) n -> p c n", p=128), "wuq")
    load_w(g, POOL, wukv[:], WUKV.rearrange("(c p) n -> p c n", p=128), "wukv")
    for nm, col0, nchk, gcol0, nfeat in (("cq", 4096, 4, V_MQG, 512), ("ckv", 4608, 2, V_MKG, 256)):
        raws = [stg[c] for c in range(nchk)]
        for c in range(nchk):
            proj_fm(col0 + c * 128, 128, evac_to(raws[c], f"{nm}{c}"))
        for ti, (t0, n) in enumerate(TILES):
            col_rstd(g, [(raws[c][:, t0:t0 + n], [(f"{nm}{c}", ti)]) for c in range(nchk)], nfeat, rstd[:, t0:t0 + n], t0, n, sq, None, ("rstd", ti), rtmp)
        cn = [sb16[c] for c in range(nchk)]
        for c in range(nchk):
            for ti, (t0, n) in enumerate(TILES):
                P.add(DVE, lambda e, c=c, t0=t0, n=n: e.scalar_tensor_tensor(out=cn[c][:, t0:t0 + n], in0=raws[c][:, t0:t0 + n], scalar=g.vec[:, gcol0 + c:gcol0 + c + 1], in1=rstd[:, t0:t0 + n], op0=ALU.mult, op1=ALU.mult),
                      reads=[(f"{nm}{c}", ti), ("rstd", ti), "vec"], writes=[(f"{nm}n{c}", ti)])
        if nm == "cq":
            for h in range(4):
                on = stg[4 + h % 2]
                onb = ket[h % 2][:].rearrange("p a b -> p (a b)")
                orb = ket[(h + 1) % 2][:].rearrange("p a b -> p (a b)")
                for ti, (t0, n) in enumerate(TILES):
                    pi = 1 + state["nps"] % 4
                    state["nps"] += 1
                    pp = g.ps[pi]
                    for c in range(4):
                        P.add(PE, lambda e, c=c, pp=pp, h=h: e.matmul(pp[:, 0:n], lhsT=wuq[:, c, h * 192:h * 192 + 128], rhs=cn[c][:, t0:t0 + n], start=(c == 0), stop=(c == 3)),
                              reads=["wuq"] + [(f"cqn{cc}", ti) for cc in range(4)], writes=[("ps", pi)])
                    P.add(ACT, lambda e, pp=pp, t0=t0, n=n, onb=onb: e.copy(out=onb[:, t0:t0 + n], in_=pp[:, 0:n]), writes=[("ps", pi), ("mqn_s", h % 2, ti)])
                dma_out(MQN[:, h, :], onb, [("mqn_s", h % 2, ti) for ti in range(3)])
                qr_raw = sb16[4]
                for ti, (t0, n) in enumerate(TILES):
                    pi = 1 + state["nps"] % 4
                    state["nps"] += 1
                    pp = g.ps[pi]
                    for c in range(4):
                        P.add(PE, lambda e, c=c, pp=pp, h=h: e.matmul(pp[0:64, 0:n], lhsT=wuq[:, c, h * 192 + 128:h * 192 + 192], rhs=cn[c][:, t0:t0 + n], start=(c == 0), stop=(c == 3)),
                              reads=["wuq"] + [(f"cqn{cc}", ti) for cc in range(4)], writes=[("ps", pi)])
                    P.add(ACT, lambda e, pp=pp, t0=t0, n=n: e.copy(out=qr_raw[0:64, t0:t0 + n], in_=pp[0:64, 0:n]), writes=[("ps", pi), ("qrraw", ti)])
                qro = sb16[5]
                for ti, (t0, n) in enumerate(TILES[:2]):
                    rope_apply(g, qr_raw, 64, g.rrb[0:64, :], cosR, sinR, qro, 5 + ti % 2, t0, n, rt1, rt2, [("qrraw", ti)], [("qro", ti)])
                P.add(POOL, lambda e: e.tensor_copy(out=qro[0:64, TL:T], in_=qr_raw[0:64, TL:T]), reads=[("qrraw", 2)], writes=[("qro", 2)])
                dma_out(MQR[:, h, :], qro[0:64, :], k3("qro"))
        else:
            for h in range(4):
                onb = ket[h % 2][:].rearrange("p a b -> p (a b)")
                for ti, (t0, n) in enumerate(TILES):
                    pi = 1 + state["nps"] % 4
                    state["nps"] += 1
                    pp = g.ps[pi]
                    for c in range(2):
                        P.add(PE, lambda e, c=c, pp=pp, h=h: e.matmul(pp[:, 0:n], lhsT=wukv[:, c, h * 256:h * 256 + 128], rhs=cn[c][:, t0:t0 + n], start=(c == 0), stop=(c == 1)),
                              reads=["wukv"] + [(f"ckvn{cc}", ti) for cc in range(2)], writes=[("ps", pi)])
                    P.add(ACT, lambda e, pp=pp, t0=t0, n=n, onb=onb: e.copy(out=onb[:, t0:t0 + n], in_=pp[:, 0:n]), writes=[("ps", pi), ("mkn_s", h % 2, ti)])
                dma_out(MKN[:, h, :], onb, [("mkn_s", h % 2, ti) for ti in range(3)])
            vst = [ar.alloc("vst", [128, 512], BF16) for _ in range(2)]
            wv4 = wukv[:].rearrange("p c (h x) -> p c h x", x=256)
            for tt in range(10):
                pi = 1 + state["nps"] % 4
                state["nps"] += 1
                pp = g.ps[pi]
                for h in range(4):
                    for c in range(2):
                        P.add(PE, lambda e, c=c, pp=pp, tt=tt, h=h: e.matmul(pp[:, h * 128:(h + 1) * 128], lhsT=cn[c][:, tt * 128:(tt + 1) * 128], rhs=wv4[:, c, h, 128:256], start=(c == 0), stop=(c == 1)),
                              reads=["wukv"] + [(f"ckvn{cc}", ti) for cc in range(2) for ti in range(3)], writes=[("ps", pi)])
                vb = vst[tt % 2]
                P.add(ACT, lambda e, pp=pp, vb=vb: e.copy(out=vb[:], in_=pp[:, 0:512]), writes=[("ps", pi), ("vst", tt % 2)])
                dma_out(MV[tt * 128:(tt + 1) * 128, :], vb[:], [("vst", tt % 2)])
    kr_raw = sb16[2]
    proj_fm(4864, 64, evac_to(kr_raw, "krraw", np_=64))
    kro = sb16[3]
    for ti, (t0, n) in enumerate(TILES[:2]):
        rope_apply(g, kr_raw, 64, g.rrb[0:64, :], cosR, sinR, kro, 5 + ti % 2, t0, n, rt1, rt2, [("krraw", ti)], [("kro", ti)])
    P.add(POOL, lambda e: e.tensor_copy(out=kro[0:64, TL:T], in_=kr_raw[0:64, TL:T]), reads=[("krraw", 2)], writes=[("kro", 2)])
    dma_out(MKR, kro[0:64, :], k3("kro"))

    wt = ar.alloc("wt", [128, KC, 512], BF16)
    tst = [ar.alloc("tst", [128, 512], BF16) for _ in range(2)]
    for col0, ncols, DST in ((1280, 256, GV), (2048, 512, HV)):
        load_w(g, POOL, wt[:, :, 0:ncols], WIN[:, col0:col0 + ncols].rearrange("(c p) n -> p c n", p=128), "wt")
        for tt in range(10):
            ti = 0 if tt < 4 else (1 if tt < 8 else 2)
            pi = 1 + state["nps"] % 4
            state["nps"] += 1
            pp = g.ps[pi]
            for c in range(KC):
                P.add(PE, lambda e, c=c, pp=pp, tt=tt: e.matmul(pp[:, 0:ncols], lhsT=hT[:, c, tt * 128:(tt + 1) * 128], rhs=wt[:, c, 0:ncols], start=(c == 0), stop=(c == KC - 1)),
                      reads=["wt"] + ([("h", cc, ti) for cc in range(KC)] if c == 0 else []), writes=[("ps", pi)])
            vb = tst[tt % 2]
            P.add(ACT, lambda e, pp=pp, vb=vb: e.copy(out=vb[:, 0:ncols], in_=pp[:, 0:ncols]), writes=[("ps", pi), ("tst", tt % 2)])
            dma_out(DST[tt * 128:(tt + 1) * 128, :], vb[:, 0:ncols], [("tst", tt % 2)])
    if F is not None:
        return fin
    P.emit(final_wait_ops=fin)
    return nc


DEBUG_MIX = False
NKEY = SEQ + TCX
NKC = NKEY // 128


def build_B(l, last, F=None, half=0):
    if F is None:
        nc = bass.Bass("TRN2", target_bir_lowering=False)
        dt = lambda n, s, d=F32, k="ExternalInput": nc.dram_tensor(n, s, d, kind=k).ap()
    else:
        nc = F.nc
        dt = lambda n, s, d=F32, k="ExternalInput": F.tensor_b(n, s, d, k, l, half)
    XT = dt("xT", [D, T]); VEC = dt("vec", [128, NVEC]); CST = dt("cst", [128, NCST]); MODS = dt("mods", [128, 288])
    GQ = dt("gq", [128, 8, T], BF16); GKA = dt("gk", [128, 2, NKEY], BF16); GVA = dt("gv", [NKEY, 256], BF16)
    MQN = dt("mqn", [128, 4, T], BF16); MQR = dt("mqr", [64, 4, T], BF16); MKN = dt("mkn", [128, 4, NKEY], BF16)
    MKR = dt("mkr", [64, NKEY], BF16); MVA = dt("mv", [NKEY, 512], BF16)
    HQE = dt("hqe", [128, 2, 4, T], BF16); HKT = dt("hkt", [128, 2, 4, T], BF16); HKE = dt("hke", [T, 2, 4, 128], BF16)
    HV = dt("hv", [T, 512], BF16); HSC = dt("hsc", [128, 2, 3, 4, NCH], F32); HG = dt("hg", [128, 4, T], F32)
    PKE = dt("pke", [TL, 2, 4, 128], BF16); PV = dt("pv", [TL, 512], BF16); PSC = dt("psc", [128, 2, 3, 4, 64], F32)
    WOUT = dt("w_out", [D, D]); W2I = dt("w2i", [D, 2 * FH]); W2O = dt("w2o", [FH, D])
    OUT = dt("out", [D, TL if last else T], F32, "ExternalOutput")
    MIXD = dt("mixdbg", [128, KC, T], BF16, "ExternalOutput") if DEBUG_MIX else None

    if F is None:
        P = Prog(nc)
        ar = Arena(nc)
        g = setup_common(nc, P, ar, CST, VEC)
        P.add(SP, lambda e: e.dma_start(out=g.mods[:].rearrange("p j w -> p (j w)"), in_=MODS), writes=["mods"], dma=True)
        derive_mods(g)
    else:
        P, ar, g = F.P, F.ar, F.g
    ps = g.ps
    fin = []
    pdirs = (0, 1) if F is None else ((1,) if half == 0 else (0,))

    SEGS = ((0, 0, 0, TL), (TL, 1, 0, TL), (2 * TL, 0, TL, TCX))

    def ld_feat(dst_tile, np_, name, unf_ap, idx):
        if F is None:
            P.add(SP, lambda e: e.dma_start(out=dst_tile[0:np_, :], in_=unf_ap), dma=True)
            return
        for d0, hf, s0, n in SEGS:
            src = F.mid[(name, l, hf)]
            sap = src[:, idx, s0:s0 + n] if idx is not None else src[:, s0:s0 + n]
            P.add(SP, lambda e, sap=sap, d0=d0, n=n: e.dma_start(out=dst_tile[0:np_, d0:d0 + n], in_=sap), dma=True)

    def ld_tok(dst_tile, name, unf_t, c0, c1):
        if F is None:
            P.add(SP, lambda e: e.dma_start(out=dst_tile[:], in_=unf_t[:, c0:c1].rearrange("(c p) n -> p c n", p=128)), dma=True)
            return
        for d0, hf, s0, n in SEGS:
            sap = F.mid[(name, l, hf)][s0:s0 + n, c0:c1].rearrange("(c p) n -> p c n", p=128)
            P.add(SP, lambda e, sap=sap, d0=d0, n=n: e.dma_start(out=dst_tile[:, d0 // 128:(d0 + n) // 128, :], in_=sap), dma=True)

    qtiles = TILES[:2] if last else TILES
    mixT = ar.alloc("mixT", [128, KC, T], BF16)
    m0 = ar.mark()

    oacc = ar.alloc("oacc", [128, 4, T], F32)
    m_scan = ar.mark()
    hqe = ar.alloc("hqe", [128, 4, T], BF16)
    hkt = ar.alloc("hkt", [128, 4, T], BF16)
    hke = ar.alloc("hke", [64, 20, 4, 128], BF16)
    hv = ar.alloc("hv", [64, 20, 512], BF16)
    pke = ar.alloc("pke", [64, 16, 4, 128], BF16)
    pv = ar.alloc("pv", [64, 16, 512], BF16)
    hsc = ar.alloc("hsc", [128, 2, 3, 4, NCH], F32)
    psc = ar.alloc("psc", [128, 2, 3, 4, 64], F32)
    S = ar.alloc("S", [128, 4, 128], F32)
    Sbs = [ar.alloc("Sb", [128, 4, 128], BF16) for _ in range(2)]
    stmp = ar.alloc("stmp", [128, 4, 128], F32)
    sc = ar.alloc("sc", [64, 4, 64], BF16)
    keMs = [ar.alloc("keM", [64, 4, 4, 128], BF16) for _ in range(2)]
    P.add(SP, lambda e: e.dma_start(out=hv[:], in_=HV.rearrange("(c p) n -> p c n", p=64)), dma=True)
    PVs = PV if F is None else F.mid[("hv", l, 1 - half)][0:TL, :]
    PSCs = PSC if F is None else F.mid[("hsc", l, 1 - half)][:, :, :, :, 0:64]
    P.add(SP, lambda e: e.dma_start(out=pv[:], in_=PVs.rearrange("(c p) n -> p c n", p=64)), dma=True)
    P.add(SP, lambda e: e.dma_start(out=hsc[:], in_=HSC), dma=True)
    P.add(SP, lambda e: e.dma_start(out=psc[:], in_=PSCs), dma=True)
    P.add(POOL, lambda e: e.memset(oacc[:], 0.0))
    bc = lambda ap: ap.rearrange("p (h o) -> p h o", o=1).to_broadcast([128, 4, 128])
    nstep = 0
    for d in range(2):
        for h in range(4):
            P.add(SP, lambda e, h=h: e.dma_start(out=hqe[:, h, :], in_=HQE[:, d, h, :]), dma=True)
            P.add(SP, lambda e, h=h: e.dma_start(out=hkt[:, h, :], in_=HKT[:, d, h, :]), dma=True)
        P.add(SP, lambda e: e.dma_start(out=hke[:], in_=HKE[:, d, :, :].rearrange("(c p) h k -> p c h k", p=64)), dma=True)
        PKEs = PKE[:, d, :, :] if F is None else F.mid[("hke", l, 1 - half)][0:TL, d, :, :]
        if d in pdirs:
            P.add(SP, lambda e: e.dma_start(out=pke[:], in_=PKEs.rearrange("(c p) h k -> p c h k", p=64)), dma=True)
        P.add(DVE, lambda e: e.memset(S[:], 0.0))
        mcol = C_MF if d == 0 else C_MB
        mask = g.cst[0:64, mcol:mcol + 64]
        ctx_t = [16, 17, 18, 19]
        lat_t = list(range(16))
        jord = [0, 1, 2, 3]
        if d == 1:
            ctx_t, lat_t, jord = ctx_t[::-1], lat_t[::-1], jord[::-1]
        steps = [("own", c) for c in ctx_t] + ([("par", c) for c in lat_t] if d in pdirs else []) + [("own", c) for c in lat_t]
        for kind, tt in steps:
            KE, V, SCL = (hke, hv, hsc) if kind == "own" else (pke, pv, psc)
            t0 = tt * 64
            want_out = kind == "own" and not (last and tt >= 16)
            keM = keMs[nstep % 2]
            nstep += 1
            pa, po = ps[1], ps[2]
            for j in range(4):
                P.add(POOL, lambda e, j=j: e.tensor_scalar(out=keM[:, j, :, :], in0=KE[0:64, tt, :, :], scalar1=g.cst[0:64, C_RM + j:C_RM + j + 1], scalar2=None, op0=ALU.mult))
            if want_out:
                for h in range(4):
                    P.add(PE, lambda e, h=h: e.matmul(pa[0:64, h * 64:(h + 1) * 64], lhsT=hkt[:, h, t0:t0 + 64], rhs=hqe[:, h, t0:t0 + 64], start=True, stop=True))
                P.add(DVE, lambda e: e.tensor_tensor(out=sc[:], in0=pa[0:64, 0:256].rearrange("p (h t) -> p h t", h=4),
                                                     in1=mask.rearrange("p (o t) -> p o t", o=1).to_broadcast([64, 4, 64]), op=ALU.mult))
                for h in range(4):
                    P.add(PE, lambda e, h=h: e.matmul(po[:, h * 64:(h + 1) * 64], lhsT=V[0:64, tt, h * 128:(h + 1) * 128], rhs=sc[0:64, h, :], start=(h == 0), stop=False, skip_group_check=True))
            for ji, j in enumerate(jord):
                sci = tt * 4 + j
                if want_out:
                    Sb = Sbs[(nstep * 4 + ji) % 2]
                    P.add(DVE, lambda e, Sb=Sb: e.tensor_tensor(out=Sb[:], in0=S[:], in1=bc(SCL[:, d, 0, :, sci]), op=ALU.mult))
                    for h in range(4):
                        P.add(PE, lambda e, h=h, Sb=Sb: e.matmul(po[:, h * 64 + j * CH:h * 64 + (j + 1) * CH], lhsT=Sb[:, h, :], rhs=hqe[:, h, t0 + j * CH:t0 + (j + 1) * CH],
                                                                 start=False, stop=(ji == 3 and h == 3), skip_group_check=True))
                pS = ps[3 + ji % 2]
                for h in range(4):
                    P.add(PE, lambda e, h=h: e.matmul(pS[:, h * 128:(h + 1) * 128], lhsT=keM[0:64, j, h, :], rhs=V[0:64, tt, h * 128:(h + 1) * 128], start=True, stop=True))
                P.add(POOL, lambda e: e.tensor_tensor(out=S[:], in0=S[:], in1=bc(SCL[:, d, 1, :, sci]), op=ALU.mult))
                P.add(DVE, lambda e: e.tensor_tensor(out=stmp[:], in0=pS[:].rearrange("p (h v) -> p h v", h=4), in1=bc(SCL[:, d, 2, :, sci]), op=ALU.mult))
                P.add(POOL, lambda e: e.tensor_tensor(out=S[:], in0=S[:], in1=stmp[:], op=ALU.add))
            if want_out:
                P.add(DVE, lambda e: e.tensor_tensor(out=oacc[:, :, t0:t0 + 64], in0=po[:, 0:256].rearrange("p (h t) -> p h t", h=4), in1=oacc[:, :, t0:t0 + 64], op=ALU.add))
    ar.release(m_scan)
    hgt = ar.alloc("hgt", [128, T], F32)
    rstd = ar.alloc("rstd", [128, T], F32)
    rtmp = ar.alloc("rtmp", [128, 512], F32)
    sq = [ar.alloc("sq", [128, 512], BF16) for _ in range(2)]
    ytmp = ar.alloc("ytmp", [128, 512], F32)
    okeys = []
    for h in range(4):
        P.add(SP, lambda e, h=h: e.dma_start(out=hgt[:], in_=HG[:, h, :]), writes=["hgt"], dma=True)
        for ti, (t0, n) in enumerate(qtiles):
            col_rstd(g, [(oacc[:, h, t0:t0 + n], okeys)], 128, rstd[:, t0:t0 + n], t0, n, sq, None, ("rstd", ti), rtmp)
            P.add(DVE, lambda e, h=h: e.scalar_tensor_tensor(out=ytmp[:, 0:n], in0=oacc[:, h, t0:t0 + n], scalar=g.vec[:, V_HNG:V_HNG + 1], in1=rstd[:, t0:t0 + n], op0=ALU.mult, op1=ALU.mult),
                  reads=okeys + [("rstd", ti), "vec"], writes=["ytmp"])
            P.add(POOL, lambda e, h=h: e.tensor_tensor(out=mixT[:, 8 + h, t0:t0 + n], in0=ytmp[:, 0:n], in1=hgt[:, t0:t0 + n], op=ALU.mult),
                  reads=["ytmp", "hgt"], writes=[("mix", 8 + h, ti)])
    pass
    ar.release(m0)

    qh = [ar.alloc("qh", [128, T], BF16) for _ in range(2)]
    qr = [ar.alloc("qr", [64, T], BF16) for _ in range(2)]
    kT = [ar.alloc("kT", [128, NKEY], BF16) for _ in range(2)]
    kr = ar.alloc("kr", [64, NKEY], BF16)
    vv = [ar.alloc("vv", [128, NKC, 128], BF16) for _ in range(2)]
    pT = [ar.alloc("pT", [128, 512], BF16) for _ in range(2)]
    rec = ar.alloc("rec", [128, 512], F32)
    sqb = [ar.alloc("sqa", [128, 512], BF16) for _ in range(2)]
    mx = ar.alloc("mx", [128, 16], F32)
    bias = ar.alloc("bias", [128, 2], F32)
    ld_feat(kr, 64, "mkr", MKR, None)
    cnt = {"s": 0, "o": 0, "sq": 0}

    def max_sq(srcs, ntok, dst_col, tiles):
        cols = []
        for ti, (t0, n) in enumerate(tiles):
            for i, (ap, np_, rk) in enumerate(srcs):
                b = cnt["sq"] % 2
                cnt["sq"] += 1
                P.add(ACT, lambda e, ap=ap, b=b, np_=np_: e.activation(out=sqb[b][0:np_, 0:n], in_=ap[0:np_, t0:t0 + n], func=AF.Square), reads=rk, writes=[("sqa", b)])
                P.add(PE, lambda e, b=b, i=i, np_=np_: e.matmul(ps[0][:, 0:n], lhsT=g.oneb[0:np_, :], rhs=sqb[b][0:np_, 0:n], start=(i == 0), stop=(i == len(srcs) - 1)),
                      reads=[("sqa", b), "oneb"], writes=[("ps", 0)])
            P.add(DVE, lambda e, ti=ti: e.reduce_max(out=mx[:, 8 + ti:9 + ti], in_=ps[0][:, 0:n], axis=AX.X), writes=[("ps", 0), ("mxp", ti)])
            cols.append(ti)
        P.add(DVE, lambda e: e.reduce_max(out=mx[:, dst_col:dst_col + 1], in_=mx[:, 8:8 + len(cols)], axis=AX.X),
              reads=[("mxp", ti) for ti in cols], writes=[("mx", dst_col)])

    KT5 = ((0, 512), (512, 512), (1024, 512), (1536, 512), (2048, 256))
    heads = [("g", h) for h in range(8)] + [("m", h) for h in range(4)]
    for hi_, (kind, h) in enumerate(heads):
        b = hi_ % 2
        if kind == "g":
            kv = h // 4
            scale = 128 ** -0.5
            mixi = h
            P.add(SP, lambda e: e.dma_start(out=qh[b][:], in_=GQ[:, h, :]), writes=[("qh", b)], dma=True)
            if h % 4 == 0:
                kb = kv % 2
                ld_feat(kT[kb], 128, "gk", GKA[:, kv, :] if F is None else None, kv)
                ld_tok(vv[kb], "gv", GVA, kv * 128, (kv + 1) * 128)
                max_sq([(kT[kb], 128, [("kT", kb)])], NKEY, 1, KT5)
            qs = [(qh[b], 128, [("qh", b)])]
        else:
            scale = 192 ** -0.5
            mixi = 12 + h
            kb = h % 2
            P.add(SP, lambda e: e.dma_start(out=qh[b][:], in_=MQN[:, h, :]), writes=[("qh", b)], dma=True)
            P.add(SP, lambda e: e.dma_start(out=qr[b][:], in_=MQR[:, h, :]), writes=[("qr", b)], dma=True)
            ld_feat(kT[kb], 128, "mkn", MKN[:, h, :] if F is None else None, h)
            ld_tok(vv[kb], "mv", MVA, h * 128, (h + 1) * 128)
            max_sq([(kT[kb], 128, [("kT", kb)]), (kr, 64, ["kr"])], NKEY, 1, KT5)
            qs = [(qh[b], 128, [("qh", b)]), (qr[b], 64, [("qr", b)])]
        max_sq(qs, T, 0, TILES)
        P.add(DVE, lambda e: e.tensor_tensor(out=mx[:, 2:3], in0=mx[:, 0:1], in1=mx[:, 1:2], op=ALU.mult), reads=[("mx", 0), ("mx", 1)], writes=[("mx", 2)])
        P.add(ACT, lambda e: e.activation(out=mx[:, 3:4], in_=mx[:, 2:3], func=AF.Sqrt, scale=float(scale * scale)), reads=[("mx", 2)], writes=[("mx", 3)])
        bcol = hi_ % 2
        P.add(DVE, lambda e: e.tensor_scalar_mul(out=bias[:, bcol:bcol + 1], in0=mx[:, 3:4], scalar1=-1.0), reads=[("mx", 3)], writes=[("bias", bcol)])
        for ti, (t0, n) in enumerate(qtiles):
            chunks = list(range(NKC)) if t0 < TL else [16, 17]
            oi = cnt["o"] % 2
            cnt["o"] += 1
            po, psm = ps[3 + oi], ps[5 + oi]
            for ci, c in enumerate(chunks):
                si = 1 + cnt["s"] % 2
                pb = cnt["s"] % 2
                cnt["s"] += 1
                pss = ps[si]
                if kind == "g":
                    P.add(PE, lambda e: e.matmul(pss[:, 0:n], lhsT=kT[kb][:, c * 128:(c + 1) * 128], rhs=qh[b][:, t0:t0 + n], start=True, stop=True),
                          reads=[("kT", kb), ("qh", b)], writes=[("ps", si)])
                else:
                    P.add(PE, lambda e: e.matmul(pss[:, 0:n], lhsT=kT[kb][:, c * 128:(c + 1) * 128], rhs=qh[b][:, t0:t0 + n], start=True, stop=False),
                          reads=[("kT", kb), ("qh", b)], writes=[("ps", si)])
                    P.add(PE, lambda e: e.matmul(pss[:, 0:n], lhsT=kr[0:64, c * 128:(c + 1) * 128], rhs=qr[b][0:64, t0:t0 + n], start=False, stop=True),
                          reads=["kr", ("qr", b)], writes=[("ps", si)])
                P.add(ACT, lambda e: e.activation(out=pT[pb][:, 0:n], in_=pss[:, 0:n], func=AF.Exp, bias=bias[:, bcol:bcol + 1], scale=float(scale)),
                      reads=[("bias", bcol)], writes=[("ps", si), ("pT", pb)])
                P.add(PE, lambda e: e.matmul(po[:, 0:n], lhsT=vv[kb][:, c, :], rhs=pT[pb][:, 0:n], start=(ci == 0), stop=(ci == len(chunks) - 1)),
                      reads=[("vv", kb), ("pT", pb)], writes=[("ps", 3 + oi)])
                P.add(PE, lambda e: e.matmul(psm[:, 0:n], lhsT=g.oneb[:], rhs=pT[pb][:, 0:n], start=(ci == 0), stop=(ci == len(chunks) - 1)),
                      reads=["oneb", ("pT", pb)], writes=[("ps", 5 + oi)])
            P.add(DVE, lambda e: e.reciprocal(out=rec[:, 0:n], in_=psm[:, 0:n]), writes=[("ps", 5 + oi), "rec"])
            P.add(DVE, lambda e: e.tensor_tensor(out=mixT[:, mixi, t0:t0 + n], in0=po[:, 0:n], in1=rec[:, 0:n], op=ALU.mult),
                  reads=["rec"], writes=[("ps", 3 + oi), ("mix", mixi, ti)])
    pass
    ar.release(m0)

    if DEBUG_MIX:
        nm = TL if last else T
        fin.append(P.add(SP, lambda e: e.dma_start(out=MIXD[:, :, 0:nm], in_=mixT[:, :, 0:nm]), dma=True))
        pass
    xT = ar.alloc("xT", [128, KC, T], F32)
    for c in range(KC):
        P.add(SP, lambda e, c=c: e.dma_start(out=xT[:, c, :], in_=XT[c * 128:(c + 1) * 128, :]),
              writes=[("x", c, 0), ("x", c, 1), ("x", c, 2)], dma=True)
    fb = alloc_ffn_bufs(ar)
    nps = 0
    for ch in range(KC):
        b = ch % 2
        load_w(g, POOL, fb.wg[b][:], WOUT[:, ch * 128:(ch + 1) * 128].rearrange("(c p) n -> p c n", p=128), ("wg", b))
        for ti, (t0, n) in enumerate(qtiles):
            which = 0 if t0 < TL else 1
            q = nps % 2
            nps += 1
            pp = ps[5 + q]
            for c in range(KC):
                P.add(PE, lambda e, c=c: e.matmul(pp[:, 0:n], lhsT=fb.wg[b][:, c, :], rhs=mixT[:, c, t0:t0 + n], start=(c == 0), stop=(c == KC - 1)),
                      reads=[("wg", b)], writes=[("ps", 5 + q)])
            P.add(DVE, lambda e: e.scalar_tensor_tensor(out=xT[:, ch, t0:t0 + n], in0=pp[:, 0:n], scalar=g.mods[:, 5 * 16 + ch, which:which + 1], in1=xT[:, ch, t0:t0 + n], op0=ALU.mult, op1=ALU.add),
                  reads=["mods"], writes=[("ps", 5 + q), ("x", ch, ti)])
    pass
    hT = mixT
    norm_mod(g, xT, hT, 6, 7, fb, tiles=qtiles)
    ffn(g, xT, hT, W2I, W2O, 8, fb, tiles=qtiles)
    if last:
        for ti, (t0, n) in enumerate(qtiles):
            col_rstd(g, [(xT[:, c, t0:t0 + n], [("x", c, ti)]) for c in range(KC)], D, fb.rstd[:, t0:t0 + n], t0, n, fb.sq, None, ("rstd", ti), fb.rtmp)
            for c in range(KC):
                P.add(DVE, lambda e, c=c: e.scalar_tensor_tensor(out=xT[:, c, t0:t0 + n], in0=xT[:, c, t0:t0 + n], scalar=g.vec[:, V_FG + c:V_FG + c + 1], in1=fb.rstd[:, t0:t0 + n], op0=ALU.mult, op1=ALU.mult),
                      reads=[("rstd", ti), "vec"], writes=[("x", c, ti)])
    ncols = TL if last else T
    for c in range(KC):
        fin.append(P.add(SP, lambda e, c=c: e.dma_start(out=OUT[c * 128:(c + 1) * 128, :], in_=xT[:, c, 0:ncols]),
                         reads=[("x", c, 0), ("x", c, 1), ("x", c, 2)], dma=True))
    if F is not None:
        return fin
    P.emit(final_wait_ops=fin)
    return nc


class Fused:
    def __init__(self, nc, P, ar, g):
        self.nc, self.P, self.ar, self.g = nc, P, ar, g
        self.mid, self.xin, self.w, self.rope, self.outs = {}, {}, {}, {}, {}

    def tensor(self, n, s, d, k, l, half):
        if n == "xT":
            return self.xin[(l, half)]
        if n == "rope":
            return self.rope[half]
        if n in ("vec", "cst", "mods", "w_mod"):
            return None
        if k == "ExternalInput":
            return self.w[(n, l)]
        t = self.nc.dram_tensor(f"{n}_{l}_{half}", s, d, kind="Internal").ap()
        self.mid[(n, l, half)] = t
        return t

    def tensor_b(self, n, s, d, k, l, half):
        if n == "xT":
            return self.mid[("xo", l, half)]
        if n in ("vec", "cst", "mods", "gk", "gv", "mkn", "mkr", "mv", "pke", "pv", "psc"):
            return None
        if n in ("gq", "mqn", "mqr", "hqe", "hkt", "hke", "hv", "hsc", "hg"):
            return self.mid[(n, l, half)]
        if n == "out":
            return self.xin[(l + 1, half)] if (l + 1, half) in self.xin else self.outs[half]
        if n == "mixdbg":
            return self.nc.dram_tensor(f"mixdbg_{l}_{half}", s, d, kind="ExternalOutput").ap()
        return self.w[(n, l)]


W_A = (("w1i", [D, 2 * FH]), ("w1o", [FH, D]), ("w_in", [D, 4928]), ("w_uq", [512, 768]), ("w_ukv", [256, 1024]))
W_B = (("w_out", [D, D]), ("w2i", [D, 2 * FH]), ("w2o", [FH, D]))
DEPTH = 2


def build_fused():
    nc = bass.Bass("TRN2", target_bir_lowering=False)
    ein = lambda n, s, d=F32: nc.dram_tensor(n, s, d, kind="ExternalInput").ap()
    CST = ein("cst", [128, NCST])
    VECS = [ein(f"vec{l}", [128, NVEC]) for l in range(DEPTH)]
    WMODS = [ein(f"w_mod{l}", [D, NMOD * D]) for l in range(DEPTH)]
    P = Prog(nc)
    ar = Arena(nc)
    g = setup_common(nc, P, ar, CST, VECS[0])
    F = Fused(nc, P, ar, g)
    for h in range(2):
        F.xin[(0, h)] = ein(f"xT{h}", [D, T])
        F.rope[h] = ein(f"rope{h}", [128, 4, TL])
        F.outs[h] = nc.dram_tensor(f"out{h}", [D, TL], F32, kind="ExternalOutput").ap()
        for l in range(1, DEPTH):
            F.xin[(l, h)] = nc.dram_tensor(f"x{l}_{h}", [D, T], F32, kind="Internal").ap()
    for l in range(DEPTH):
        for n, shp in W_A + W_B:
            F.w[(n, l)] = ein(f"{n}{l}", shp)
    base = ar.mark()
    fin = []
    for l in range(DEPTH):
        last = l == DEPTH - 1
        if l > 0:
            P.add(SP, lambda e: e.dma_start(out=g.vec[:], in_=VECS[l]), dma=True)
        ar.release(base)
        compute_mods(g, ar, WMODS[l])
        for half in range(2):
            ar.release(base)
            build_A(l, last, F, half)
        for half in range(2):
            ar.release(base)
            f = build_B(l, last, F, half)
            if last:
                fin += f
    P.emit(final_wait_ops=fin)
    return nc


NCORES = 4
_CACHE = {}


def fused_inputs(inp, b, cst, ropes):
    m = {"cst": cst}
    for h in range(2):
        xl = inp["x"][b, h * TL:(h + 1) * TL]
        m[f"xT{h}"] = np.ascontiguousarray(np.concatenate([xl, inp["ctx"][b]], axis=0).T.astype(np.float32))
        m[f"rope{h}"] = ropes[h]
    for l in range(DEPTH):
        m[f"vec{l}"] = host_vec(inp, l, b)
        m[f"w_mod{l}"] = inp["w_mod"][l]
        m[f"w1i{l}"] = inp["w_ffn1_in"][l]
        m[f"w1o{l}"] = inp["w_ffn1_out"][l]
        m[f"w_in{l}"] = inp["w_in"][l]
        m[f"w_uq{l}"] = inp["w_uq"][l]
        m[f"w_ukv{l}"] = inp["w_ukv"][l]
        m[f"w_out{l}"] = inp["w_out"][l]
        m[f"w2i{l}"] = inp["w_ffn2_in"][l]
        m[f"w2o{l}"] = inp["w_ffn2_out"][l]
    return m


def kernel(**inputs):
    inp = {k: np.asarray(v) for k, v in inputs.items()}
    cst = host_consts()
    ropes = [host_rope(0), host_rope(1)]
    if "nc" not in _CACHE:
        _CACHE["nc"] = build_fused()
    cores = list(range(NCORES))
    res = run_bass_kernel_spmd(_CACHE["nc"], [fused_inputs(inp, b, cst, ropes) for b in cores], core_ids=cores)
    out = np.zeros((4, SEQ, D), np.float32)
    for b in cores:
        for h in range(2):
            out[b, h * TL:(h + 1) * TL, :] = res.results[b][f"out{h}"].T
    return out


def _a_inputs(inp, l, core, xT, cst, ropes):
    b, half = core // 2, core % 2
    return {"xT": xT, "vec": host_vec(inp, l, b), "cst": cst, "rope": ropes[half],
            "w_mod": inp["w_mod"][l], "w1i": inp["w_ffn1_in"][l], "w1o": inp["w_ffn1_out"][l], "w_in": inp["w_in"][l],
            "w_uq": inp["w_uq"][l], "w_ukv": inp["w_ukv"][l]}


def _b_inputs(inp, l, core, ra, cst):
    b, half = core // 2, core % 2
    r = ra[core]
    r0, r1 = ra[2 * b], ra[2 * b + 1]
    rp = ra[core ^ 1]
    catk = lambda k: np.ascontiguousarray(np.concatenate([r0[k][..., :TL], r1[k][..., :TL], r0[k][..., TL:]], axis=-1))
    catv = lambda k: np.ascontiguousarray(np.concatenate([r0[k][:TL], r1[k][:TL], r0[k][TL:]], axis=0))
    pdir = 1 if half == 0 else 0
    pke = np.zeros((TL, 2, 4, 128), ml_dtypes.bfloat16)
    pke[:, pdir] = rp["hke"][:TL, pdir]
    psc = np.ones((128, 2, 3, 4, 64), np.float32)
    psc[:, pdir] = rp["hsc"][:, pdir, :, :, :64]
    psc[:, 1 - pdir, 2] = 0.0
    return {"xT": r["xo"], "vec": host_vec(inp, l, b), "cst": cst, "mods": r["mods"],
            "gq": r["gq"], "gk": catk("gk"), "gv": catv("gv"),
            "mqn": r["mqn"], "mqr": r["mqr"], "mkn": catk("mkn"), "mkr": catk("mkr"), "mv": catv("mv"),
            "hqe": r["hqe"], "hkt": r["hkt"], "hke": r["hke"], "hv": r["hv"], "hsc": r["hsc"], "hg": r["hg"],
            "pke": pke, "pv": np.ascontiguousarray(rp["hv"][:TL]), "psc": psc,
            "w_out": inp["w_out"][l], "w2i": inp["w_ffn2_in"][l], "w2o": inp["w_ffn2_out"][l]}
```
